# Optimizing a Trainium2 kernel written in Bass

```python
import math
import jax, jax.numpy as jnp
from jax import lax
import numpy as np

D_MODEL = 1024
BATCH = 4
SEQ = 4096
DEPTH = 2

GRID_W = 64
QBLK = 128
EPS = 1e-6
ROPE_THETA = 10000.0

DIFF_HEADS = 4
DIFF_HD = 64
GQA_HEADS = 8
GQA_KV_HEADS = 2
GQA_HD = 64
MLA_HEADS = 8
MLA_Q_RANK = 256
MLA_KV_RANK = 128
MLA_NOPE = 64
MLA_ROPE = 32
MLA_V = 64
S5_GROUPS = 32
S5_GROUP_CH = 16
S5_STATE = 64
S5_WIDTH = S5_GROUPS * S5_GROUP_CH
FFN_HIDDEN = 4 * D_MODEL
N_BRANCH = 4

A_Q = DIFF_HEADS * 2 * DIFF_HD
A_K = DIFF_HEADS * 2 * DIFF_HD
A_V = DIFF_HEADS * 2 * DIFF_HD
B_Q = GQA_HEADS * GQA_HD
B_KV = GQA_KV_HEADS * GQA_HD
IN_SIZES = (A_Q, A_K, A_V, B_Q, B_KV, B_KV, MLA_Q_RANK, MLA_KV_RANK, MLA_ROPE, S5_WIDTH)
IN_WIDTH = sum(IN_SIZES)
BR_A = DIFF_HEADS * 2 * DIFF_HD
BR_B = GQA_HEADS * GQA_HD
BR_C = MLA_HEADS * MLA_V
BR_D = S5_WIDTH

kernel_name = "hybrid_gated_parallel_encoder"


def _rmsnorm(x, g):
    xf = x.astype(jnp.float32)
    y = xf * lax.rsqrt(jnp.mean(xf * xf, axis=-1, keepdims=True) + EPS)
    return y.astype(x.dtype) * g


def _rope_angles(pos, dim):
    inv = ROPE_THETA ** (-jnp.arange(0, dim, 2, dtype=jnp.float32) / dim)
    ang = pos.astype(jnp.float32)[:, None] * inv[None, :]
    return jnp.cos(ang), jnp.sin(ang)


def _apply_rope(x, cos, sin):
    half = x.shape[-1] // 2
    extra = x.ndim - 3
    c = cos.reshape(cos.shape[:1] + (1,) * extra + cos.shape[1:]).astype(x.dtype)
    s = sin.reshape(sin.shape[:1] + (1,) * extra + sin.shape[1:]).astype(x.dtype)
    x1, x2 = x[..., :half], x[..., half:]
    return jnp.concatenate([x1 * c - x2 * s, x1 * s + x2 * c], axis=-1)


def _sweep_queries(block_fn, qs):
    B, L = qs[0].shape[:2]
    nb = L // QBLK
    split = lambda a: jnp.moveaxis(a.reshape((B, nb, QBLK) + a.shape[2:]), 1, 0)
    out = lax.map(lambda args: block_fn(args[0], *args[1]),
                  (jnp.arange(nb), tuple(split(a) for a in qs)))
    out = jnp.moveaxis(out, 0, 1)
    return out.reshape((B, L) + out.shape[3:])


def _diff_attention(q, k, v, lq1, lk1, lq2, lk2, subln, layer_idx):
    B, L, H = q.shape[:3]
    f32 = jnp.float32
    lam_init = 0.8 - 0.6 * math.exp(-0.3 * layer_idx)
    lam = (jnp.exp(jnp.sum(lq1.astype(f32) * lk1.astype(f32)))
           - jnp.exp(jnp.sum(lq2.astype(f32) * lk2.astype(f32))) + lam_init)
    slopes = jnp.asarray(2.0 ** (-8.0 * np.arange(1, H + 1) / H), dtype=f32)
    kpos = jnp.arange(L, dtype=f32)
    scale = DIFF_HD ** -0.5

    def block(i, qb):
        s = jnp.einsum('bqhcd,bkhcd->bhcqk', qb, k).astype(f32) * scale
        qpos = (i * QBLK + jnp.arange(QBLK)).astype(f32)
        bias = -slopes[:, None, None, None] * jnp.abs(qpos[:, None] - kpos[None, :])
        p = jax.nn.softmax(s + bias, axis=-1)
        w = p[:, :, 0] - lam * p[:, :, 1]
        return jnp.einsum('bhqk,bkhe->bqhe', w.astype(v.dtype), v)

    o = _sweep_queries(block, (q,))
    o = _rmsnorm(o, subln) * (1.0 - lam_init)
    return o.reshape(B, L, H * 2 * DIFF_HD)


def _gqa_axial(q, k, v, gq, gk):
    B, L = q.shape[:2]
    rows = L // GRID_W
    row = jnp.repeat(jnp.arange(rows), GRID_W)
    col = jnp.tile(jnp.arange(GRID_W), rows)
    half = GQA_HD // 2
    rc, rs = _rope_angles(row, half)
    cc, cs = _rope_angles(col, half)
    rot = lambda t: jnp.concatenate([_apply_rope(t[..., :half], rc, rs),
                                     _apply_rope(t[..., half:], cc, cs)], axis=-1)
    q = rot(_rmsnorm(q, gq))
    k = rot(_rmsnorm(k, gk))
    q = q.reshape(B, L, GQA_KV_HEADS, GQA_HEADS // GQA_KV_HEADS, GQA_HD)
    scale = GQA_HD ** -0.5

    def block(i, qb):
        s = jnp.einsum('bqgrd,bkgd->bgrqk', qb, k).astype(jnp.float32) * scale
        p = jax.nn.softmax(s, axis=-1)
        return jnp.einsum('bgrqk,bkgd->bqgrd', p.astype(v.dtype), v)

    o = _sweep_queries(block, (q,))
    return o.reshape(B, L, GQA_HEADS * GQA_HD)


def _mla(cq, ckv, kr, gq, gkv, w_uq, w_ukv):
    B, L = cq.shape[:2]
    cos, sin = _rope_angles(jnp.arange(L), MLA_ROPE)
    q = (_rmsnorm(cq, gq) @ w_uq).reshape(B, L, MLA_HEADS, MLA_NOPE + MLA_ROPE)
    kv = (_rmsnorm(ckv, gkv) @ w_ukv).reshape(B, L, MLA_HEADS, MLA_NOPE + MLA_V)
    q_nope = q[..., :MLA_NOPE]
    q_rope = _apply_rope(q[..., MLA_NOPE:], cos, sin)
    k_nope, v = kv[..., :MLA_NOPE], kv[..., MLA_NOPE:]
    k_rope = _apply_rope(kr, cos, sin)
    scale = (MLA_NOPE + MLA_ROPE) ** -0.5

    def block(i, qn, qr):
        s = (jnp.einsum('bqhd,bkhd->bhqk', qn, k_nope)
             + jnp.einsum('bqhr,bkr->bhqk', qr, k_rope))
        p = jax.nn.softmax(s.astype(jnp.float32) * scale, axis=-1)
        return jnp.einsum('bhqk,bkhd->bqhd', p.astype(v.dtype), v)

    o = _sweep_queries(block, (q_nope, q_rope))
    return o.reshape(B, L, MLA_HEADS * MLA_V)


def _complex_affine_combine(e1, e2):
    a1r, a1i, b1r, b1i = e1
    a2r, a2i, b2r, b2i = e2
    return (a2r * a1r - a2i * a1i,
            a2r * a1i + a2i * a1r,
            a2r * b1r - a2i * b1i + b2r,
            a2r * b1i + a2i * b1r + b2i)


def _s5_direction(u, a_re, a_im, log_dt, b_re, b_im, c_re, c_im, reverse):
    f32 = jnp.float32
    L = u.shape[1]
    a_re = jnp.minimum(a_re.astype(f32), -1e-4)
    a_im = a_im.astype(f32)
    dt = jnp.exp(log_dt.astype(f32))[:, None]
    mag = jnp.exp(a_re * dt)
    lb_re = mag * jnp.cos(a_im * dt)
    lb_im = mag * jnp.sin(a_im * dt)
    den = a_re * a_re + a_im * a_im
    n_re = lb_re - 1.0
    f_re = (n_re * a_re + lb_im * a_im) / den
    f_im = (lb_im * a_re - n_re * a_im) / den
    b_re, b_im = b_re.astype(f32), b_im.astype(f32)
    bb_re = f_re[..., None] * b_re - f_im[..., None] * b_im
    bb_im = f_re[..., None] * b_im + f_im[..., None] * b_re
    x_re = jnp.einsum('blgh,gph->lbgp', u, bb_re)
    x_im = jnp.einsum('blgh,gph->lbgp', u, bb_im)
    at_re = jnp.broadcast_to(lb_re, (L, 1) + lb_re.shape)
    at_im = jnp.broadcast_to(lb_im, (L, 1) + lb_im.shape)
    _, _, s_re, s_im = lax.associative_scan(_complex_affine_combine,
                                            (at_re, at_im, x_re, x_im),
                                            reverse=reverse, axis=0)
    return (jnp.einsum('lbgp,ghp->blgh', s_re, c_re.astype(f32))
            - jnp.einsum('lbgp,ghp->blgh', s_im, c_im.astype(f32)))


def _s5_mixer(u, a_re, a_im, log_dt, b_re, b_im, c_re, c_im, d, w_glu):
    B, L = u.shape[:2]
    ug = u.reshape(B, L, S5_GROUPS, S5_GROUP_CH)
    uf = ug.astype(jnp.float32)
    y = (_s5_direction(uf, a_re[0], a_im[0], log_dt[0], b_re[0], b_im[0], c_re[0], c_im[0], False)
         + _s5_direction(uf, a_re[1], a_im[1], log_dt[1], b_re[1], b_im[1], c_re[1], c_im[1], True))
    y = y.astype(u.dtype) + d * ug
    y = jax.nn.gelu(y.reshape(B, L, S5_WIDTH))
    h = y @ w_glu
    return h[..., :S5_WIDTH] * jax.nn.sigmoid(h[..., S5_WIDTH:])


def setup_inputs(seed: int = 0) -> dict:
    key = jax.random.key(seed)
    ks = iter(jax.random.split(key, 48))
    nrm = lambda shape, scale: jax.random.normal(next(ks), shape, jnp.float32) * scale
    gain = lambda shape: 1.0 + 0.05 * jax.random.normal(next(ks), shape, jnp.float32)
    D, G, P, H = D_MODEL, S5_GROUPS, S5_STATE, S5_GROUP_CH
    a_im_base = jnp.pi * jnp.arange(P, dtype=jnp.float32)
    return {
        "x": nrm((BATCH, SEQ, D), 1.0),
        "norm_pre_mix": gain((DEPTH, D)),
        "norm_post_mix": gain((DEPTH, D)),
        "norm_pre_ffn": gain((DEPTH, D)),
        "norm_post_ffn": gain((DEPTH, D)),
        "w_in": nrm((DEPTH, D, IN_WIDTH), D ** -0.5),
        "w_gate": nrm((DEPTH, D, N_BRANCH * D), D ** -0.5),
        "diff_lam_q1": nrm((DEPTH, DIFF_HD), 0.1),
        "diff_lam_k1": nrm((DEPTH, DIFF_HD), 0.1),
        "diff_lam_q2": nrm((DEPTH, DIFF_HD), 0.1),
        "diff_lam_k2": nrm((DEPTH, DIFF_HD), 0.1),
        "diff_subln": gain((DEPTH, 2 * DIFF_HD)),
        "gqa_q_norm": gain((DEPTH, GQA_HD)),
        "gqa_k_norm": gain((DEPTH, GQA_HD)),
        "mla_q_norm": gain((DEPTH, MLA_Q_RANK)),
        "mla_kv_norm": gain((DEPTH, MLA_KV_RANK)),
        "mla_w_uq": nrm((DEPTH, MLA_Q_RANK, MLA_HEADS * (MLA_NOPE + MLA_ROPE)), MLA_Q_RANK ** -0.5),
        "mla_w_ukv": nrm((DEPTH, MLA_KV_RANK, MLA_HEADS * (MLA_NOPE + MLA_V)), MLA_KV_RANK ** -0.5),
        "s5_a_re": -0.5 + nrm((DEPTH, 2, G, P), 0.01),
        "s5_a_im": a_im_base + nrm((DEPTH, 2, G, P), 0.01),
        "s5_log_dt": jax.random.uniform(next(ks), (DEPTH, 2, G), jnp.float32,
                                        math.log(1e-3), math.log(1e-1)),
        "s5_b_re": nrm((DEPTH, 2, G, P, H), (2.0 * H) ** -0.5),
        "s5_b_im": nrm((DEPTH, 2, G, P, H), (2.0 * H) ** -0.5),
        "s5_c_re": nrm((DEPTH, 2, G, H, P), P ** -0.5),
        "s5_c_im": nrm((DEPTH, 2, G, H, P), P ** -0.5),
        "s5_d": nrm((DEPTH, G, H), 1.0),
        "s5_w_glu": nrm((DEPTH, S5_WIDTH, 2 * S5_WIDTH), S5_WIDTH ** -0.5),
        "w_br_a": nrm((DEPTH, BR_A, D), BR_A ** -0.5),
        "w_br_b": nrm((DEPTH, BR_B, D), BR_B ** -0.5),
        "w_br_c": nrm((DEPTH, BR_C, D), BR_C ** -0.5),
        "w_br_d": nrm((DEPTH, BR_D, D), BR_D ** -0.5),
        "w_out": nrm((DEPTH, D, D), D ** -0.5),
        "w_ffn_in": nrm((DEPTH, D, FFN_HIDDEN), D ** -0.5),
        "w_ffn_out": nrm((DEPTH, FFN_HIDDEN, D), FFN_HIDDEN ** -0.5),
    }


def reference(x, norm_pre_mix, norm_post_mix, norm_pre_ffn, norm_post_ffn, w_in, w_gate,
              diff_lam_q1, diff_lam_k1, diff_lam_q2, diff_lam_k2, diff_subln,
              gqa_q_norm, gqa_k_norm, mla_q_norm, mla_kv_norm, mla_w_uq, mla_w_ukv,
              s5_a_re, s5_a_im, s5_log_dt, s5_b_re, s5_b_im, s5_c_re, s5_c_im, s5_d, s5_w_glu,
              w_br_a, w_br_b, w_br_c, w_br_d, w_out, w_ffn_in, w_ffn_out):
    B, L, _ = x.shape
    split_idx = [int(v) for v in np.cumsum(IN_SIZES)[:-1]]
    h = x
    for i in range(DEPTH):
        xn = _rmsnorm(h, norm_pre_mix[i])
        proj = xn @ w_in[i]
        a_q, a_k, a_v, b_q, b_k, b_v, c_q, c_kv, c_kr, d_u = jnp.split(proj, split_idx, axis=-1)
        ya = _diff_attention(a_q.reshape(B, L, DIFF_HEADS, 2, DIFF_HD),
                             a_k.reshape(B, L, DIFF_HEADS, 2, DIFF_HD),
                             a_v.reshape(B, L, DIFF_HEADS, 2 * DIFF_HD),
                             diff_lam_q1[i], diff_lam_k1[i], diff_lam_q2[i], diff_lam_k2[i],
                             diff_subln[i], i)
        yb = _gqa_axial(b_q.reshape(B, L, GQA_HEADS, GQA_HD),
                        b_k.reshape(B, L, GQA_KV_HEADS, GQA_HD),
                        b_v.reshape(B, L, GQA_KV_HEADS, GQA_HD),
                        gqa_q_norm[i], gqa_k_norm[i])
        yc = _mla(c_q, c_kv, c_kr, mla_q_norm[i], mla_kv_norm[i], mla_w_uq[i], mla_w_ukv[i])
        yd = _s5_mixer(d_u, s5_a_re[i], s5_a_im[i], s5_log_dt[i], s5_b_re[i], s5_b_im[i],
                       s5_c_re[i], s5_c_im[i], s5_d[i], s5_w_glu[i])
        gates = jax.nn.sigmoid(xn @ w_gate[i]).reshape(B, L, N_BRANCH, D_MODEL)
        mixed = (gates[:, :, 0] * (ya @ w_br_a[i]) + gates[:, :, 1] * (yb @ w_br_b[i])
                 + gates[:, :, 2] * (yc @ w_br_c[i]) + gates[:, :, 3] * (yd @ w_br_d[i]))
        h = h + _rmsnorm(mixed @ w_out[i], norm_post_mix[i])
        fn = _rmsnorm(h, norm_pre_ffn[i])
        f = jnp.square(jax.nn.relu(fn @ w_ffn_in[i])) @ w_ffn_out[i]
        h = h + _rmsnorm(f, norm_post_ffn[i])
    return h
```

```python
import math
import ml_dtypes
import numpy as np
import concourse.bass as bass
import concourse.mybir as mybir
from concourse.bass_utils import run_bass_kernel_spmd

F32 = mybir.dt.float32
BF16 = mybir.dt.bfloat16
AF = mybir.ActivationFunctionType
ALU = mybir.AluOpType
AX = mybir.AxisListType


class Buf:
    __slots__ = ("name", "w", "r")

    def __init__(self, name=""):
        self.name = name
        self.w = None
        self.r = []


class _Rec:
    def __getattr__(self, name):
        def f(*a, **k):
            self.call = (name, a, k)
            return self
        return f


class Sched:
    ENGS = ("pe", "act", "dve", "pool", "sp")
    PHASE = 12000

    def __init__(self, nc, n_dma_sems=32):
        self.nc = nc
        self.q = {e: [] for e in self.ENGS}
        self.eng_sem = {}
        self.eng_cnt = {e: 0 for e in self.ENGS}
        self.seen = {e: {} for e in self.ENGS}
        self._ctx = []
        self.n_dma = n_dma_sems
        self.dma_sems = []
        self.dma_val = []
        self.dma_i = 0
        self.n_inst = {e: 0 for e in self.ENGS}
        self.cc_toks = []
        self.actual = {}
        self.dma_rr = {}
        self.pe_sem_ids = set()
        for i in range(n_dma_sems):
            cm = nc.semaphore(f"dq{i}")
            self.dma_sems.append(cm.__enter__())
            self._ctx.append(cm)
            self.dma_val.append(0)
        for e in self.ENGS:
            self._new_eng_sem(e)

    def _new_eng_sem(self, e):
        cm = self.nc.semaphore(f"s_{e}_{len(self._ctx)}")
        self.eng_sem[e] = cm.__enter__()
        self._ctx.append(cm)
        self.eng_cnt[e] = 0
        if e == "pe":
            self.pe_sem_ids.add(id(self.eng_sem[e]))

    def close(self):
        for cm in reversed(self._ctx):
            cm.__exit__(None, None, None)

    def _eng(self, e):
        nc = self.nc
        return {"pe": nc.tensor, "act": nc.scalar, "dve": nc.vector,
                "pool": nc.gpsimd, "sp": nc.sync}[e]

    def _wait(self, e, tok):
        sem, val = tok
        key = id(sem)
        if e == "pe" and key in self.pe_sem_ids:
            return
        if self.seen[e].get(key, 0) >= val:
            return
        self.seen[e][key] = val
        self.q[e].append(("wait", sem, val))

    def _deps(self, e, reads, writes):
        for b in reads:
            if b.w is not None:
                self._wait(e, b.w)
        for b in writes:
            if b.w is not None:
                self._wait(e, b.w)
            for t in b.r:
                self._wait(e, t)

    def _commit(self, tok, reads, writes):
        for b in reads:
            b.r.append(tok)
            if len(b.r) > 12:
                best = {}
                for s, v in b.r:
                    if id(s) not in best or best[id(s)][1] < v:
                        best[id(s)] = (s, v)
                b.r = list(best.values())
        for b in writes:
            b.w = tok
            b.r = []

    def op(self, e, fn, reads=(), writes=()):
        if self.eng_cnt[e] >= self.PHASE:
            self._new_eng_sem(e)
        self._deps(e, reads, writes)
        self.eng_cnt[e] += 1
        tok = (self.eng_sem[e], self.eng_cnt[e])
        rec = _Rec()
        fn(rec)
        self.q[e].append(("op", rec.call, tok[0], tok[1]))
        self._commit(tok, reads, writes)
        self.n_inst[e] += 1
        return tok

    def dma(self, e, out, in_, reads=(), writes=(), **kw):
        lo, hi = (0, self.n_dma - 8) if e != "pool" else (self.n_dma - 8, self.n_dma)
        key = "sw" if e == "pool" else "hw"
        i = self.dma_rr.get(key, lo)
        self.dma_rr[key] = lo + ((i - lo + 1) % (hi - lo))
        sem = self.dma_sems[i]
        if self.dma_val[i] > 0:
            self._wait(e, (sem, self.dma_val[i]))
        self._deps(e, reads, writes)
        self.dma_val[i] += 16
        tok = (sem, self.dma_val[i])
        self.q[e].append(("dma", out, in_, sem, kw))
        self._commit(tok, reads, writes)
        self.n_inst[e] += 1
        return tok

    def wait_all(self, e, toks):
        for t in toks:
            self._wait(e, t)

    def cc(self, kind, ins, outs, groups, reads=(), writes=()):
        e = "pool"
        cm = self.nc.semaphore(f"cc{len(self._ctx)}")
        sem = cm.__enter__()
        self._ctx.append(cm)
        self._deps(e, reads, writes)
        tok = (sem, 1)
        self.q[e].append(("cc", kind, ins, outs, groups, sem))
        self._commit(tok, reads, writes)
        self.cc_toks.append(tok)
        return tok

    def flush(self, include_cc=True):
        nc = self.nc
        toks = []
        for i, sem in enumerate(self.dma_sems):
            if self.dma_val[i] > 0:
                toks.append((sem, self.dma_val[i]))
        for e in self.ENGS:
            if self.eng_cnt[e] > 0:
                toks.append((self.eng_sem[e], self.eng_cnt[e]))
        if include_cc:
            toks += self.cc_toks
        for e in self.ENGS:
            for t in toks:
                self._wait(e, t)
        eng_sem_ids = set()
        needed = {}
        for e in self.ENGS:
            for it in self.q[e]:
                if it[0] == "op":
                    eng_sem_ids.add(id(it[2]))
        for e in self.ENGS:
            for it in self.q[e]:
                if it[0] == "wait" and id(it[1]) in eng_sem_ids:
                    needed.setdefault(id(it[1]), set()).add(it[2])
        vmap = {}
        for e in self.ENGS:
            for it in self.q[e]:
                if it[0] == "op":
                    sid, v = id(it[2]), it[3]
                    if v in needed.get(sid, ()):
                        self.actual[sid] = self.actual.get(sid, 0) + 1
                        vmap[(sid, v)] = self.actual[sid]
        with nc.Block() as block:
            def emit(e, eng):
                for it in self.q[e]:
                    if it[0] == "wait":
                        sid = id(it[1])
                        if sid in eng_sem_ids:
                            eng.wait_ge(it[1], vmap[(sid, it[2])])
                        else:
                            eng.wait_ge(it[1], it[2])
                    elif it[0] == "op":
                        name, a, k = it[1]
                        ins = getattr(eng, name)(*a, **k)
                        if (id(it[2]), it[3]) in vmap:
                            ins.then_inc(it[2], 1)
                    elif it[0] == "cc":
                        _, kind, ins, outs, groups, sem = it
                        eng.collective_compute(kind, ALU.bypass, replica_groups=groups, ins=ins, outs=outs).then_inc(sem, 1)
                    else:
                        _, out, in_, sem, kw = it
                        eng.dma_start(out=out, in_=in_, **kw).then_inc(sem, 16)

            @block.tensor
            def _(eng):
                emit("pe", eng)

            @block.scalar
            def _(eng):
                emit("act", eng)

            @block.vector
            def _(eng):
                emit("dve", eng)

            @block.gpsimd
            def _(eng):
                emit("pool", eng)

            @block.sync
            def _(eng):
                emit("sp", eng)
        self.q = {e: [] for e in self.ENGS}


class TileAlloc:
    _uid = [0]

    def __init__(self, nc):
        self.nc = nc
        self._ctx = []
        TileAlloc._uid[0] += 1
        self.tag = f"a{TileAlloc._uid[0]}_"

    def sb(self, name, shape, dtype):
        cm = self.nc.sbuf_tensor("sb_" + self.tag + name, list(shape), dtype)
        t = cm.__enter__()
        self._ctx.append(cm)
        return t

    def ps(self, name, shape, dtype):
        cm = self.nc.psum_tensor("ps_" + self.tag + name, list(shape), dtype)
        t = cm.__enter__()
        self._ctx.append(cm)
        return t

    def close(self):
        for cm in reversed(self._ctx):
            cm.__exit__(None, None, None)


BF = ml_dtypes.bfloat16
NT = 2048
NTI = NT // 128
EPS = 1e-6


def _dram(nc, name, shape, dt, kind):
    return nc.dram_tensor(name, list(shape), dt, kind=kind).ap()


class Ctx:
    def __init__(self):
        self.nc = bass.Bass("TRN2", target_bir_lowering=False)
        self.S = Sched(self.nc)
        self.A = TileAlloc(self.nc)
        self.ins = {}

    def inp(self, name, shape, dt=F32):
        return _dram(self.nc, name, shape, dt, "ExternalInput")

    def out(self, name, shape, dt=F32):
        return _dram(self.nc, name, shape, dt, "ExternalOutput")


def load_weight_bf16(C, dst, dstbuf, src, rows, cols, stage, stagebufs, ktiles, idx=[0]):
    S = C.S
    for kt in range(ktiles):
        i = idx[0] % len(stage)
        idx[0] += 1
        st, sb = stage[i], stagebufs[i]
        r0 = kt * 128
        nr = min(128, rows - r0)
        S.dma("sp", st[0:nr, 0:cols], src[r0:r0 + nr, :], writes=[sb])
        eng = "dve" if i % 2 == 0 else "pool"
        S.op(eng, lambda e, st=st, kt=kt, nr=nr: e.tensor_copy(out=dst[0:nr, kt, 0:cols], in_=st[0:nr, 0:cols]),
             reads=[sb], writes=[dstbuf])


def build_P():
    C = Ctx()
    nc, S, A = C.nc, C.S, C.A
    h = C.inp("h", [NT, 1024])
    g_pre = C.inp("g_pre", [1024])
    w_in = C.inp("w_in", [1024, 3232])
    gB = C.inp("gB", [640])
    gC = C.inp("gC", [384])
    w_uq = C.inp("w_uq", [256, 768])
    w_ukv = C.inp("w_ukv", [128, 1024])
    cosb = C.inp("cosb", [NT, 64])
    sinb = C.inp("sinb", [NT, 64])
    cosc = C.inp("cosc", [NT, 32])
    sinc = C.inp("sinc", [NT, 32])
    identd = C.inp("identd", [128, 128], BF16)
    oa = C.out("oa", [NT, 1536], BF16)
    ob = C.out("ob", [NT, 768], BF16)
    ocq = C.out("ocq", [NT, 768], BF16)
    ockv = C.out("ockv", [NT, 1024], BF16)
    ockr = C.out("ockr", [NT, 32], BF16)
    ou = C.out("ou", [NT, 512], F32)

    ident = A.sb("ident", [128, 128], BF16); b_ident = Buf()
    S.dma("sp", ident[:], identd[:, :], writes=[b_ident])
    g_bc = A.sb("g_bc", [128, 1024], F32); b_g = Buf()
    S.dma("pool", g_bc[:], g_pre.partition_broadcast(128), writes=[b_g])
    gB_bc = A.sb("gB_bc", [128, 640], F32); b_gB = Buf()
    S.dma("pool", gB_bc[:], gB.partition_broadcast(128), writes=[b_gB])
    gC_bc = A.sb("gC_bc", [128, 384], F32); b_gC = Buf()
    S.dma("pool", gC_bc[:], gC.partition_broadcast(128), writes=[b_gC])
    epsT = A.sb("epsT", [128, 1], F32); b_eps = Buf()
    S.op("dve", lambda e: e.memset(epsT[:], EPS), writes=[b_eps])

    stage = [A.sb(f"wst{i}", [128, 3232], F32) for i in range(2)]
    stageb = [Buf() for _ in range(2)]
    wb = A.sb("wb", [128, 8, 3232], BF16); b_wb = Buf()
    load_weight_bf16(C, wb, b_wb, w_in, 1024, 3232, stage, stageb, 8)
    wuq = A.sb("wuq", [128, 2, 768], BF16); b_wuq = Buf()
    load_weight_bf16(C, wuq, b_wuq, w_uq, 256, 768, stage, stageb, 2)
    wukv = A.sb("wukv", [128, 1, 1024], BF16); b_wukv = Buf()
    load_weight_bf16(C, wukv, b_wukv, w_ukv, 128, 1024, stage, stageb, 1)

    NB = 2
    ht = [A.sb(f"ht{i}", [128, 1024], F32) for i in range(NB)]; b_ht = [Buf() for _ in range(NB)]
    junk = A.sb("junk", [128, 1024], F32); b_junk = Buf()
    st1 = [A.sb(f"st1{i}", [128, 16], F32) for i in range(NB)]; b_st1 = [Buf() for _ in range(NB)]
    xn = [A.sb(f"xn{i}", [128, 1024], BF16) for i in range(NB)]; b_xn = [Buf() for _ in range(NB)]
    xnT = [A.sb(f"xnT{i}", [128, 8, 128], BF16) for i in range(NB)]; b_xnT = [Buf() for _ in range(NB)]
    pj = [A.sb(f"pj{i}", [128, 3232], F32) for i in range(NB)]; b_pj = [Buf() for _ in range(NB)]
    tA = [A.sb(f"tA{i}", [128, 1536], BF16) for i in range(NB)]; b_tA = [Buf() for _ in range(NB)]
    tB = [A.sb(f"tB{i}", [128, 768], BF16) for i in range(NB)]; b_tB = [Buf() for _ in range(NB)]
    wk1 = A.sb("wk1", [128, 768], F32); b_wk1 = Buf()
    wk2 = A.sb("wk2", [128, 768], F32); b_wk2 = Buf()
    wk3 = A.sb("wk3", [128, 768], F32); b_wk3 = Buf()
    lat = A.sb("lat", [128, 384], BF16); b_lat = Buf()
    latT = A.sb("latT", [128, 3, 128], BF16); b_latT = Buf()
    qc = A.sb("qc", [128, 768], F32); b_qc = Buf()
    tCq = [A.sb(f"tCq{i}", [128, 768], BF16) for i in range(NB)]; b_tCq = [Buf() for _ in range(NB)]
    tCkv = [A.sb(f"tCkv{i}", [128, 1024], BF16) for i in range(NB)]; b_tCkv = [Buf() for _ in range(NB)]
    tCkr = [A.sb(f"tCkr{i}", [128, 32], BF16) for i in range(NB)]; b_tCkr = [Buf() for _ in range(NB)]
    rb = [A.sb(f"rb{i}", [128, 192], F32) for i in range(NB)]; b_rb = [Buf() for _ in range(NB)]

    pT = A.ps("pT", [128, 1024], BF16); b_pT = Buf()
    pP = [A.ps(f"pP{i}", [128, 512], F32) for i in range(2)]; b_pP = [Buf() for _ in range(2)]
    pU = [A.ps(f"pU{i}", [128, 512], F32) for i in range(2)]; b_pU = [Buf() for _ in range(2)]

    chunks = [(c0, min(512, 3232 - c0)) for c0 in range(0, 3232, 512)]
    for t in range(NTI):
        i = t % NB
        r0 = t * 128
        S.dma("sp", ht[i][:], h[r0:r0 + 128, :], writes=[b_ht[i]])
        S.dma("sp", rb[i][:, 0:64], cosb[r0:r0 + 128, :], writes=[b_rb[i]])
        S.dma("sp", rb[i][:, 64:128], sinb[r0:r0 + 128, :], writes=[b_rb[i]])
        S.dma("sp", rb[i][:, 128:160], cosc[r0:r0 + 128, :], writes=[b_rb[i]])
        S.dma("sp", rb[i][:, 160:192], sinc[r0:r0 + 128, :], writes=[b_rb[i]])
        s1 = st1[i]
        S.op("act", lambda e, i=i, s1=s1: e.activation(out=junk[:], in_=ht[i][:], func=AF.Square, accum_out=s1[:, 0:1]),
             reads=[b_ht[i]], writes=[b_junk, b_st1[i]])
        S.op("act", lambda e, s1=s1: e.activation(out=s1[:, 1:2], in_=s1[:, 0:1], func=AF.Sqrt, bias=epsT[:, 0:1], scale=1.0 / 1024),
             reads=[b_st1[i], b_eps], writes=[b_st1[i]])
        S.op("dve", lambda e, s1=s1: e.reciprocal(out=s1[:, 2:3], in_=s1[:, 1:2]), reads=[b_st1[i]], writes=[b_st1[i]])
        S.op("dve", lambda e, i=i, s1=s1: e.scalar_tensor_tensor(out=xn[i][:], in0=ht[i][:], scalar=s1[:, 2:3], in1=g_bc[:],
                                                               op0=ALU.mult, op1=ALU.mult),
             reads=[b_ht[i], b_st1[i], b_g], writes=[b_xn[i]])
        for kt in range(8):
            S.op("pe", lambda e, i=i, kt=kt: e.transpose(out=pT[:, kt * 128:(kt + 1) * 128], in_=xn[i][:, kt * 128:(kt + 1) * 128], identity=ident[:]),
                 reads=[b_xn[i], b_ident], writes=[b_pT])
        S.op("act", lambda e, i=i: e.copy(out=xnT[i][:].rearrange("p k t -> p (k t)"), in_=pT[:]), reads=[b_pT], writes=[b_xnT[i]])
        for ci, (c0, cw) in enumerate(chunks):
            pb = ci % 2
            for kt in range(8):
                S.op("pe", lambda e, i=i, kt=kt, c0=c0, cw=cw, pb=pb: e.matmul(pP[pb][:, 0:cw], lhsT=xnT[i][:, kt, :], rhs=wb[:, kt, c0:c0 + cw],
                                                                                start=(kt == 0), stop=(kt == 7)),
                     reads=[b_xnT[i], b_wb], writes=[b_pP[pb]])
            eng = "act" if ci % 2 == 0 else "dve"
            if eng == "act":
                S.op("act", lambda e, i=i, c0=c0, cw=cw, pb=pb: e.copy(out=pj[i][:, c0:c0 + cw], in_=pP[pb][:, 0:cw]), reads=[b_pP[pb]], writes=[b_pj[i]])
            else:
                S.op("dve", lambda e, i=i, c0=c0, cw=cw, pb=pb: e.tensor_copy(out=pj[i][:, c0:c0 + cw], in_=pP[pb][:, 0:cw]), reads=[b_pP[pb]], writes=[b_pj[i]])
        p = pj[i]
        S.op("pool", lambda e, i=i, p=p: e.tensor_copy(out=tA[i][:], in_=p[:, 0:1536]), reads=[b_pj[i]], writes=[b_tA[i]])
        S.dma("sp", oa[r0:r0 + 128, :], tA[i][:], reads=[b_tA[i]])
        S.dma("sp", ou[r0:r0 + 128, :], p[:, 2720:3232], reads=[b_pj[i]])
        xB = p[:, 1536:2176]
        S.op("dve", lambda e, xB=xB: e.tensor_tensor(out=wk1[:, 0:640], in0=xB, in1=xB, op=ALU.mult), reads=[b_pj[i]], writes=[b_wk1])
        S.op("dve", lambda e, s1=s1: e.tensor_reduce(out=s1[:, 4:14], in_=wk1[:, 0:640].rearrange("p (h d) -> p h d", h=10), axis=AX.X, op=ALU.add),
             reads=[b_wk1], writes=[b_st1[i]])
        S.op("act", lambda e, s1=s1: e.activation(out=s1[:, 4:14], in_=s1[:, 4:14], func=AF.Sqrt, bias=epsT[:, 0:1], scale=1.0 / 64),
             reads=[b_st1[i], b_eps], writes=[b_st1[i]])
        S.op("dve", lambda e, s1=s1: e.reciprocal(out=s1[:, 4:14], in_=s1[:, 4:14]), reads=[b_st1[i]], writes=[b_st1[i]])
        S.op("dve", lambda e, xB=xB, s1=s1: e.tensor_tensor(out=wk1[:, 0:640].rearrange("p (h d) -> p h d", h=10), in0=xB.rearrange("p (h d) -> p h d", h=10),
                                                           in1=s1[:, 4:14].unsqueeze(2).broadcast_to([128, 10, 64]), op=ALU.mult),
             reads=[b_pj[i], b_st1[i]], writes=[b_wk1])
        S.op("pool", lambda e: e.tensor_tensor(out=wk1[:, 0:640], in0=wk1[:, 0:640], in1=gB_bc[:], op=ALU.mult), reads=[b_wk1, b_gB], writes=[b_wk1])
        cosB = rb[i][:, 0:64].unsqueeze(1).broadcast_to([128, 10, 64])
        S.op("pool", lambda e, cosB=cosB: e.tensor_tensor(out=wk2[:, 0:640].rearrange("p (h d) -> p h d", h=10), in0=wk1[:, 0:640].rearrange("p (h d) -> p h d", h=10),
                                                         in1=cosB, op=ALU.mult), reads=[b_wk1, b_rb[i]], writes=[b_wk2])
        x5 = wk1[:, 0:640].rearrange("p (h a b d) -> p h a b d", h=10, a=2, b=2)
        o5 = wk3[:, 0:640].rearrange("p (h a b d) -> p h a b d", h=10, a=2, b=2)
        s4 = rb[i][:, 64:128].rearrange("p (a b d) -> p a b d", a=2, b=2)
        for bb in range(2):
            S.op("dve", lambda e, bb=bb, x5=x5, o5=o5, s4=s4: e.tensor_tensor(out=o5[:, :, :, bb, :], in0=x5[:, :, :, 1 - bb, :],
                                                                             in1=s4[:, :, bb, :].unsqueeze(1).broadcast_to([128, 10, 2, 16]), op=ALU.mult),
                 reads=[b_wk1, b_rb[i]], writes=[b_wk3])
        S.op("dve", lambda e, i=i: e.tensor_tensor(out=tB[i][:, 0:640], in0=wk2[:, 0:640], in1=wk3[:, 0:640], op=ALU.add), reads=[b_wk2, b_wk3], writes=[b_tB[i]])
        S.op("pool", lambda e, i=i, p=p: e.tensor_copy(out=tB[i][:, 640:768], in_=p[:, 2176:2304]), reads=[b_pj[i]], writes=[b_tB[i]])
        S.dma("sp", ob[r0:r0 + 128, :], tB[i][:], reads=[b_tB[i]])
        xC = p[:, 2304:2688]
        S.op("dve", lambda e, xC=xC: e.tensor_tensor(out=wk2[:, 0:384], in0=xC, in1=xC, op=ALU.mult), reads=[b_pj[i]], writes=[b_wk2])
        S.op("dve", lambda e, s1=s1: e.tensor_reduce(out=s1[:, 14:15], in_=wk2[:, 0:256], axis=AX.X, op=ALU.add), reads=[b_wk2], writes=[b_st1[i]])
        S.op("dve", lambda e, s1=s1: e.tensor_reduce(out=s1[:, 15:16], in_=wk2[:, 256:384], axis=AX.X, op=ALU.add), reads=[b_wk2], writes=[b_st1[i]])
        S.op("act", lambda e, s1=s1: e.activation(out=s1[:, 14:15], in_=s1[:, 14:15], func=AF.Sqrt, bias=epsT[:, 0:1], scale=1.0 / 256),
             reads=[b_st1[i], b_eps], writes=[b_st1[i]])
        S.op("act", lambda e, s1=s1: e.activation(out=s1[:, 15:16], in_=s1[:, 15:16], func=AF.Sqrt, bias=epsT[:, 0:1], scale=1.0 / 128),
             reads=[b_st1[i], b_eps], writes=[b_st1[i]])
        S.op("dve", lambda e, s1=s1: e.reciprocal(out=s1[:, 14:16], in_=s1[:, 14:16]), reads=[b_st1[i]], writes=[b_st1[i]])
        S.op("dve", lambda e, p=p, s1=s1: e.scalar_tensor_tensor(out=lat[:, 0:256], in0=p[:, 2304:2560], scalar=s1[:, 14:15], in1=gC_bc[:, 0:256],
                                                                op0=ALU.mult, op1=ALU.mult), reads=[b_pj[i], b_st1[i], b_gC], writes=[b_lat])
        S.op("dve", lambda e, p=p, s1=s1: e.scalar_tensor_tensor(out=lat[:, 256:384], in0=p[:, 2560:2688], scalar=s1[:, 15:16], in1=gC_bc[:, 256:384],
                                                                op0=ALU.mult, op1=ALU.mult), reads=[b_pj[i], b_st1[i], b_gC], writes=[b_lat])
        for kt in range(3):
            S.op("pe", lambda e, kt=kt: e.transpose(out=pT[:, kt * 128:(kt + 1) * 128], in_=lat[:, kt * 128:(kt + 1) * 128], identity=ident[:]),
                 reads=[b_lat, b_ident], writes=[b_pT])
        S.op("act", lambda e: e.copy(out=latT[:].rearrange("p k t -> p (k t)"), in_=pT[:, 0:384]), reads=[b_pT], writes=[b_latT])
        for cj in range(2):
            for kt in range(2):
                S.op("pe", lambda e, cj=cj, kt=kt: e.matmul(pU[cj][:, 0:384], lhsT=latT[:, kt, :], rhs=wuq[:, kt, cj * 384:(cj + 1) * 384],
                                                           start=(kt == 0), stop=(kt == 1)), reads=[b_latT, b_wuq], writes=[b_pU[cj]])
            S.op("act", lambda e, cj=cj: e.copy(out=qc[:, cj * 384:(cj + 1) * 384], in_=pU[cj][:, 0:384]), reads=[b_pU[cj]], writes=[b_qc])
        q3 = qc[:].rearrange("p (h d) -> p h d", h=8)
        o3 = tCq[i][:].rearrange("p (h d) -> p h d", h=8)
        S.op("pool", lambda e, q3=q3, o3=o3: e.tensor_copy(out=o3[:, :, 0:64], in_=q3[:, :, 0:64]), reads=[b_qc], writes=[b_tCq[i]])
        cosC = rb[i][:, 128:160]
        sinC = rb[i][:, 160:192]
        w2 = wk2[:, 0:256].rearrange("p (h d) -> p h d", h=8)
        w3 = wk3[:, 0:256].rearrange("p (h d) -> p h d", h=8)
        S.op("dve", lambda e, q3=q3, w2=w2, cosC=cosC: e.tensor_tensor(out=w2, in0=q3[:, :, 64:96], in1=cosC.unsqueeze(1).broadcast_to([128, 8, 32]), op=ALU.mult),
             reads=[b_qc, b_rb[i]], writes=[b_wk2])
        for bb in range(2):
            S.op("dve", lambda e, bb=bb, q3=q3, w3=w3, sinC=sinC: e.tensor_tensor(out=w3[:, :, bb * 16:(bb + 1) * 16], in0=q3[:, :, 64 + (1 - bb) * 16:64 + (2 - bb) * 16],
                                                                               in1=sinC[:, bb * 16:(bb + 1) * 16].unsqueeze(1).broadcast_to([128, 8, 16]), op=ALU.mult),
                 reads=[b_qc, b_rb[i]], writes=[b_wk3])
        S.op("dve", lambda e, o3=o3, w2=w2, w3=w3: e.tensor_tensor(out=o3[:, :, 64:96], in0=w2, in1=w3, op=ALU.add), reads=[b_wk2, b_wk3], writes=[b_tCq[i]])
        S.dma("sp", ocq[r0:r0 + 128, :], tCq[i][:], reads=[b_tCq[i]])
        for cj in range(2):
            S.op("pe", lambda e, cj=cj: e.matmul(pU[cj][:, 0:512], lhsT=latT[:, 2, :], rhs=wukv[:, 0, cj * 512:(cj + 1) * 512], start=True, stop=True),
                 reads=[b_latT, b_wukv], writes=[b_pU[cj]])
            S.op("act", lambda e, cj=cj, i=i: e.copy(out=tCkv[i][:, cj * 512:(cj + 1) * 512], in_=pU[cj][:, 0:512]), reads=[b_pU[cj]], writes=[b_tCkv[i]])
        S.dma("sp", ockv[r0:r0 + 128, :], tCkv[i][:], reads=[b_tCkv[i]])
        kr = p[:, 2688:2720]
        S.op("dve", lambda e, kr=kr, cosC=cosC: e.tensor_tensor(out=wk2[:, 256:288], in0=kr, in1=cosC, op=ALU.mult), reads=[b_pj[i], b_rb[i]], writes=[b_wk2])
        for bb in range(2):
            S.op("dve", lambda e, bb=bb, kr=kr, sinC=sinC: e.tensor_tensor(out=wk3[:, 256 + bb * 16:256 + (bb + 1) * 16], in0=kr[:, (1 - bb) * 16:(2 - bb) * 16],
                                                                         in1=sinC[:, bb * 16:(bb + 1) * 16], op=ALU.mult), reads=[b_pj[i], b_rb[i]], writes=[b_wk3])
        S.op("dve", lambda e, i=i: e.tensor_tensor(out=tCkr[i][:], in0=wk2[:, 256:288], in1=wk3[:, 256:288], op=ALU.add), reads=[b_wk2, b_wk3], writes=[b_tCkr[i]])
        S.dma("sp", ockr[r0:r0 + 128, :], tCkr[i][:], reads=[b_tCkr[i]])
    S.flush()
    return nc


def rope_tables(pos_row, pos_col, pos_lin):
    def ang(pos, dim):
        inv = 10000.0 ** (-np.arange(0, dim, 2, dtype=np.float64) / dim)
        a = pos.astype(np.float64)[:, None] * inv[None, :]
        return np.cos(a), np.sin(a)
    rc, rs = ang(pos_row, 32)
    cc, cs = ang(pos_col, 32)
    c1, s1 = ang(pos_lin, 32)
    cosb = np.concatenate([rc, rc, cc, cc], 1).astype(np.float32)
    sinb = np.concatenate([-rs, rs, -cs, cs], 1).astype(np.float32)
    cosc = np.concatenate([c1, c1], 1).astype(np.float32)
    sinc = np.concatenate([-s1, s1], 1).astype(np.float32)
    return cosb, sinb, cosc, sinc


SLOPES = [2.0 ** (-8.0 * (i + 1) / 4) for i in range(4)]
TABW = 3968
TABOFF = 1920


def attn_maps():
    maps = []
    for hh in range(4):
        for c in range(2):
            maps.append(dict(q=hh * 2 + c, k=hh * 2 + c, v=("A", hh), d=64, dv=128, scale=64 ** -0.5, slope=hh, oa=hh * 2 + c))
    for j in range(8):
        maps.append(dict(q=8 + j, k=8 + j // 4, v=("BC", j // 4), d=64, dv=64, scale=64 ** -0.5, slope=None, obc=j))
    for j in range(8):
        maps.append(dict(q=16 + j, k=10 + j, v=("BC", 2 + j), d=96, dv=64, scale=96 ** -0.5, slope=None, obc=8 + j))
    return maps


def build_Q1(maps=None):
    C = Ctx()
    nc, S, A = C.nc, C.S, C.A
    if maps is None:
        maps = attn_maps()
    QT = C.inp("QT", [24, 96, NT], BF16)
    KT = C.inp("KT", [18, 96, 4096], BF16)
    VA = C.inp("VA", [4, 128, 32, 128], BF16)
    VBC = C.inp("VBC", [10, 128, 32, 64], BF16)
    tabOwn = C.inp("tabOwn", [4, 128, TABW])
    tabPar = C.inp("tabPar", [4, 128, TABW])
    OA = C.out("OA", [8, 128, NT], F32)
    OBC = C.out("OBC", [16, 64, NT], BF16)

    ones = A.sb("ones", [128, 128], BF16); b_ones = Buf()
    S.op("dve", lambda e: e.memset(ones[:], 1.0), writes=[b_ones])
    qt = [A.sb(f"qt{i}", [96, NT], BF16) for i in range(2)]; b_qt = [Buf() for _ in range(2)]
    kt_ = [A.sb(f"kt{i}", [96, 4096], BF16) for i in range(2)]; b_kt = [Buf() for _ in range(2)]
    vt = [A.sb(f"vt{i}", [128, 32, 128], BF16) for i in range(2)]; b_vt = [Buf() for _ in range(2)]
    tO = A.sb("tO", [128, TABW], F32); b_tO = Buf()
    tP = A.sb("tP", [128, TABW], F32); b_tP = Buf()
    NP = 3
    e32 = [A.sb(f"e32{i}", [128, 512], F32) for i in range(NP)]; b_e32 = [Buf() for _ in range(NP)]
    pb = [A.sb(f"pb{i}", [128, 512], BF16) for i in range(NP)]; b_pb = [Buf() for _ in range(NP)]
    rz = [A.sb(f"rz{i}", [128, 512], F32) for i in range(2)]; b_rz = [Buf() for _ in range(2)]
    ot32 = [A.sb(f"ot32{i}", [128, 512], F32) for i in range(2)]; b_ot32 = [Buf() for _ in range(2)]
    ot16 = [A.sb(f"ot16{i}", [128, 512], BF16) for i in range(2)]; b_ot16 = [Buf() for _ in range(2)]
    psS = [A.ps(f"psS{i}", [128, 512], F32) for i in range(NP)]; b_psS = [Buf() for _ in range(NP)]
    psO = [A.ps(f"psO{i}", [128, 512], F32) for i in range(2)]; b_psO = [Buf() for _ in range(2)]
    psZ = [A.ps(f"psZ{i}", [128, 512], F32) for i in range(2)]; b_psZ = [Buf() for _ in range(2)]

    cur = dict(q=None, k=None, v=None, slope=None)
    slot = dict(q=-1, k=-1, v=-1)

    def ensure_loaded(m):
        if cur["q"] != m["q"]:
            slot["q"] = (slot["q"] + 1) % 2
            S.dma("sp", qt[slot["q"]][0:m["d"], :], QT[m["q"], 0:m["d"], :], writes=[b_qt[slot["q"]]])
            cur["q"] = m["q"]
        if cur["k"] != m["k"]:
            slot["k"] = (slot["k"] + 1) % 2
            S.dma("sp", kt_[slot["k"]][0:m["d"], :], KT[m["k"], 0:m["d"], :], writes=[b_kt[slot["k"]]])
            cur["k"] = m["k"]
        if cur["v"] != m["v"]:
            slot["v"] = (slot["v"] + 1) % 2
            kind, vi = m["v"]
            src = VA[vi] if kind == "A" else VBC[vi]
            dv = m["dv"]
            S.dma("pool", vt[slot["v"]][:, :, 0:dv], src, writes=[b_vt[slot["v"]]])
            cur["v"] = m["v"]
        if m["slope"] is not None and cur["slope"] != m["slope"]:
            S.dma("pool", tO[:], tabOwn[m["slope"]], writes=[b_tO])
            S.dma("pool", tP[:], tabPar[m["slope"]], writes=[b_tP])
            cur["slope"] = m["slope"]
        return slot["q"], slot["k"], slot["v"]

    cnt = [0]
    oq = [0]
    for m in maps:
        sq, sk, sv = ensure_loaded(m)
        d, dv, scale = m["d"], m["dv"], m["scale"]
        for qb in range(4):
            ob_ = oq[0] % 2
            oq[0] += 1
            qs = slice(qb * 512, (qb + 1) * 512)

            def emit_S(kt):
                j = (cnt[0] + kt) % NP
                S.op("pe", lambda e, j=j, kt=kt: e.matmul(psS[j][:], lhsT=kt_[sk][0:d, kt * 128:(kt + 1) * 128], rhs=qt[sq][0:d, qs], start=True, stop=True),
                     reads=[b_kt[sk], b_qt[sq]], writes=[b_psS[j]])

            def emit_rest(kt):
                j = (cnt[0] + kt) % NP
                if m["slope"] is not None:
                    S.op("act", lambda e, j=j: e.activation(out=e32[j][:], in_=psS[j][:], func=AF.Exp, scale=scale), reads=[b_psS[j]], writes=[b_e32[j]])
                    if kt < 16:
                        w = 512 * qb - 128 * kt + TABOFF
                        tab, tb = tO, b_tO
                    else:
                        w = 512 * qb - 128 * (kt - 16) + TABOFF
                        tab, tb = tP, b_tP
                    S.op("dve", lambda e, j=j, w=w, tab=tab: e.tensor_tensor(out=pb[j][:], in0=e32[j][:], in1=tab[:, w:w + 512], op=ALU.mult),
                         reads=[b_e32[j], tb], writes=[b_pb[j]])
                else:
                    S.op("act", lambda e, j=j: e.activation(out=pb[j][:], in_=psS[j][:], func=AF.Exp, scale=scale), reads=[b_psS[j]], writes=[b_pb[j]])
                S.op("pe", lambda e, j=j, kt=kt: e.matmul(psO[ob_][0:dv, :], lhsT=vt[sv][:, kt, 0:dv], rhs=pb[j][:], start=(kt == 0), stop=(kt == 31)),
                     reads=[b_vt[sv], b_pb[j]], writes=[b_psO[ob_]])
                S.op("pe", lambda e, j=j, kt=kt: e.matmul(psZ[ob_][0:dv, :], lhsT=ones[:, 0:dv], rhs=pb[j][:], start=(kt == 0), stop=(kt == 31)),
                     reads=[b_ones, b_pb[j]], writes=[b_psZ[ob_]])

            LOOK = 2
            for kt in range(min(LOOK, 32)):
                emit_S(kt)
            for kt in range(32):
                if kt + LOOK < 32:
                    emit_S(kt + LOOK)
                emit_rest(kt)
            cnt[0] += 32
            S.op("dve", lambda e, ob_=ob_: e.reciprocal(out=rz[ob_][0:dv, :], in_=psZ[ob_][0:dv, :]), reads=[b_psZ[ob_]], writes=[b_rz[ob_]])
            if "oa" in m:
                S.op("dve", lambda e, ob_=ob_: e.tensor_tensor(out=ot32[ob_][0:dv, :], in0=psO[ob_][0:dv, :], in1=rz[ob_][0:dv, :], op=ALU.mult),
                     reads=[b_psO[ob_], b_rz[ob_]], writes=[b_ot32[ob_]])
                S.dma("sp", OA[m["oa"], :, qs], ot32[ob_][0:dv, :], reads=[b_ot32[ob_]])
            else:
                S.op("dve", lambda e, ob_=ob_: e.tensor_tensor(out=ot16[ob_][0:dv, :], in0=psO[ob_][0:dv, :], in1=rz[ob_][0:dv, :], op=ALU.mult),
                     reads=[b_psO[ob_], b_rz[ob_]], writes=[b_ot16[ob_]])
                S.dma("sp", OBC[m["obc"], :, qs], ot16[ob_][0:dv, :], reads=[b_ot16[ob_]])
    S.flush()
    return nc


def alibi_tables(half):
    x = np.arange(TABW, dtype=np.float64)[None, :]
    ki = np.arange(128, dtype=np.float64)[:, None]
    delta = 2048.0 if half == 1 else -2048.0
    own = np.stack([np.exp(-m * np.abs(x - TABOFF - ki)) for m in SLOPES]).astype(np.float32)
    par = np.stack([np.exp(-m * np.abs(delta + x - TABOFF - ki)) for m in SLOPES]).astype(np.float32)
    return own, par


def emit_sincos(S, T, bT, ki, bki, t1, bt1, t2, bt2, out_cos, bcos, out_sin, bsin, halfpi, bhp, neg1=None, engs=("dve", "pool")):
    e0, e1 = engs
    S.op(e1, lambda e: e.tensor_copy(out=ki, in_=T), reads=[bT], writes=[bki])
    S.op(e1, lambda e: e.tensor_copy(out=t1, in_=ki), reads=[bki], writes=[bt1])
    S.op(e0, lambda e: e.tensor_tensor(out=t1, in0=T, in1=t1, op=ALU.subtract), reads=[bT, bt1], writes=[bt1])
    S.op("act", lambda e: e.activation(out=t2, in_=t1, func=AF.Sin, scale=math.pi), reads=[bt1], writes=[bt2])
    S.op("act", lambda e: e.activation(out=t1, in_=t1, func=AF.Abs), reads=[bt1], writes=[bt1])
    S.op("act", lambda e: e.activation(out=t1, in_=t1, func=AF.Sin, scale=-math.pi, bias=halfpi), reads=[bt1, bhp], writes=[bt1])
    S.op("dve", lambda e: e.scalar_tensor_tensor(out=out_sin, in0=t2, scalar=2.0, in1=t1, op0=ALU.mult, op1=ALU.mult),
         reads=[bt1, bt2], writes=[bsin])
    S.op("act", lambda e: e.activation(out=t2, in_=t1, func=AF.Square), reads=[bt1], writes=[bt2])
    S.op("act", lambda e: e.activation(out=out_cos, in_=t2, func=AF.Identity, scale=2.0, bias=neg1), reads=[bt2, bhp], writes=[bcos])


NGP = 8
SEG = 2048


def build_Q2():
    C = Ctx()
    nc, S, A = C.nc, C.S, C.A
    I32 = mybir.dt.int32
    uT = C.inp("uT", [2, 128, 4096])
    areT = C.inp("areT", [128, 2 * NGP]); aimT = C.inp("aimT", [128, 2 * NGP]); ldtT = C.inp("ldtT", [128, 2 * NGP])
    Bre = C.inp("Bre", [2, NGP, 128, 128]); Bim = C.inp("Bim", [2, NGP, 128, 128])
    Cre = C.inp("Cre", [2, NGP, 128, 128]); Cim = C.inp("Cim", [2, NGP, 128, 128])
    iotad = C.inp("iota", [128, SEG])
    identd = C.inp("identd", [128, 128], BF16)
    yT = C.out("yT", [2, 128, 4096])

    NC_ = 2 * NGP
    ident = A.sb("ident", [128, 128], BF16); b_ident = Buf()
    S.dma("sp", ident[:], identd[:, :], writes=[b_ident])
    iota = A.sb("iota", [128, SEG], F32); b_iota = Buf()
    S.dma("sp", iota[:], iotad[:, :], writes=[b_iota])
    halfpi = A.sb("halfpi", [128, 1], F32); b_hp = Buf()
    S.op("dve", lambda e: e.memset(halfpi[:], math.pi / 2), writes=[b_hp])

    def small(name, dt=F32):
        return A.sb(name, [128, NC_], dt), Buf()
    are, b_are = small("are"); aim, b_aim = small("aim"); ldt, b_ldt = small("ldt")
    S.dma("sp", are[:], areT[:, :], writes=[b_are]); S.dma("sp", aim[:], aimT[:, :], writes=[b_aim]); S.dma("sp", ldt[:], ldtT[:, :], writes=[b_ldt])
    dt_, b_dt = small("dt"); mag, b_mag = small("mag"); thn, b_thn = small("thn")
    c1, b_c1 = small("c1"); s1, b_s1 = small("s1"); ski, b_ski = small("ski", I32); sa, b_sa = small("sa"); sb_, b_sb = small("sb")
    lbre, b_lbre = small("lbre"); lbim, b_lbim = small("lbim"); rden, b_rden = small("rden"); nre, b_nre = small("nre")
    fre, b_fre = small("fre"); fim, b_fim = small("fim"); nfim, b_nfim = small("nfim"); w1, b_w1 = small("w1"); w2, b_w2 = small("w2")
    D = "dve"
    S.op(D, lambda e: e.tensor_scalar_min(out=are[:], in0=are[:], scalar1=-1e-4), reads=[b_are], writes=[b_are])
    S.op("act", lambda e: e.activation(out=dt_[:], in_=ldt[:], func=AF.Exp), reads=[b_ldt], writes=[b_dt])
    S.op(D, lambda e: e.tensor_tensor(out=w1[:], in0=are[:], in1=dt_[:], op=ALU.mult), reads=[b_are, b_dt], writes=[b_w1])
    S.op("act", lambda e: e.activation(out=mag[:], in_=w1[:], func=AF.Exp), reads=[b_w1], writes=[b_mag])
    S.op(D, lambda e: e.tensor_tensor(out=w2[:], in0=aim[:], in1=dt_[:], op=ALU.mult), reads=[b_aim, b_dt], writes=[b_w2])
    S.op(D, lambda e: e.tensor_scalar(out=thn[:], in0=w2[:], scalar1=1.0 / (2 * math.pi), scalar2=None, op0=ALU.mult), reads=[b_w2], writes=[b_thn])
    emit_sincos(S, thn[:], b_thn, ski[:], b_ski, sa[:], b_sa, sb_[:], b_sb, c1[:], b_c1, s1[:], b_s1, halfpi[:, 0:1], b_hp)
    S.op(D, lambda e: e.tensor_tensor(out=lbre[:], in0=mag[:], in1=c1[:], op=ALU.mult), reads=[b_mag, b_c1], writes=[b_lbre])
    S.op(D, lambda e: e.tensor_tensor(out=lbim[:], in0=mag[:], in1=s1[:], op=ALU.mult), reads=[b_mag, b_s1], writes=[b_lbim])
    S.op(D, lambda e: e.tensor_tensor(out=w1[:], in0=are[:], in1=are[:], op=ALU.mult), reads=[b_are], writes=[b_w1])
    S.op(D, lambda e: e.tensor_tensor(out=w2[:], in0=aim[:], in1=aim[:], op=ALU.mult), reads=[b_aim], writes=[b_w2])
    S.op(D, lambda e: e.tensor_tensor(out=w1[:], in0=w1[:], in1=w2[:], op=ALU.add), reads=[b_w1, b_w2], writes=[b_w1])
    S.op(D, lambda e: e.reciprocal(out=rden[:], in_=w1[:]), reads=[b_w1], writes=[b_rden])
    S.op(D, lambda e: e.tensor_scalar_add(out=nre[:], in0=lbre[:], scalar1=-1.0), reads=[b_lbre], writes=[b_nre])
    S.op(D, lambda e: e.tensor_tensor(out=w1[:], in0=nre[:], in1=are[:], op=ALU.mult), reads=[b_nre, b_are], writes=[b_w1])
    S.op(D, lambda e: e.tensor_tensor(out=w2[:], in0=lbim[:], in1=aim[:], op=ALU.mult), reads=[b_lbim, b_aim], writes=[b_w2])
    S.op(D, lambda e: e.tensor_tensor(out=w1[:], in0=w1[:], in1=w2[:], op=ALU.add), reads=[b_w1, b_w2], writes=[b_w1])
    S.op(D, lambda e: e.tensor_tensor(out=fre[:], in0=w1[:], in1=rden[:], op=ALU.mult), reads=[b_w1, b_rden], writes=[b_fre])
    S.op(D, lambda e: e.tensor_tensor(out=w1[:], in0=lbim[:], in1=are[:], op=ALU.mult), reads=[b_lbim, b_are], writes=[b_w1])
    S.op(D, lambda e: e.tensor_tensor(out=w2[:], in0=nre[:], in1=aim[:], op=ALU.mult), reads=[b_nre, b_aim], writes=[b_w2])
    S.op(D, lambda e: e.tensor_tensor(out=w1[:], in0=w1[:], in1=w2[:], op=ALU.subtract), reads=[b_w1, b_w2], writes=[b_w1])
    S.op(D, lambda e: e.tensor_tensor(out=fim[:], in0=w1[:], in1=rden[:], op=ALU.mult), reads=[b_w1, b_rden], writes=[b_fim])
    S.op(D, lambda e: e.tensor_scalar(out=nfim[:], in0=fim[:], scalar1=-1.0, scalar2=None, op0=ALU.mult), reads=[b_fim], writes=[b_nfim])

    uTb = A.sb("uTb", [128, 2, 4096], BF16); b_uTb = Buf()
    yacc = A.sb("yacc", [128, 2, 4096], F32); b_yacc = Buf()
    XR = A.sb("XR", [128, SEG], F32); b_XR = Buf()
    XI = A.sb("XI", [128, SEG], F32); b_XI = Buf()
    T1 = A.sb("T1", [128, SEG], F32); b_T1 = Buf()
    T2 = A.sb("T2", [128, SEG], F32); b_T2 = Buf()
    T3 = A.sb("T3", [128, SEG], F32); b_T3 = Buf()
    Ct = A.sb("Ct", [128, SEG], F32); b_Ct = Buf()
    St = A.sb("St", [128, SEG], F32); b_St = Buf()
    KI = A.sb("KI", [128, SEG], I32); b_KI = Buf()
    sre = A.sb("sre", [128, SEG], BF16); b_sre = Buf()
    sim_ = A.sb("sim", [128, SEG], BF16); b_sim = Buf()
    carry = A.sb("carry", [128, 2], F32); b_carry = Buf()
    bst = [A.sb(f"bst{i}", [128, 128], F32) for i in range(4)]; b_bst = [Buf() for _ in range(4)]
    wpre = [A.sb(f"wpre{i}", [128, 128], BF16) for i in range(2)]; b_wpre = [Buf() for _ in range(2)]
    lx = A.sb("lx", [128, 2, 128], BF16); b_lx = Buf()
    ly = A.sb("ly", [128, 2, 128], BF16); b_ly = Buf()
    pT = A.ps("pT", [128, 256], BF16); b_pT = Buf()
    pX = [A.ps(f"pX{i}", [128, 512], F32) for i in range(4)]; b_pX = [Buf() for _ in range(4)]
    pY = [A.ps(f"pY{i}", [128, 512], F32) for i in range(2)]; b_pY = [Buf() for _ in range(2)]

    for ct in range(2):
        for sg in range(2):
            st, bs = (T1, b_T1) if sg == 0 else (T2, b_T2)
            S.dma("sp", st[:], uT[ct, :, sg * SEG:(sg + 1) * SEG], writes=[bs])
            S.op("pool", lambda e, st=st, ct=ct, sg=sg: e.tensor_copy(out=uTb[:, ct, sg * SEG:(sg + 1) * SEG], in_=st[:]), reads=[bs], writes=[b_uTb])

    ycnt = [0]
    for gp in range(NGP):
        ct = gp // 4
        for dr in range(2):
            col = dr * NGP + gp
            cs = slice(col, col + 1)
            S.dma("sp", bst[0][:], Bre[dr, gp], writes=[b_bst[0]])
            S.dma("sp", bst[1][:], Bim[dr, gp], writes=[b_bst[1]])
            S.dma("sp", bst[2][:], Cre[dr, gp], writes=[b_bst[2]])
            S.dma("sp", bst[3][:], Cim[dr, gp], writes=[b_bst[3]])
            S.op("pool", lambda e, cs=cs: e.tensor_scalar(out=T3[:, 0:128], in0=bst[0][:], scalar1=fre[:, cs], scalar2=None, op0=ALU.mult),
                 reads=[b_bst[0], b_fre], writes=[b_T3])
            S.op(D, lambda e, cs=cs: e.scalar_tensor_tensor(out=wpre[0][:], in0=bst[1][:], scalar=nfim[:, cs], in1=T3[:, 0:128], op0=ALU.mult, op1=ALU.add),
                 reads=[b_bst[1], b_nfim, b_T3], writes=[b_wpre[0]])
            S.op("pool", lambda e, cs=cs: e.tensor_scalar(out=T3[:, 128:256], in0=bst[0][:], scalar1=fim[:, cs], scalar2=None, op0=ALU.mult),
                 reads=[b_bst[0], b_fim], writes=[b_T3])
            S.op(D, lambda e, cs=cs: e.scalar_tensor_tensor(out=wpre[1][:], in0=bst[1][:], scalar=fre[:, cs], in1=T3[:, 128:256], op0=ALU.mult, op1=ALU.add),
                 reads=[b_bst[1], b_fre, b_T3], writes=[b_wpre[1]])
            for ri in range(2):
                S.op("pe", lambda e, ri=ri: e.transpose(out=pT[:, ri * 128:(ri + 1) * 128], in_=wpre[ri][:], identity=ident[:]),
                     reads=[b_wpre[ri], b_ident], writes=[b_pT])
            S.op("act", lambda e: e.copy(out=lx[:].rearrange("p a b -> p (a b)"), in_=pT[:]), reads=[b_pT], writes=[b_lx])
            S.op("act", lambda e: e.copy(out=ly[:, 0, :], in_=bst[2][:]), reads=[b_bst[2]], writes=[b_ly])
            S.op("act", lambda e: e.mul(out=ly[:, 1, :], in_=bst[3][:], mul=-1.0), reads=[b_bst[3]], writes=[b_ly])
            segs = [0, 1] if dr == 0 else [1, 0]
            for si, sg in enumerate(segs):
                t0 = sg * SEG
                if dr == 0:
                    io, off = iota[:, :], float(t0)
                else:
                    io, off = iota[:, ::-1], float(4095 - t0 - (SEG - 1))
                S.op(D, lambda e, io=io, off=off, cs=cs: e.tensor_scalar(out=T1[:], in0=io, scalar1=off, scalar2=thn[:, cs], op0=ALU.add, op1=ALU.mult),
                     reads=[b_iota, b_thn], writes=[b_T1])
                emit_sincos(S, T1[:], b_T1, KI[:], b_KI, T2[:], b_T2, T3[:], b_T3, Ct[:], b_Ct, St[:], b_St, halfpi[:, 0:1], b_hp)
                for blk in range(4):
                    ts = slice(t0 + blk * 512, t0 + (blk + 1) * 512)
                    for ri in range(2):
                        pi = (blk % 2) * 2 + ri
                        S.op("pe", lambda e, ri=ri, pi=pi, ts=ts: e.matmul(pX[pi][:], lhsT=lx[:, ri, :], rhs=uTb[:, ct, ts], start=True, stop=True),
                             reads=[b_lx, b_uTb], writes=[b_pX[pi]])
                        dst, bd = (XR, b_XR) if ri == 0 else (XI, b_XI)
                        S.op("act", lambda e, dst=dst, pi=pi, blk=blk: e.copy(out=dst[:, blk * 512:(blk + 1) * 512], in_=pX[pi][:]), reads=[b_pX[pi]], writes=[bd])
                S.op(D, lambda e: e.tensor_tensor(out=T1[:], in0=Ct[:], in1=XR[:], op=ALU.mult), reads=[b_Ct, b_XR], writes=[b_T1])
                S.op("pool", lambda e: e.tensor_tensor(out=T2[:], in0=St[:], in1=XI[:], op=ALU.mult), reads=[b_St, b_XI], writes=[b_T2])
                S.op(D, lambda e: e.tensor_tensor(out=T1[:], in0=T1[:], in1=T2[:], op=ALU.add), reads=[b_T1, b_T2], writes=[b_T1])
                S.op("pool", lambda e: e.tensor_tensor(out=T2[:], in0=Ct[:], in1=XI[:], op=ALU.mult), reads=[b_Ct, b_XI], writes=[b_T2])
                S.op("pool", lambda e: e.tensor_tensor(out=T3[:], in0=St[:], in1=XR[:], op=ALU.mult), reads=[b_St, b_XR], writes=[b_T3])
                S.op("pool", lambda e: e.tensor_tensor(out=T2[:], in0=T2[:], in1=T3[:], op=ALU.subtract), reads=[b_T2, b_T3], writes=[b_T2])
                rmul = mag[:, cs].broadcast_to([128, SEG])
                for zi, (src, bsrc, dst, bdst) in enumerate(((T1, b_T1, XR, b_XR), (T2, b_T2, XI, b_XI))):
                    init = 0.0 if si == 0 else carry[:, zi:zi + 1]
                    if dr == 0:
                        o_ap, d_ap = dst[:, :], src[:, :]
                    else:
                        o_ap, d_ap = dst[:, ::-1], src[:, ::-1]
                    S.op(D, lambda e, o_ap=o_ap, d_ap=d_ap, init=init, rmul=rmul: e.tensor_tensor_scan(out=o_ap, data0=rmul, data1=d_ap, initial=init,
                                                                                                       op0=ALU.mult, op1=ALU.add),
                         reads=[bsrc, b_mag, b_carry], writes=[bdst])
                if si == 0:
                    ccol = SEG - 1 if dr == 0 else 0
                    S.op("pool", lambda e, ccol=ccol: e.tensor_copy(out=carry[:, 0:1], in_=XR[:, ccol:ccol + 1]), reads=[b_XR], writes=[b_carry])
                    S.op("pool", lambda e, ccol=ccol: e.tensor_copy(out=carry[:, 1:2], in_=XI[:, ccol:ccol + 1]), reads=[b_XI], writes=[b_carry])
                S.op(D, lambda e: e.tensor_tensor(out=T1[:], in0=Ct[:], in1=XR[:], op=ALU.mult), reads=[b_Ct, b_XR], writes=[b_T1])
                S.op("pool", lambda e: e.tensor_tensor(out=T3[:], in0=St[:], in1=XI[:], op=ALU.mult), reads=[b_St, b_XI], writes=[b_T3])
                S.op(D, lambda e: e.tensor_tensor(out=sre[:], in0=T1[:], in1=T3[:], op=ALU.subtract), reads=[b_T1, b_T3], writes=[b_sre])
                S.op("pool", lambda e: e.tensor_tensor(out=T2[:], in0=St[:], in1=XR[:], op=ALU.mult), reads=[b_St, b_XR], writes=[b_T2])
                S.op("pool", lambda e: e.tensor_tensor(out=T3[:], in0=Ct[:], in1=XI[:], op=ALU.mult), reads=[b_Ct, b_XI], writes=[b_T3])
                S.op("pool", lambda e: e.tensor_tensor(out=sim_[:], in0=T2[:], in1=T3[:], op=ALU.add), reads=[b_T2, b_T3], writes=[b_sim])
                first = (gp % 4 == 0 and dr == 0)
                for blk in range(4):
                    pi = ycnt[0] % 2
                    ycnt[0] += 1
                    bs_ = slice(blk * 512, (blk + 1) * 512)
                    ts = slice(t0 + blk * 512, t0 + (blk + 1) * 512)
                    S.op("pe", lambda e, pi=pi, bs_=bs_: e.matmul(pY[pi][:], lhsT=ly[:, 0, :], rhs=sre[:, bs_], start=True, stop=False),
                         reads=[b_ly, b_sre], writes=[b_pY[pi]])
                    S.op("pe", lambda e, pi=pi, bs_=bs_: e.matmul(pY[pi][:], lhsT=ly[:, 1, :], rhs=sim_[:, bs_], start=False, stop=True),
                         reads=[b_ly, b_sim], writes=[b_pY[pi]])
                    if first:
                        S.op("act", lambda e, pi=pi, ts=ts: e.copy(out=yacc[:, ct, ts], in_=pY[pi][:]), reads=[b_pY[pi]], writes=[b_yacc])
                    else:
                        S.op(D, lambda e, pi=pi, ts=ts: e.tensor_tensor(out=yacc[:, ct, ts], in0=pY[pi][:], in1=yacc[:, ct, ts], op=ALU.add),
                             reads=[b_pY[pi], b_yacc], writes=[b_yacc])
    for ct in range(2):
        S.dma("sp", yT[ct], yacc[:, ct, :], reads=[b_yacc])
    S.flush()
    return nc


def s5_host_layout(inputs, li, half):
    g0 = 16 * half
    def colmat(a):
        out = np.zeros((128, 2 * NGP), np.float32)
        for dr in range(2):
            for gp in range(NGP):
                for g2 in range(2):
                    out[g2 * 64:(g2 + 1) * 64, dr * NGP + gp] = a[dr, g0 + 2 * gp + g2]
        return out
    are = colmat(inputs["s5_a_re"][li]); aim = colmat(inputs["s5_a_im"][li])
    ldt = colmat(np.repeat(inputs["s5_log_dt"][li][:, :, None], 64, axis=2))
    Bre = np.zeros((2, NGP, 128, 128), np.float32); Bim = np.zeros_like(Bre); Cre = np.zeros_like(Bre); Cim = np.zeros_like(Bre)
    for dr in range(2):
        for gp in range(NGP):
            for g2 in range(2):
                g = g0 + 2 * gp + g2
                c0 = 32 * (gp % 4) + 16 * g2
                Bre[dr, gp, g2 * 64:(g2 + 1) * 64, c0:c0 + 16] = inputs["s5_b_re"][li, dr, g]
                Bim[dr, gp, g2 * 64:(g2 + 1) * 64, c0:c0 + 16] = inputs["s5_b_im"][li, dr, g]
                Cre[dr, gp, g2 * 64:(g2 + 1) * 64, c0:c0 + 16] = inputs["s5_c_re"][li, dr, g].T
                Cim[dr, gp, g2 * 64:(g2 + 1) * 64, c0:c0 + 16] = inputs["s5_c_im"][li, dr, g].T
    return dict(areT=are, aimT=aim, ldtT=ldt, Bre=Bre, Bim=Bim, Cre=Cre, Cim=Cim)


def rms_rstd(S, src, b_src, junk, b_junk, st, b_st, epsT, b_eps, width):
    S.op("act", lambda e: e.activation(out=junk, in_=src, func=AF.Square, accum_out=st[:, 0:1]), reads=[b_src], writes=[b_junk, b_st])
    S.op("act", lambda e: e.activation(out=st[:, 1:2], in_=st[:, 0:1], func=AF.Sqrt, bias=epsT, scale=1.0 / width), reads=[b_st, b_eps], writes=[b_st])
    S.op("dve", lambda e: e.reciprocal(out=st[:, 2:3], in_=st[:, 1:2]), reads=[b_st], writes=[b_st])


def build_Q3(lam_init):
    C = Ctx()
    nc, S, A = C.nc, C.S, C.A
    h = C.inp("h", [NT, 1024]); g_pre = C.inp("g_pre", [1024]); g_post = C.inp("g_post", [1024])
    OAtok = C.inp("OAtok", [NT, 8, 128]); lamv = C.inp("lamv", [256]); subln = C.inp("subln", [128])
    ybT = C.inp("ybT", [4, 128, NT], BF16); ycT = C.inp("ycT", [4, 128, NT], BF16)
    ysT = C.inp("ysT", [4, 128, NT]); uTo = C.inp("uTo", [4, 128, NT]); dcol = C.inp("dcol", [128, 4])
    w_glu = C.inp("w_glu", [512, 1024]); w_gate = C.inp("w_gate", [1024, 4096]); w_br = C.inp("w_br", [4, 512, 1024]); w_out = C.inp("w_out", [1024, 1024])
    identd = C.inp("identd", [128, 128], BF16)
    h1 = C.out("h1", [NT, 1024])

    HT = 1024
    HTI = HT // 128
    D = "dve"
    ident = A.sb("ident", [128, 128], BF16); b_ident = Buf()
    S.dma("sp", ident[:], identd[:, :], writes=[b_ident])
    epsT = A.sb("epsT", [128, 1], F32); b_eps = Buf()
    S.op(D, lambda e: e.memset(epsT[:], EPS), writes=[b_eps])
    g_bc = A.sb("g_bc", [128, 1024], F32); b_g = Buf()
    S.dma("pool", g_bc[:], g_pre.partition_broadcast(128), writes=[b_g])
    gp_bc = A.sb("gp_bc", [128, 1024], F32); b_gp = Buf()
    S.dma("pool", gp_bc[:], g_post.partition_broadcast(128), writes=[b_gp])
    lam_bc = A.sb("lam_bc", [128, 256], F32); b_lam = Buf()
    S.dma("pool", lam_bc[:], lamv.partition_broadcast(128), writes=[b_lam])
    sub_bc = A.sb("sub_bc", [128, 128], F32); b_sub = Buf()
    S.dma("pool", sub_bc[:], subln.partition_broadcast(128), writes=[b_sub])
    S.op("act", lambda e: e.mul(out=sub_bc[:], in_=sub_bc[:], mul=1.0 - lam_init), reads=[b_sub], writes=[b_sub])
    dc = A.sb("dc", [128, 4], F32); b_dc = Buf()
    S.dma("sp", dc[:], dcol[:, :], writes=[b_dc])
    lt = A.sb("lt", [128, 128], F32); b_lt = Buf()
    ls = A.sb("ls", [128, 8], F32); b_ls = Buf()
    S.op(D, lambda e: e.tensor_tensor(out=lt[:, 0:64], in0=lam_bc[:, 0:64], in1=lam_bc[:, 64:128], op=ALU.mult), reads=[b_lam], writes=[b_lt])
    S.op(D, lambda e: e.tensor_tensor(out=lt[:, 64:128], in0=lam_bc[:, 128:192], in1=lam_bc[:, 192:256], op=ALU.mult), reads=[b_lam], writes=[b_lt])
    S.op(D, lambda e: e.tensor_reduce(out=ls[:, 0:2], in_=lt[:].rearrange("p (a d) -> p a d", a=2), axis=AX.X, op=ALU.add), reads=[b_lt], writes=[b_ls])
    S.op("act", lambda e: e.activation(out=ls[:, 2:4], in_=ls[:, 0:2], func=AF.Exp), reads=[b_ls], writes=[b_ls])
    S.op(D, lambda e: e.tensor_tensor(out=ls[:, 4:5], in0=ls[:, 3:4], in1=ls[:, 2:3], op=ALU.subtract), reads=[b_ls], writes=[b_ls])
    S.op(D, lambda e: e.tensor_scalar_add(out=ls[:, 5:6], in0=ls[:, 4:5], scalar1=-lam_init), reads=[b_ls], writes=[b_ls])
    nlam = ls[:, 5:6]

    stg = [A.sb(f"stg{i}", [128, 1024], F32) for i in range(2)]; b_stg = [Buf() for _ in range(2)]
    sidx = [0]

    def load_w(dst, b_dst, src, ktiles, cols):
        for kt in range(ktiles):
            i = sidx[0] % 2
            sidx[0] += 1
            S.dma("sp", stg[i][:, 0:cols], src[kt * 128:(kt + 1) * 128, :], writes=[b_stg[i]])
            eng = "pool" if i else "act"
            if eng == "act":
                S.op("act", lambda e, i=i, kt=kt: e.copy(out=dst[:, kt, 0:cols], in_=stg[i][:, 0:cols]), reads=[b_stg[i]], writes=[b_dst])
            else:
                S.op("pool", lambda e, i=i, kt=kt: e.tensor_copy(out=dst[:, kt, 0:cols], in_=stg[i][:, 0:cols]), reads=[b_stg[i]], writes=[b_dst])

    wg = A.sb("wg", [128, 8, 1024], BF16); b_wg = Buf()
    wb = A.sb("wb", [128, 4, 1024], BF16); b_wb = Buf()
    mixed = A.sb("mixed", [128, HTI, 1024], F32); b_mixed = [Buf() for _ in range(HTI)]
    xnT = A.sb("xnT", [128, 8, HT], BF16); b_xnT = Buf()
    yaT = A.sb("yaT", [128, 4, HT], BF16); b_yaT = Buf()
    gT = A.sb("gT", [128, 4, HT], BF16); b_gT = Buf()
    sgT = A.sb("sgT", [128, 4, HT], BF16); b_sgT = Buf()
    ydT = A.sb("ydT", [128, 4, HT], BF16); b_ydT = Buf()
    ht = [A.sb(f"ht{i}", [128, 1024], F32) for i in range(2)]; b_ht = [Buf() for _ in range(2)]
    junk = A.sb("junk", [128, 1024], F32); b_junk = Buf()
    st = [A.sb(f"st{i}", [128, 8], F32) for i in range(2)]; b_st = [Buf() for _ in range(2)]
    xn = [A.sb(f"xn{i}", [128, 1024], BF16) for i in range(2)]; b_xn = [Buf() for _ in range(2)]
    oat = [A.sb(f"oat{i}", [128, 8, 128], F32) for i in range(2)]; b_oat = [Buf() for _ in range(2)]
    cmb = A.sb("cmb", [128, 512], F32); b_cmb = Buf()
    cm2 = A.sb("cm2", [128, 512], F32); b_cm2 = Buf()
    yab = A.sb("yab", [128, 512], BF16); b_yab = Buf()
    e1 = A.sb("e1", [128, HT], F32); b_e1 = Buf()
    e2 = A.sb("e2", [128, HT], F32); b_e2 = Buf()
    e3 = A.sb("e3", [128, HT], F32); b_e3 = Buf()
    sg = [A.sb(f"sg{i}", [128, 512], F32) for i in range(2)]; b_sg = [Buf() for _ in range(2)]
    tm = [A.sb(f"tm{i}", [128, 512], F32) for i in range(2)]; b_tm = [Buf() for _ in range(2)]
    osb = A.sb("osb", [128, 1024], F32); b_osb = Buf()
    mT = A.sb("mT", [128, 8, 128], BF16); b_mT = Buf()
    pT = A.ps("pT", [128, 1024], BF16); b_pT = Buf()
    pA = [A.ps(f"pA{i}", [128, 512], F32) for i in range(2)]; b_pA = [Buf() for _ in range(2)]
    pB = [A.ps(f"pB{i}", [128, 512], F32) for i in range(2)]; b_pB = [Buf() for _ in range(2)]
    pG = [A.ps(f"pG{i}", [128, 512], F32) for i in range(2)]; b_pG = [Buf() for _ in range(2)]

    for th in range(NT // HT):
        tb0 = th * HT
        for t in range(HTI):
            i = t % 2
            r0 = tb0 + t * 128
            S.dma("sp", ht[i][:], h[r0:r0 + 128, :], writes=[b_ht[i]])
            S.dma("sp", oat[i][:], OAtok[r0:r0 + 128], writes=[b_oat[i]])
            rms_rstd(S, ht[i][:], b_ht[i], junk[:], b_junk, st[i], b_st[i], epsT[:, 0:1], b_eps, 1024)
            S.op(D, lambda e, i=i: e.scalar_tensor_tensor(out=xn[i][:], in0=ht[i][:], scalar=st[i][:, 2:3], in1=g_bc[:], op0=ALU.mult, op1=ALU.mult),
                 reads=[b_ht[i], b_st[i], b_g], writes=[b_xn[i]])
            for kt in range(8):
                S.op("pe", lambda e, i=i, kt=kt: e.transpose(out=pT[:, kt * 128:(kt + 1) * 128], in_=xn[i][:, kt * 128:(kt + 1) * 128], identity=ident[:]),
                     reads=[b_xn[i], b_ident], writes=[b_pT])
            S.op("act", lambda e, t=t: e.copy(out=xnT[:, :, t * 128:(t + 1) * 128], in_=pT[:].rearrange("p (k t) -> p k t", k=8)), reads=[b_pT], writes=[b_xnT])
            o4 = oat[i][:].rearrange("p (h c) d -> p h c d", c=2)
            c3 = cmb[:].rearrange("p (h d) -> p h d", h=4)
            S.op(D, lambda e, o4=o4, c3=c3: e.scalar_tensor_tensor(out=c3, in0=o4[:, :, 1, :], scalar=nlam, in1=o4[:, :, 0, :], op0=ALU.mult, op1=ALU.add),
                 reads=[b_oat[i], b_ls], writes=[b_cmb])
            S.op("pool", lambda e: e.tensor_tensor(out=cm2[:], in0=cmb[:], in1=cmb[:], op=ALU.mult), reads=[b_cmb], writes=[b_cm2])
            S.op(D, lambda e, i=i: e.tensor_reduce(out=st[i][:, 4:8], in_=cm2[:].rearrange("p (h d) -> p h d", h=4), axis=AX.X, op=ALU.add),
                 reads=[b_cm2], writes=[b_st[i]])
            S.op("act", lambda e, i=i: e.activation(out=st[i][:, 4:8], in_=st[i][:, 4:8], func=AF.Sqrt, bias=epsT[:, 0:1], scale=1.0 / 128),
                 reads=[b_st[i], b_eps], writes=[b_st[i]])
            S.op(D, lambda e, i=i: e.reciprocal(out=st[i][:, 4:8], in_=st[i][:, 4:8]), reads=[b_st[i]], writes=[b_st[i]])
            S.op(D, lambda e, i=i, c3=c3: e.tensor_tensor(out=cm2[:].rearrange("p (h d) -> p h d", h=4), in0=c3,
                                                         in1=st[i][:, 4:8].unsqueeze(2).broadcast_to([128, 4, 128]), op=ALU.mult),
                 reads=[b_cmb, b_st[i]], writes=[b_cm2])
            S.op("pool", lambda e: e.tensor_tensor(out=yab[:].rearrange("p (h d) -> p h d", h=4), in0=cm2[:].rearrange("p (h d) -> p h d", h=4),
                                                  in1=sub_bc[:].unsqueeze(1).broadcast_to([128, 4, 128]), op=ALU.mult),
                 reads=[b_cm2, b_sub], writes=[b_yab])
            for kt in range(4):
                S.op("pe", lambda e, kt=kt: e.transpose(out=pT[:, kt * 128:(kt + 1) * 128], in_=yab[:, kt * 128:(kt + 1) * 128], identity=ident[:]),
                     reads=[b_yab, b_ident], writes=[b_pT])
            S.op("act", lambda e, t=t: e.copy(out=yaT[:, :, t * 128:(t + 1) * 128], in_=pT[:, 0:512].rearrange("p (k t) -> p k t", k=4)), reads=[b_pT], writes=[b_yaT])
        load_w(wb, b_wb, w_glu, 4, 1024)
        for ct in range(4):
            S.dma("sp", e1[:], ysT[ct, :, tb0:tb0 + HT], writes=[b_e1])
            S.dma("sp", e2[:], uTo[ct, :, tb0:tb0 + HT], writes=[b_e2])
            S.op(D, lambda e, ct=ct: e.scalar_tensor_tensor(out=e1[:], in0=e2[:], scalar=dc[:, ct:ct + 1], in1=e1[:], op0=ALU.mult, op1=ALU.add),
                 reads=[b_e1, b_e2, b_dc], writes=[b_e1])
            S.op("pool", lambda e: e.tensor_tensor(out=e2[:], in0=e1[:], in1=e1[:], op=ALU.mult), reads=[b_e1], writes=[b_e2])
            S.op("pool", lambda e: e.tensor_scalar(out=e2[:], in0=e2[:], scalar1=0.044715, scalar2=1.0, op0=ALU.mult, op1=ALU.add), reads=[b_e2], writes=[b_e2])
            S.op(D, lambda e: e.tensor_tensor(out=e2[:], in0=e2[:], in1=e1[:], op=ALU.mult), reads=[b_e1, b_e2], writes=[b_e2])
            S.op("act", lambda e: e.activation(out=e3[:], in_=e2[:], func=AF.Sigmoid, scale=2.0 * math.sqrt(2.0 / math.pi)), reads=[b_e2], writes=[b_e3])
            S.op(D, lambda e, ct=ct: e.tensor_tensor(out=gT[:, ct, :], in0=e1[:], in1=e3[:], op=ALU.mult), reads=[b_e1, b_e3], writes=[b_gT])
        gcnt = 0
        for och in (4, 5, 6, 7, 0, 1, 2, 3):
            for tb in range(HT // 512):
                pi = gcnt % 2
                gcnt += 1
                ts = slice(tb * 512, (tb + 1) * 512)
                for kt in range(4):
                    S.op("pe", lambda e, pi=pi, kt=kt, och=och, ts=ts: e.matmul(pG[pi][:], lhsT=wb[:, kt, och * 128:(och + 1) * 128], rhs=gT[:, kt, ts],
                                                                                start=(kt == 0), stop=(kt == 3)), reads=[b_wb, b_gT], writes=[b_pG[pi]])
                if och >= 4:
                    S.op("act", lambda e, pi=pi, och=och, ts=ts: e.activation(out=sgT[:, och - 4, ts], in_=pG[pi][:], func=AF.Sigmoid), reads=[b_pG[pi]], writes=[b_sgT])
                else:
                    S.op(D, lambda e, pi=pi, och=och, ts=ts: e.tensor_tensor(out=ydT[:, och, ts], in0=pG[pi][:], in1=sgT[:, och, ts], op=ALU.mult),
                         reads=[b_pG[pi], b_sgT], writes=[b_ydT])
        for b in range(4):
            load_w(wg, b_wg, w_gate[:, b * 1024:(b + 1) * 1024], 8, 1024)
            load_w(wb, b_wb, w_br[b], 4, 1024)
            if b == 0:
                yT_, b_yT = yaT, b_yaT
            elif b == 3:
                yT_, b_yT = ydT, b_ydT
            else:
                src = ybT if b == 1 else ycT
                for kt in range(4):
                    S.dma("sp", gT[:, kt, :], src[kt, :, tb0:tb0 + HT], writes=[b_gT])
                yT_, b_yT = gT, b_gT
            for t in range(HTI):
                tsl = slice(t * 128, (t + 1) * 128)
                for hc in range(2):
                    cs = slice(hc * 512, (hc + 1) * 512)
                    for kt in range(4):
                        S.op("pe", lambda e, kt=kt, hc=hc, tsl=tsl, cs=cs, yT_=yT_: e.matmul(pA[hc][:], lhsT=yT_[:, kt, tsl], rhs=wb[:, kt, cs],
                                                                                             start=(kt == 0), stop=(kt == 3)), reads=[b_yT, b_wb], writes=[b_pA[hc]])
                    for kt in range(8):
                        S.op("pe", lambda e, kt=kt, hc=hc, tsl=tsl, cs=cs: e.matmul(pB[hc][:], lhsT=xnT[:, kt, tsl], rhs=wg[:, kt, cs],
                                                                                    start=(kt == 0), stop=(kt == 7)), reads=[b_xnT, b_wg], writes=[b_pB[hc]])
                    S.op("act", lambda e, hc=hc: e.activation(out=sg[hc][:], in_=pB[hc][:], func=AF.Sigmoid), reads=[b_pB[hc]], writes=[b_sg[hc]])
                    if b == 0:
                        S.op(D, lambda e, hc=hc, t=t, cs=cs: e.tensor_tensor(out=mixed[:, t, cs], in0=pA[hc][:], in1=sg[hc][:], op=ALU.mult),
                             reads=[b_pA[hc], b_sg[hc]], writes=[b_mixed[t]])
                    else:
                        S.op(D, lambda e, hc=hc: e.tensor_tensor(out=tm[hc][:], in0=pA[hc][:], in1=sg[hc][:], op=ALU.mult),
                             reads=[b_pA[hc], b_sg[hc]], writes=[b_tm[hc]])
                        S.op("pool", lambda e, hc=hc, t=t, cs=cs: e.tensor_tensor(out=mixed[:, t, cs], in0=mixed[:, t, cs], in1=tm[hc][:], op=ALU.add),
                             reads=[b_tm[hc], b_mixed[t]], writes=[b_mixed[t]])
        load_w(wg, b_wg, w_out, 8, 1024)
        for t in range(HTI):
            i = t % 2
            r0 = tb0 + t * 128
            S.dma("sp", ht[i][:], h[r0:r0 + 128, :], writes=[b_ht[i]])
            S.op("act", lambda e, i=i, t=t: e.copy(out=xn[i][:], in_=mixed[:, t, :]), reads=[b_mixed[t]], writes=[b_xn[i]])
            for kt in range(8):
                S.op("pe", lambda e, i=i, kt=kt: e.transpose(out=pT[:, kt * 128:(kt + 1) * 128], in_=xn[i][:, kt * 128:(kt + 1) * 128], identity=ident[:]),
                     reads=[b_xn[i], b_ident], writes=[b_pT])
            S.op("act", lambda e: e.copy(out=mT[:].rearrange("p k t -> p (k t)"), in_=pT[:]), reads=[b_pT], writes=[b_mT])
            for hc in range(2):
                cs = slice(hc * 512, (hc + 1) * 512)
                for kt in range(8):
                    S.op("pe", lambda e, kt=kt, hc=hc, cs=cs: e.matmul(pA[hc][:], lhsT=mT[:, kt, :], rhs=wg[:, kt, cs], start=(kt == 0), stop=(kt == 7)),
                         reads=[b_mT, b_wg], writes=[b_pA[hc]])
                if hc == 0:
                    S.op("act", lambda e, hc=hc, cs=cs: e.copy(out=osb[:, cs], in_=pA[hc][:]), reads=[b_pA[hc]], writes=[b_osb])
                else:
                    S.op(D, lambda e, hc=hc, cs=cs: e.tensor_copy(out=osb[:, cs], in_=pA[hc][:]), reads=[b_pA[hc]], writes=[b_osb])
            rms_rstd(S, osb[:], b_osb, junk[:], b_junk, st[i], b_st[i], epsT[:, 0:1], b_eps, 1024)
            S.op(D, lambda e, i=i: e.scalar_tensor_tensor(out=osb[:], in0=osb[:], scalar=st[i][:, 2:3], in1=gp_bc[:], op0=ALU.mult, op1=ALU.mult),
                 reads=[b_osb, b_st[i], b_gp], writes=[b_osb])
            S.op("pool", lambda e, i=i: e.tensor_tensor(out=ht[i][:], in0=ht[i][:], in1=osb[:], op=ALU.add), reads=[b_ht[i], b_osb], writes=[b_ht[i]])
            S.dma("sp", h1[r0:r0 + 128, :], ht[i][:], reads=[b_ht[i]])
    S.flush()
    return nc


def build_Q4():
    C = Ctx()
    nc, S, A = C.nc, C.S, C.A
    h1 = C.inp("h1", [NT, 1024]); g_pre = C.inp("g_pre", [1024]); g_post = C.inp("g_post", [1024])
    w_fi = C.inp("w_fi", [1024, 4096]); w_fo = C.inp("w_fo", [4096, 1024])
    identd = C.inp("identd", [128, 128], BF16)
    h2 = C.out("h2", [NT, 1024])
    D = "dve"
    ident = A.sb("ident", [128, 128], BF16); b_ident = Buf()
    S.dma("sp", ident[:], identd[:, :], writes=[b_ident])
    epsT = A.sb("epsT", [128, 1], F32); b_eps = Buf()
    S.op(D, lambda e: e.memset(epsT[:], EPS), writes=[b_eps])
    g_bc = A.sb("g_bc", [128, 1024], F32); b_g = Buf()
    S.dma("pool", g_bc[:], g_pre.partition_broadcast(128), writes=[b_g])
    gp_bc = A.sb("gp_bc", [128, 1024], F32); b_gp = Buf()
    S.dma("pool", gp_bc[:], g_post.partition_broadcast(128), writes=[b_gp])
    stg = [A.sb(f"stg{i}", [128, 1024], F32) for i in range(2)]; b_stg = [Buf() for _ in range(2)]
    sidx = [0]

    def load_w(dst, b_dst, src, ktiles, cols):
        for kt in range(ktiles):
            i = sidx[0] % 2
            sidx[0] += 1
            S.dma("sp", stg[i][:, 0:cols], src[kt * 128:(kt + 1) * 128, :], writes=[b_stg[i]])
            if i == 0:
                S.op("pool", lambda e, i=i, kt=kt: e.tensor_copy(out=dst[:, kt, 0:cols], in_=stg[i][:, 0:cols]), reads=[b_stg[i]], writes=[b_dst])
            else:
                S.op(D, lambda e, i=i, kt=kt: e.tensor_copy(out=dst[:, kt, 0:cols], in_=stg[i][:, 0:cols]), reads=[b_stg[i]], writes=[b_dst])

    fnT = A.sb("fnT", [128, 8, NT], BF16); b_fnT = Buf()
    facc = A.sb("facc", [128, NTI, 1024], F32); b_facc = [Buf() for _ in range(NTI)]
    hidT = A.sb("hidT", [128, 4, NT], BF16); b_hidT = Buf()
    wfi = A.sb("wfi", [128, 8, 512], BF16); b_wfi = Buf()
    wfo = A.sb("wfo", [128, 4, 1024], BF16); b_wfo = Buf()
    ht = [A.sb(f"ht{i}", [128, 1024], F32) for i in range(2)]; b_ht = [Buf() for _ in range(2)]
    junk = A.sb("junk", [128, 1024], F32); b_junk = Buf()
    st = [A.sb(f"st{i}", [128, 8], F32) for i in range(2)]; b_st = [Buf() for _ in range(2)]
    xn = [A.sb(f"xn{i}", [128, 1024], BF16) for i in range(2)]; b_xn = [Buf() for _ in range(2)]
    r32 = [A.sb(f"r32{i}", [128, 512], F32) for i in range(2)]; b_r32 = [Buf() for _ in range(2)]
    pT = A.ps("pT", [128, 1024], BF16); b_pT = Buf()
    pH = [A.ps(f"pH{i}", [128, 512], F32) for i in range(2)]; b_pH = [Buf() for _ in range(2)]
    pF = [A.ps(f"pF{i}", [128, 512], F32) for i in range(4)]; b_pF = [Buf() for _ in range(4)]

    for t in range(NTI):
        i = t % 2
        r0 = t * 128
        S.dma("sp", ht[i][:], h1[r0:r0 + 128, :], writes=[b_ht[i]])
        rms_rstd(S, ht[i][:], b_ht[i], junk[:], b_junk, st[i], b_st[i], epsT[:, 0:1], b_eps, 1024)
        S.op(D, lambda e, i=i: e.scalar_tensor_tensor(out=xn[i][:], in0=ht[i][:], scalar=st[i][:, 2:3], in1=g_bc[:], op0=ALU.mult, op1=ALU.mult),
             reads=[b_ht[i], b_st[i], b_g], writes=[b_xn[i]])
        for kt in range(8):
            S.op("pe", lambda e, i=i, kt=kt: e.transpose(out=pT[:, kt * 128:(kt + 1) * 128], in_=xn[i][:, kt * 128:(kt + 1) * 128], identity=ident[:]),
                 reads=[b_xn[i], b_ident], writes=[b_pT])
        S.op("act", lambda e, t=t: e.copy(out=fnT[:, :, t * 128:(t + 1) * 128], in_=pT[:].rearrange("p (k t) -> p k t", k=8)), reads=[b_pT], writes=[b_fnT])
    hcnt = 0
    fcnt = 0
    for c in range(8):
        load_w(wfi, b_wfi, w_fi[:, c * 512:(c + 1) * 512], 8, 512)
        load_w(wfo, b_wfo, w_fo[c * 512:(c + 1) * 512, :], 4, 1024)
        for j in range(4):
            for tb in range(4):
                pi = hcnt % 2
                hcnt += 1
                ts = slice(tb * 512, (tb + 1) * 512)
                for kt in range(8):
                    S.op("pe", lambda e, pi=pi, kt=kt, j=j, ts=ts: e.matmul(pH[pi][:], lhsT=wfi[:, kt, j * 128:(j + 1) * 128], rhs=fnT[:, kt, ts],
                                                                            start=(kt == 0), stop=(kt == 7)), reads=[b_wfi, b_fnT], writes=[b_pH[pi]])
                S.op("act", lambda e, pi=pi: e.activation(out=r32[pi][:], in_=pH[pi][:], func=AF.Relu), reads=[b_pH[pi]], writes=[b_r32[pi]])
                S.op("pool", lambda e, pi=pi, j=j, ts=ts: e.tensor_tensor(out=hidT[:, j, ts], in0=r32[pi][:], in1=r32[pi][:], op=ALU.mult),
                     reads=[b_r32[pi]], writes=[b_hidT])
        for t in range(NTI):
            tsl = slice(t * 128, (t + 1) * 128)
            for hc in range(2):
                pi = fcnt % 4
                fcnt += 1
                cs = slice(hc * 512, (hc + 1) * 512)
                for j in range(4):
                    S.op("pe", lambda e, pi=pi, j=j, tsl=tsl, cs=cs: e.matmul(pF[pi][:], lhsT=hidT[:, j, tsl], rhs=wfo[:, j, cs], start=(j == 0), stop=(j == 3)),
                         reads=[b_hidT, b_wfo], writes=[b_pF[pi]])
                if c == 0:
                    S.op("act", lambda e, pi=pi, t=t, cs=cs: e.copy(out=facc[:, t, cs], in_=pF[pi][:]), reads=[b_pF[pi]], writes=[b_facc[t]])
                else:
                    S.op(D, lambda e, pi=pi, t=t, cs=cs: e.tensor_tensor(out=facc[:, t, cs], in0=pF[pi][:], in1=facc[:, t, cs], op=ALU.add),
                         reads=[b_pF[pi], b_facc[t]], writes=[b_facc[t]])
    for t in range(NTI):
        i = t % 2
        r0 = t * 128
        S.dma("sp", ht[i][:], h1[r0:r0 + 128, :], writes=[b_ht[i]])
        rms_rstd(S, facc[:, t, :], b_facc[t], junk[:], b_junk, st[i], b_st[i], epsT[:, 0:1], b_eps, 1024)
        S.op(D, lambda e, i=i, t=t: e.scalar_tensor_tensor(out=facc[:, t, :], in0=facc[:, t, :], scalar=st[i][:, 2:3], in1=gp_bc[:], op0=ALU.mult, op1=ALU.mult),
             reads=[b_facc[t], b_st[i], b_gp], writes=[b_facc[t]])
        S.op("pool", lambda e, i=i, t=t: e.tensor_tensor(out=ht[i][:], in0=ht[i][:], in1=facc[:, t, :], op=ALU.add), reads=[b_ht[i], b_facc[t]], writes=[b_ht[i]])
        S.dma("sp", h2[r0:r0 + 128, :], ht[i][:], reads=[b_ht[i]])
    S.flush()
    return nc


def _run(nc, in_maps):
    res = run_bass_kernel_spmd(nc, in_maps, core_ids=list(range(len(in_maps))))
    return res.results


def _c(a):
    return np.ascontiguousarray(a)


def kernel_unfused(**inputs):
    x = np.asarray(inputs["x"], dtype=np.float32)
    P = {k: np.asarray(v, dtype=np.float32) for k, v in inputs.items() if k != "x"}
    NCORE = 8
    ident = np.eye(128, dtype=BF)
    iota = _c(np.tile(np.arange(SEG, dtype=np.float32)[None, :], (128, 1)))
    toks = [np.arange(hf * NT, (hf + 1) * NT) for hf in range(2)]
    rope = [rope_tables(t // 64, t % 64, t) for t in toks]
    alibi = [alibi_tables(hf) for hf in range(2)]
    h = [_c(x[c // 2, toks[c % 2]]) for c in range(NCORE)]
    for li in range(2):
        lam_init = 0.8 - 0.6 * math.exp(-0.3 * li)
        gB = np.concatenate([np.tile(P["gqa_q_norm"][li], 8), np.tile(P["gqa_k_norm"][li], 2)])
        gC = np.concatenate([P["mla_q_norm"][li], P["mla_kv_norm"][li]])
        maps = []
        for c in range(NCORE):
            cb, sb_, cc, sc = rope[c % 2]
            maps.append(dict(h=h[c], g_pre=P["norm_pre_mix"][li], w_in=P["w_in"][li], gB=gB, gC=gC, w_uq=P["mla_w_uq"][li], w_ukv=P["mla_w_ukv"][li],
                             cosb=cb, sinb=sb_, cosc=cc, sinc=sc, identd=ident))
        rp = _run(build_P(), maps)
        maps = []
        for c in range(NCORE):
            b, hf = c // 2, c % 2
            own, par = rp[2 * b + hf], rp[2 * b + 1 - hf]
            cat = lambda k: np.concatenate([own[k], par[k]], axis=0)
            oa, ob, ockv, ockr = cat("oa"), cat("ob"), cat("ockv"), cat("ockr")
            QT = np.zeros((24, 96, NT), BF); KT = np.zeros((18, 96, 4096), BF)
            QT[0:8, 0:64] = own["oa"][:, 0:512].reshape(NT, 8, 64).transpose(1, 2, 0)
            QT[8:16, 0:64] = own["ob"][:, 0:512].reshape(NT, 8, 64).transpose(1, 2, 0)
            QT[16:24] = own["ocq"].reshape(NT, 8, 96).transpose(1, 2, 0)
            KT[0:8, 0:64] = oa[:, 512:1024].reshape(4096, 8, 64).transpose(1, 2, 0)
            KT[8:10, 0:64] = ob[:, 512:640].reshape(4096, 2, 64).transpose(1, 2, 0)
            kv = ockv.reshape(4096, 8, 128)
            KT[10:18, 0:64] = kv[:, :, 0:64].transpose(1, 2, 0)
            KT[10:18, 64:96] = ockr.T[None, :, :]
            VA = oa[:, 1024:1536].reshape(32, 128, 4, 128).transpose(2, 1, 0, 3)
            vb = ob[:, 640:768].reshape(32, 128, 2, 64).transpose(2, 1, 0, 3)
            vc = kv[:, :, 64:128].reshape(32, 128, 8, 64).transpose(2, 1, 0, 3)
            VBC = np.concatenate([vb, vc], axis=0)
            tO, tP = alibi[hf]
            maps.append(dict(QT=_c(QT), KT=_c(KT), VA=_c(VA), VBC=_c(VBC), tabOwn=tO, tabPar=tP))
        rq1 = _run(build_Q1(), maps)
        maps = []
        for c in range(NCORE):
            b, hf = c // 2, c % 2
            u_full = np.concatenate([rp[2 * b]["ou"], rp[2 * b + 1]["ou"]], axis=0)
            uT = _c(u_full.T.reshape(4, 128, 4096)[2 * hf:2 * hf + 2])
            m = s5_host_layout({k: P[k] for k in ("s5_a_re", "s5_a_im", "s5_log_dt", "s5_b_re", "s5_b_im", "s5_c_re", "s5_c_im")}, li, hf)
            m.update(uT=uT, iota=iota, identd=ident)
            maps.append(m)
        rq2 = _run(build_Q2(), maps)
        maps = []
        w_br = _c(np.stack([P["w_br_a"][li], P["w_br_b"][li], P["w_br_c"][li], P["w_br_d"][li]]))
        lamv = np.concatenate([P["diff_lam_q1"][li], P["diff_lam_k1"][li], P["diff_lam_q2"][li], P["diff_lam_k2"][li]])
        dcol = _c(P["s5_d"][li].reshape(4, 128).T)
        for c in range(NCORE):
            b, hf = c // 2, c % 2
            ys_full = np.concatenate([rq2[2 * b]["yT"], rq2[2 * b + 1]["yT"]], axis=0)
            ysT = _c(ys_full[:, :, toks[hf]])
            uTo = _c(rp[c]["ou"].T.reshape(4, 128, NT))
            OAtok = _c(rq1[c]["OA"].transpose(2, 0, 1))
            obc = rq1[c]["OBC"]
            maps.append(dict(h=h[c], g_pre=P["norm_pre_mix"][li], g_post=P["norm_post_mix"][li], OAtok=OAtok, lamv=lamv, subln=P["diff_subln"][li],
                             ybT=_c(obc[0:8].reshape(4, 128, NT)), ycT=_c(obc[8:16].reshape(4, 128, NT)), ysT=ysT, uTo=uTo, dcol=dcol,
                             w_glu=P["s5_w_glu"][li], w_gate=P["w_gate"][li], w_br=w_br, w_out=P["w_out"][li], identd=ident))
        rq3 = _run(build_Q3(lam_init), maps)
        maps = [dict(h1=rq3[c]["h1"], g_pre=P["norm_pre_ffn"][li], g_post=P["norm_post_ffn"][li], w_fi=P["w_ffn_in"][li], w_fo=P["w_ffn_out"][li], identd=ident)
                for c in range(NCORE)]
        rq4 = _run(build_Q4(), maps)
        h = [rq4[c]["h2"] for c in range(NCORE)]
    out = np.zeros_like(x)
    for c in range(NCORE):
        out[c // 2, toks[c % 2]] = h[c]
    return out


PAIRS = [[0, 1], [2, 3], [4, 5], [6, 7]]
TABW2 = 6016
TABOFF2 = 3968
VCOLS = 4 * 16 * 128 + 10 * 16 * 64
KCH = [(0, 5), (5, 10), (10, 14), (14, 18)]


def kchunk(k):
    for j, (a, b) in enumerate(KCH):
        if a <= k < b:
            return j, k - a


def alibi_table_core(half):
    x = np.arange(TABW2, dtype=np.float64)[None, :]
    ki = np.arange(128, dtype=np.float64)[:, None]
    return np.stack([np.exp(-m * np.abs(2048.0 * half + x - TABOFF2 - ki)) for m in SLOPES]).astype(np.float32)


class Fused:
    def __init__(self):
        self.C = Ctx()
        self.nc, self.S = self.C.nc, self.C.S
        self.A = None

    def begin(self):
        self.A = TileAlloc(self.nc)
        return self.A

    def end(self, include_cc=False):
        self.S.flush(include_cc=include_cc)
        self.A.close()
        self.A = None

    def scratch(self, name, shape, dt):
        return self.nc.dram_tensor(name, list(shape), dt).ap()


def emit_P(F, li, h_src, W, SC):
    S, A = F.S, F.begin()
    D = "dve"
    ident = A.sb("ident", [128, 128], BF16); b_ident = Buf()
    S.dma("sp", ident[:], W["identd"][:, :], writes=[b_ident])
    g_bc = A.sb("g_bc", [128, 1024], F32); b_g = Buf()
    S.dma("pool", g_bc[:], W["norm_pre_mix"][li].partition_broadcast(128), writes=[b_g])
    gB_bc = A.sb("gB_bc", [128, 640], F32); b_gB = Buf()
    S.dma("pool", gB_bc[:], W["gB"][li].partition_broadcast(128), writes=[b_gB])
    gC_bc = A.sb("gC_bc", [128, 384], F32); b_gC = Buf()
    S.dma("pool", gC_bc[:], W["gC"][li].partition_broadcast(128), writes=[b_gC])
    epsT = A.sb("epsT", [128, 1], F32); b_eps = Buf()
    S.op(D, lambda e: e.memset(epsT[:], EPS), writes=[b_eps])

    stg = [A.sb(f"stg{i}", [128, 1024], F32) for i in range(2)]; b_stg = [Buf() for _ in range(2)]
    sidx = [0]

    def load_w(dst, b_dst, src, ktiles, ncols, rows=None):
        for c0 in range(0, ncols, 1024):
            cw = min(1024, ncols - c0)
            bd = b_dst[c0 // 1024] if isinstance(b_dst, list) else b_dst
            for kt in range(ktiles):
                i = sidx[0] % 2
                sidx[0] += 1
                S.dma("sp", stg[i][:, 0:cw], src[kt * 128:(kt + 1) * 128, c0:c0 + cw], writes=[b_stg[i]])
                if i == 0:
                    S.op("pool", lambda e, i=i, kt=kt, c0=c0, cw=cw: e.tensor_copy(out=dst[:, kt, c0:c0 + cw], in_=stg[i][:, 0:cw]), reads=[b_stg[i]], writes=[bd])
                else:
                    S.op(D, lambda e, i=i, kt=kt, c0=c0, cw=cw: e.tensor_copy(out=dst[:, kt, c0:c0 + cw], in_=stg[i][:, 0:cw]), reads=[b_stg[i]], writes=[bd])

    wb = A.sb("wb", [128, 8, 3232], BF16); b_wb = [Buf() for _ in range(4)]
    load_w(wb, b_wb, W["w_in"][li], 8, 3232)
    wuq = A.sb("wuq", [128, 2, 768], BF16); b_wuq = Buf()
    load_w(wuq, b_wuq, W["mla_w_uq"][li], 2, 768)
    wukv = A.sb("wukv", [128, 1, 1024], BF16); b_wukv = Buf()
    load_w(wukv, b_wukv, W["mla_w_ukv"][li], 1, 1024)

    NB = 2
    ht = [A.sb(f"ht{i}", [128, 1024], F32) for i in range(NB)]; b_ht = [Buf() for _ in range(NB)]
    junk = A.sb("junk", [128, 1024], F32); b_junk = Buf()
    st1 = [A.sb(f"st1{i}", [128, 16], F32) for i in range(NB)]; b_st1 = [Buf() for _ in range(NB)]
    xn = [A.sb(f"xn{i}", [128, 1024], BF16) for i in range(NB)]; b_xn = [Buf() for _ in range(NB)]
    xnT = A.sb("xnT", [128, 8, 128], BF16); b_xnT = Buf()
    pj = A.sb("pj", [128, 3232], F32); b_pj = Buf()
    tA = [A.sb(f"tA{i}", [128, 1536], BF16) for i in range(NB)]; b_tA = [Buf() for _ in range(NB)]
    tB = [A.sb(f"tB{i}", [128, 768], BF16) for i in range(NB)]; b_tB = [Buf() for _ in range(NB)]
    tCq = [A.sb(f"tCq{i}", [128, 768], BF16) for i in range(NB)]; b_tCq = [Buf() for _ in range(NB)]
    tCkv = [A.sb(f"tCkv{i}", [128, 1024], BF16) for i in range(NB)]; b_tCkv = [Buf() for _ in range(NB)]
    tCkr = A.sb("tCkr", [128, 32], BF16); b_tCkr = Buf()
    tCk = A.sb("tCk", [128, 8, 96], BF16); b_tCk = Buf()
    ub = A.sb("ub", [128, 512], BF16); b_ub = Buf()
    wk1 = A.sb("wk1", [128, 768], F32); b_wk1 = Buf()
    wk2 = A.sb("wk2", [128, 768], F32); b_wk2 = Buf()
    wk3 = A.sb("wk3", [128, 768], F32); b_wk3 = Buf()
    lat = A.sb("lat", [128, 384], BF16); b_lat = Buf()
    latT = A.sb("latT", [128, 3, 128], BF16); b_latT = Buf()
    qc = A.sb("qc", [128, 768], F32); b_qc = Buf()
    rb = [A.sb(f"rb{i}", [128, 192], F32) for i in range(NB)]; b_rb = [Buf() for _ in range(NB)]
    qS = [A.sb(f"qS{i}", [96, 24, 256], BF16) for i in range(2)]; b_qS = [Buf() for _ in range(2)]
    kS = [A.sb(f"kS{i}", [96, 18, 256], BF16) for i in range(2)]; b_kS = [Buf() for _ in range(2)]
    uS = A.sb("uS", [128, 4, NT], BF16); b_uS = Buf()
    for i in range(2):
        S.op("pool", lambda e, i=i: e.memset(qS[i][:], 0.0), writes=[b_qS[i]])
        S.op("pool", lambda e, i=i: e.memset(kS[i][:], 0.0), writes=[b_kS[i]])

    pTs = [A.ps(f"pT{i}", [128, 1024], BF16) for i in range(3)]; b_pTs = [Buf() for _ in range(3)]
    pP = [A.ps(f"pP{i}", [128, 512], F32) for i in range(2)]; b_pP = [Buf() for _ in range(2)]
    pU = [A.ps(f"pU{i}", [128, 512], F32) for i in range(2)]; b_pU = [Buf() for _ in range(2)]
    pti = [0]

    def transposes(srcs, b_src, rows, dst_ap, b_dst, copy_eng):
        k = pti[0] % 3
        pti[0] += 1
        n = len(srcs)
        for j, sap in enumerate(srcs):
            S.op("pe", lambda e, k=k, j=j, sap=sap: e.transpose(out=pTs[k][0:rows, j * 128:(j + 1) * 128], in_=sap, identity=ident[:]),
                 reads=[b_src, b_ident], writes=[b_pTs[k]])
        src_v = pTs[k][0:rows, 0:n * 128].rearrange("p (n t) -> p n t", n=n)
        if copy_eng == "act":
            S.op("act", lambda e: e.copy(out=dst_ap, in_=src_v), reads=[b_pTs[k]], writes=[b_dst])
        else:
            S.op(D, lambda e: e.tensor_copy(out=dst_ap, in_=src_v), reads=[b_pTs[k]], writes=[b_dst])

    QTv = SC["QT"].rearrange("m d t -> d m t")
    KTv = [SC["kt_loc"][j].rearrange("(m d) t -> d m t", d=96) for j in range(4)]
    vA = SC["v_loc"][0].rearrange("p (h k d) -> p h k d", h=4, k=16)
    vB1 = SC["v_loc"][1].rearrange("p (h k d) -> p h k d", h=5, k=16)
    vB2 = SC["v_loc"][2].rearrange("p (h k d) -> p h k d", h=5, k=16)
    cosb, sinb, cosc, sinc = W["cosb"], W["sinb"], W["cosc"], W["sinc"]
    chunks = [(c0, min(512, 3232 - c0)) for c0 in range(0, 3232, 512)]
    for t in range(NTI):
        i = t % NB
        r0 = t * 128
        S.dma("sp", ht[i][:], h_src[r0:r0 + 128, :], writes=[b_ht[i]])
        S.dma("sp", rb[i][:, 0:64], cosb[r0:r0 + 128, :], writes=[b_rb[i]])
        S.dma("sp", rb[i][:, 64:128], sinb[r0:r0 + 128, :], writes=[b_rb[i]])
        S.dma("sp", rb[i][:, 128:160], cosc[r0:r0 + 128, :], writes=[b_rb[i]])
        S.dma("sp", rb[i][:, 160:192], sinc[r0:r0 + 128, :], writes=[b_rb[i]])
        s1 = st1[i]
        rms_rstd(S, ht[i][:], b_ht[i], junk[:], b_junk, s1, b_st1[i], epsT[:, 0:1], b_eps, 1024)
        S.op(D, lambda e, i=i, s1=s1: e.scalar_tensor_tensor(out=xn[i][:], in0=ht[i][:], scalar=s1[:, 2:3], in1=g_bc[:], op0=ALU.mult, op1=ALU.mult),
             reads=[b_ht[i], b_st1[i], b_g], writes=[b_xn[i]])
        transposes([xn[i][:, kt * 128:(kt + 1) * 128] for kt in range(8)], b_xn[i], 128, xnT[:], b_xnT, "act")
        for ci, (c0, cw) in enumerate(chunks):
            pb = ci % 2
            for kt in range(8):
                S.op("pe", lambda e, kt=kt, c0=c0, cw=cw, pb=pb: e.matmul(pP[pb][:, 0:cw], lhsT=xnT[:, kt, :], rhs=wb[:, kt, c0:c0 + cw], start=(kt == 0), stop=(kt == 7)),
                     reads=[b_xnT, b_wb[c0 // 1024]], writes=[b_pP[pb]])
            if ci % 2 == 0:
                S.op("act", lambda e, c0=c0, cw=cw, pb=pb: e.copy(out=pj[:, c0:c0 + cw], in_=pP[pb][:, 0:cw]), reads=[b_pP[pb]], writes=[b_pj])
            else:
                S.op(D, lambda e, c0=c0, cw=cw, pb=pb: e.tensor_copy(out=pj[:, c0:c0 + cw], in_=pP[pb][:, 0:cw]), reads=[b_pP[pb]], writes=[b_pj])
        p = pj
        S.op("pool", lambda e, i=i: e.tensor_copy(out=tA[i][:], in_=p[:, 0:1536]), reads=[b_pj], writes=[b_tA[i]])
        S.op("pool", lambda e: e.tensor_copy(out=ub[:], in_=p[:, 2720:3232]), reads=[b_pj], writes=[b_ub])
        xB = p[:, 1536:2176]
        S.op(D, lambda e: e.tensor_tensor(out=wk1[:, 0:640], in0=xB, in1=xB, op=ALU.mult), reads=[b_pj], writes=[b_wk1])
        S.op(D, lambda e, s1=s1: e.tensor_reduce(out=s1[:, 4:14], in_=wk1[:, 0:640].rearrange("p (h d) -> p h d", h=10), axis=AX.X, op=ALU.add),
             reads=[b_wk1], writes=[b_st1[i]])
        S.op("act", lambda e, s1=s1: e.activation(out=s1[:, 4:14], in_=s1[:, 4:14], func=AF.Sqrt, bias=epsT[:, 0:1], scale=1.0 / 64),
             reads=[b_st1[i], b_eps], writes=[b_st1[i]])
        S.op(D, lambda e, s1=s1: e.reciprocal(out=s1[:, 4:14], in_=s1[:, 4:14]), reads=[b_st1[i]], writes=[b_st1[i]])
        S.op(D, lambda e, s1=s1: e.tensor_tensor(out=wk1[:, 0:640].rearrange("p (h d) -> p h d", h=10), in0=xB.rearrange("p (h d) -> p h d", h=10),
                                                in1=s1[:, 4:14].unsqueeze(2).broadcast_to([128, 10, 64]), op=ALU.mult),
             reads=[b_pj, b_st1[i]], writes=[b_wk1])
        S.op("pool", lambda e: e.tensor_tensor(out=wk1[:, 0:640], in0=wk1[:, 0:640], in1=gB_bc[:], op=ALU.mult), reads=[b_wk1, b_gB], writes=[b_wk1])
        cosB = rb[i][:, 0:64].unsqueeze(1).broadcast_to([128, 10, 64])
        S.op("pool", lambda e, cosB=cosB: e.tensor_tensor(out=wk2[:, 0:640].rearrange("p (h d) -> p h d", h=10), in0=wk1[:, 0:640].rearrange("p (h d) -> p h d", h=10),
                                                         in1=cosB, op=ALU.mult), reads=[b_wk1, b_rb[i]], writes=[b_wk2])
        x5 = wk1[:, 0:640].rearrange("p (h a b d) -> p h a b d", h=10, a=2, b=2)
        o5 = wk3[:, 0:640].rearrange("p (h a b d) -> p h a b d", h=10, a=2, b=2)
        s4 = rb[i][:, 64:128].rearrange("p (a b d) -> p a b d", a=2, b=2)
        for bb in range(2):
            S.op(D, lambda e, bb=bb: e.tensor_tensor(out=o5[:, :, :, bb, :], in0=x5[:, :, :, 1 - bb, :],
                                                    in1=s4[:, :, bb, :].unsqueeze(1).broadcast_to([128, 10, 2, 16]), op=ALU.mult),
                 reads=[b_wk1, b_rb[i]], writes=[b_wk3])
        S.op(D, lambda e, i=i: e.tensor_tensor(out=tB[i][:, 0:640], in0=wk2[:, 0:640], in1=wk3[:, 0:640], op=ALU.add), reads=[b_wk2, b_wk3], writes=[b_tB[i]])
        S.op("pool", lambda e, i=i: e.tensor_copy(out=tB[i][:, 640:768], in_=p[:, 2176:2304]), reads=[b_pj], writes=[b_tB[i]])
        xC = p[:, 2304:2688]
        S.op(D, lambda e: e.tensor_tensor(out=wk2[:, 0:384], in0=xC, in1=xC, op=ALU.mult), reads=[b_pj], writes=[b_wk2])
        S.op(D, lambda e, s1=s1: e.tensor_reduce(out=s1[:, 14:15], in_=wk2[:, 0:256], axis=AX.X, op=ALU.add), reads=[b_wk2], writes=[b_st1[i]])
        S.op(D, lambda e, s1=s1: e.tensor_reduce(out=s1[:, 15:16], in_=wk2[:, 256:384], axis=AX.X, op=ALU.add), reads=[b_wk2], writes=[b_st1[i]])
        S.op("act", lambda e, s1=s1: e.activation(out=s1[:, 14:15], in_=s1[:, 14:15], func=AF.Sqrt, bias=epsT[:, 0:1], scale=1.0 / 256),
             reads=[b_st1[i], b_eps], writes=[b_st1[i]])
        S.op("act", lambda e, s1=s1: e.activation(out=s1[:, 15:16], in_=s1[:, 15:16], func=AF.Sqrt, bias=epsT[:, 0:1], scale=1.0 / 128),
             reads=[b_st1[i], b_eps], writes=[b_st1[i]])
        S.op(D, lambda e, s1=s1: e.reciprocal(out=s1[:, 14:16], in_=s1[:, 14:16]), reads=[b_st1[i]], writes=[b_st1[i]])
        S.op(D, lambda e, s1=s1: e.scalar_tensor_tensor(out=lat[:, 0:256], in0=p[:, 2304:2560], scalar=s1[:, 14:15], in1=gC_bc[:, 0:256], op0=ALU.mult, op1=ALU.mult),
             reads=[b_pj, b_st1[i], b_gC], writes=[b_lat])
        S.op(D, lambda e, s1=s1: e.scalar_tensor_tensor(out=lat[:, 256:384], in0=p[:, 2560:2688], scalar=s1[:, 15:16], in1=gC_bc[:, 256:384], op0=ALU.mult, op1=ALU.mult),
             reads=[b_pj, b_st1[i], b_gC], writes=[b_lat])
        transposes([lat[:, kt * 128:(kt + 1) * 128] for kt in range(3)], b_lat, 128, latT[:], b_latT, "act")
        for cj in range(2):
            for kt in range(2):
                S.op("pe", lambda e, cj=cj, kt=kt: e.matmul(pU[cj][:, 0:384], lhsT=latT[:, kt, :], rhs=wuq[:, kt, cj * 384:(cj + 1) * 384], start=(kt == 0), stop=(kt == 1)),
                     reads=[b_latT, b_wuq], writes=[b_pU[cj]])
            S.op("act", lambda e, cj=cj: e.copy(out=qc[:, cj * 384:(cj + 1) * 384], in_=pU[cj][:, 0:384]), reads=[b_pU[cj]], writes=[b_qc])
        q3 = qc[:].rearrange("p (h d) -> p h d", h=8)
        o3 = tCq[i][:].rearrange("p (h d) -> p h d", h=8)
        S.op("pool", lambda e: e.tensor_copy(out=o3[:, :, 0:64], in_=q3[:, :, 0:64]), reads=[b_qc], writes=[b_tCq[i]])
        cosC = rb[i][:, 128:160]
        sinC = rb[i][:, 160:192]
        w2 = wk2[:, 0:256].rearrange("p (h d) -> p h d", h=8)
        w3 = wk3[:, 0:256].rearrange("p (h d) -> p h d", h=8)
        S.op(D, lambda e: e.tensor_tensor(out=w2, in0=q3[:, :, 64:96], in1=cosC.unsqueeze(1).broadcast_to([128, 8, 32]), op=ALU.mult),
             reads=[b_qc, b_rb[i]], writes=[b_wk2])
        for bb in range(2):
            S.op(D, lambda e, bb=bb: e.tensor_tensor(out=w3[:, :, bb * 16:(bb + 1) * 16], in0=q3[:, :, 64 + (1 - bb) * 16:64 + (2 - bb) * 16],
                                                    in1=sinC[:, bb * 16:(bb + 1) * 16].unsqueeze(1).broadcast_to([128, 8, 16]), op=ALU.mult),
                 reads=[b_qc, b_rb[i]], writes=[b_wk3])
        S.op(D, lambda e: e.tensor_tensor(out=o3[:, :, 64:96], in0=w2, in1=w3, op=ALU.add), reads=[b_wk2, b_wk3], writes=[b_tCq[i]])
        for cj in range(2):
            S.op("pe", lambda e, cj=cj: e.matmul(pU[cj][:, 0:512], lhsT=latT[:, 2, :], rhs=wukv[:, 0, cj * 512:(cj + 1) * 512], start=True, stop=True),
                 reads=[b_latT, b_wukv], writes=[b_pU[cj]])
            S.op("act", lambda e, cj=cj, i=i: e.copy(out=tCkv[i][:, cj * 512:(cj + 1) * 512], in_=pU[cj][:, 0:512]), reads=[b_pU[cj]], writes=[b_tCkv[i]])
        kr = p[:, 2688:2720]
        S.op(D, lambda e: e.tensor_tensor(out=wk2[:, 256:288], in0=kr, in1=cosC, op=ALU.mult), reads=[b_pj, b_rb[i]], writes=[b_wk2])
        for bb in range(2):
            S.op(D, lambda e, bb=bb: e.tensor_tensor(out=wk3[:, 256 + bb * 16:256 + (bb + 1) * 16], in0=kr[:, (1 - bb) * 16:(2 - bb) * 16],
                                                    in1=sinC[:, bb * 16:(bb + 1) * 16], op=ALU.mult), reads=[b_pj, b_rb[i]], writes=[b_wk3])
        S.op(D, lambda e: e.tensor_tensor(out=tCkr[:], in0=wk2[:, 256:288], in1=wk3[:, 256:288], op=ALU.add), reads=[b_wk2, b_wk3], writes=[b_tCkr])
        kv3 = tCkv[i][:].rearrange("p (h d) -> p h d", h=8)
        S.op("pool", lambda e, kv3=kv3: e.tensor_copy(out=tCk[:, :, 0:64], in_=kv3[:, :, 0:64]), reads=[b_tCkv[i]], writes=[b_tCk])
        S.op("pool", lambda e: e.tensor_copy(out=tCk[:, :, 64:96], in_=tCkr[:].unsqueeze(1).broadcast_to([128, 8, 32])), reads=[b_tCkr], writes=[b_tCk])
        S.dma("sp", vA[:, :, t, :], tA[i][:, 1024:1536].rearrange("p (h d) -> p h d", h=4), reads=[b_tA[i]])
        S.dma("sp", vB1[:, 0:2, t, :], tB[i][:, 640:768].rearrange("p (h d) -> p h d", h=2), reads=[b_tB[i]])
        S.dma("sp", vB1[:, 2:5, t, :], kv3[:, 0:3, 64:128], reads=[b_tCkv[i]])
        S.dma("sp", vB2[:, :, t, :], kv3[:, 3:8, 64:128], reads=[b_tCkv[i]])
        sl = (t // 2) % 2
        tsl = slice((t % 2) * 128, (t % 2 + 1) * 128)
        transposes([tA[i][:, m * 64:(m + 1) * 64] for m in range(8)], b_tA[i], 64, qS[sl][0:64, 0:8, tsl], b_qS[sl], "act")
        transposes([tA[i][:, 512 + m * 64:512 + (m + 1) * 64] for m in range(8)], b_tA[i], 64, kS[sl][0:64, 0:8, tsl], b_kS[sl], "dve")
        transposes([tB[i][:, j * 64:(j + 1) * 64] for j in range(8)], b_tB[i], 64, qS[sl][0:64, 8:16, tsl], b_qS[sl], "act")
        transposes([tB[i][:, 512 + g * 64:512 + (g + 1) * 64] for g in range(2)], b_tB[i], 64, kS[sl][0:64, 8:10, tsl], b_kS[sl], "dve")
        transposes([tCq[i][:, j * 96:(j + 1) * 96] for j in range(8)], b_tCq[i], 96, qS[sl][0:96, 16:24, tsl], b_qS[sl], "act")
        transposes([tCk[:, j, :] for j in range(8)], b_tCk, 96, kS[sl][0:96, 10:18, tsl], b_kS[sl], "dve")
        transposes([ub[:, c * 128:(c + 1) * 128] for c in range(4)], b_ub, 128, uS[:, :, r0:r0 + 128], b_uS, "act")
        if t % 2 == 1:
            g0 = (t - 1) * 128
            S.dma("sp", QTv[:, :, g0:g0 + 256], qS[sl][:], reads=[b_qS[sl]])
            for j, (ka, kb_) in enumerate(KCH):
                S.dma("sp", KTv[j][:, :, g0:g0 + 256], kS[sl][:, ka:kb_, :], reads=[b_kS[sl]])
    S.dma("sp", SC["uT_loc"].rearrange("(c p) t -> p c t", p=128), uS[:], reads=[b_uS])
    F.end()


def emit_S5(F, li, W, SC, PB):
    S, A = F.S, F.begin()
    I32 = mybir.dt.int32
    D = "dve"
    NC_ = 2 * NGP
    SEG = 1024
    NSEG = 4096 // SEG
    ident = A.sb("ident", [128, 128], BF16); b_ident = Buf()
    S.dma("sp", ident[:], W["identd"][:, :], writes=[b_ident])
    iota = A.sb("iota", [128, SEG], F32); b_iota = Buf()
    S.dma("sp", iota[:], W["iota"][:, 0:SEG], writes=[b_iota])
    halfpi = A.sb("halfpi", [128, 2], F32); b_hp = Buf()
    S.op(D, lambda e: e.memset(halfpi[:, 0:1], math.pi / 2), writes=[b_hp])
    S.op(D, lambda e: e.memset(halfpi[:, 1:2], -1.0), writes=[b_hp])
    sel = A.sb("sel", [128, 2], F32); b_sel = Buf()
    S.dma("sp", sel[:], W["sel"][:, :], writes=[b_sel])

    def small(name, dt=F32):
        return A.sb(name, [128, NC_], dt), Buf()
    are, b_are = small("are"); aim, b_aim = small("aim"); ldt, b_ldt = small("ldt")
    S.dma("sp", are[:], W["areT"][li], writes=[b_are]); S.dma("sp", aim[:], W["aimT"][li], writes=[b_aim]); S.dma("sp", ldt[:], W["ldtT"][li], writes=[b_ldt])
    dt_, b_dt = small("dt"); mag, b_mag = small("mag"); thn, b_thn = small("thn")
    c1, b_c1 = small("c1"); s1, b_s1 = small("s1"); ski, b_ski = small("ski", I32); sa, b_sa = small("sa"); sb_, b_sb = small("sb")
    lbre, b_lbre = small("lbre"); lbim, b_lbim = small("lbim"); rden, b_rden = small("rden"); nre, b_nre = small("nre")
    fre, b_fre = small("fre"); fim, b_fim = small("fim"); nfim, b_nfim = small("nfim"); w1, b_w1 = small("w1"); w2, b_w2 = small("w2")
    S.op(D, lambda e: e.tensor_scalar_min(out=are[:], in0=are[:], scalar1=-1e-4), reads=[b_are], writes=[b_are])
    S.op("act", lambda e: e.activation(out=dt_[:], in_=ldt[:], func=AF.Exp), reads=[b_ldt], writes=[b_dt])
    S.op(D, lambda e: e.tensor_tensor(out=w1[:], in0=are[:], in1=dt_[:], op=ALU.mult), reads=[b_are, b_dt], writes=[b_w1])
    S.op("act", lambda e: e.activation(out=mag[:], in_=w1[:], func=AF.Exp), reads=[b_w1], writes=[b_mag])
    S.op(D, lambda e: e.tensor_tensor(out=w2[:], in0=aim[:], in1=dt_[:], op=ALU.mult), reads=[b_aim, b_dt], writes=[b_w2])
    S.op(D, lambda e: e.tensor_scalar(out=thn[:], in0=w2[:], scalar1=1.0 / (2 * math.pi), scalar2=None, op0=ALU.mult), reads=[b_w2], writes=[b_thn])
    emit_sincos(S, thn[:], b_thn, ski[:], b_ski, sa[:], b_sa, sb_[:], b_sb, c1[:], b_c1, s1[:], b_s1, halfpi[:, 0:1], b_hp, neg1=halfpi[:, 1:2])
    S.op(D, lambda e: e.tensor_tensor(out=lbre[:], in0=mag[:], in1=c1[:], op=ALU.mult), reads=[b_mag, b_c1], writes=[b_lbre])
    S.op(D, lambda e: e.tensor_tensor(out=lbim[:], in0=mag[:], in1=s1[:], op=ALU.mult), reads=[b_mag, b_s1], writes=[b_lbim])
    S.op(D, lambda e: e.tensor_tensor(out=w1[:], in0=are[:], in1=are[:], op=ALU.mult), reads=[b_are], writes=[b_w1])
    S.op(D, lambda e: e.tensor_tensor(out=w2[:], in0=aim[:], in1=aim[:], op=ALU.mult), reads=[b_aim], writes=[b_w2])
    S.op(D, lambda e: e.tensor_tensor(out=w1[:], in0=w1[:], in1=w2[:], op=ALU.add), reads=[b_w1, b_w2], writes=[b_w1])
    S.op(D, lambda e: e.reciprocal(out=rden[:], in_=w1[:]), reads=[b_w1], writes=[b_rden])
    S.op(D, lambda e: e.tensor_scalar_add(out=nre[:], in0=lbre[:], scalar1=-1.0), reads=[b_lbre], writes=[b_nre])
    S.op(D, lambda e: e.tensor_tensor(out=w1[:], in0=nre[:], in1=are[:], op=ALU.mult), reads=[b_nre, b_are], writes=[b_w1])
    S.op(D, lambda e: e.tensor_tensor(out=w2[:], in0=lbim[:], in1=aim[:], op=ALU.mult), reads=[b_lbim, b_aim], writes=[b_w2])
    S.op(D, lambda e: e.tensor_tensor(out=w1[:], in0=w1[:], in1=w2[:], op=ALU.add), reads=[b_w1, b_w2], writes=[b_w1])
    S.op(D, lambda e: e.tensor_tensor(out=fre[:], in0=w1[:], in1=rden[:], op=ALU.mult), reads=[b_w1, b_rden], writes=[b_fre])
    S.op(D, lambda e: e.tensor_tensor(out=w1[:], in0=lbim[:], in1=are[:], op=ALU.mult), reads=[b_lbim, b_are], writes=[b_w1])
    S.op(D, lambda e: e.tensor_tensor(out=w2[:], in0=nre[:], in1=aim[:], op=ALU.mult), reads=[b_nre, b_aim], writes=[b_w2])
    S.op(D, lambda e: e.tensor_tensor(out=w1[:], in0=w1[:], in1=w2[:], op=ALU.subtract), reads=[b_w1, b_w2], writes=[b_w1])
    S.op(D, lambda e: e.tensor_tensor(out=fim[:], in0=w1[:], in1=rden[:], op=ALU.mult), reads=[b_w1, b_rden], writes=[b_fim])
    S.op(D, lambda e: e.tensor_scalar(out=nfim[:], in0=fim[:], scalar1=-1.0, scalar2=None, op0=ALU.mult), reads=[b_fim], writes=[b_nfim])

    uTb = A.sb("uTb", [128, 2, 4096], BF16); b_uTb = Buf()
    yacc = A.sb("yacc", [128, 2, 4096], F32); b_yacc = Buf()
    def wset(k):
        d = {}
        for nm, dt in (("XR", F32), ("XI", F32), ("T1", F32), ("T2", F32), ("T3", F32), ("Ct", F32), ("St", F32), ("KI", I32), ("sre", BF16), ("sim", BF16)):
            d[nm] = A.sb(f"{nm}{k}", [128, SEG], dt)
            d["b_" + nm] = Buf()
        return d
    WS = [wset(0), wset(1)]
    carry = A.sb("carry", [128, 2], F32); b_carry = Buf()
    wtmp = A.sb("wtmp", [128, 256], F32); b_wtmp = Buf()
    bst = [A.sb(f"bst{i}", [128, 128], F32) for i in range(4)]; b_bst = [Buf() for _ in range(4)]
    wpre = [A.sb(f"wpre{i}", [128, 128], BF16) for i in range(2)]; b_wpre = [Buf() for _ in range(2)]
    lx = A.sb("lx", [128, 2, 128], BF16); b_lx = Buf()
    ly = A.sb("ly", [128, 2, 128], BF16); b_ly = Buf()
    pT = A.ps("pT", [128, 256], BF16); b_pT = Buf()
    pX = [A.ps(f"pX{i}", [128, 512], F32) for i in range(4)]; b_pX = [Buf() for _ in range(4)]
    pY = [A.ps(f"pY{i}", [128, 512], F32) for i in range(2)]; b_pY = [Buf() for _ in range(2)]

    ca = WS[0]['sre']; cb = WS[1]['sre']; b_sre = WS[0]['b_sre']; b_sim = WS[1]['b_sre']; T1 = WS[0]['T1']; b_T1 = WS[0]['b_T1']
    uall = SC["uT_all"]
    for r in range(2):
        for ctl in range(2):
            r0a = r * 512 + ctl * 128
            r0b = r * 512 + (2 + ctl) * 128
            for hh in range(2048 // SEG):
                c0 = hh * SEG
                S.dma("sp", ca[:], uall[r0a:r0a + 128, c0:c0 + SEG], reads=[PB["uT_all"]], writes=[b_sre])
                S.dma("sp", cb[:], uall[r0b:r0b + 128, c0:c0 + SEG], reads=[PB["uT_all"]], writes=[b_sim])
                S.op("pool", lambda e: e.tensor_scalar(out=T1[:], in0=ca[:], scalar1=sel[:, 0:1], scalar2=None, op0=ALU.mult), reads=[b_sre, b_sel], writes=[b_T1])
                S.op(D, lambda e, r=r, ctl=ctl, c0=c0: e.scalar_tensor_tensor(out=uTb[:, ctl, r * 2048 + c0:r * 2048 + c0 + SEG], in0=cb[:], scalar=sel[:, 1:2], in1=T1[:],
                                                                            op0=ALU.mult, op1=ALU.add),
                     reads=[b_sim, b_sel, b_T1], writes=[b_uTb])

    Bre, Bim, Cre, Cim = W["Bre"], W["Bim"], W["Cre"], W["Cim"]
    ycnt = [0]
    ly2 = [A.sb(f"ly2_{i}", [128, 2, 128], BF16) for i in range(2)]; b_ly2 = [Buf() for _ in range(2)]
    tasks = []
    for gp in range(NGP):
        for dr in range(2):
            segs = list(range(NSEG)) if dr == 0 else list(range(NSEG - 1, -1, -1))
            for si, sg in enumerate(segs):
                tasks.append((gp, dr, si, sg))

    def bufs(k):
        ws = WS[k % 2]
        return ([ws[n] for n in ("XR", "XI", "T1", "T2", "T3", "Ct", "St", "KI", "sre", "sim")],
                [ws["b_" + n] for n in ("XR", "XI", "T1", "T2", "T3", "Ct", "St", "KI", "sre", "sim")])

    def f1(k):
        gp, dr, si, sg = tasks[k]
        ct = gp // 4
        col = dr * NGP + gp
        cs = slice(col, col + 1)
        par = (gp * 2 + dr) % 2
        (XR, XI, T1, T2, T3, Ct, St, KI, sre, sim_), (b_XR, b_XI, b_T1, b_T2, b_T3, b_Ct, b_St, b_KI, b_sre, b_sim) = bufs(k)
        if si == 0:
            S.dma("sp", bst[0][:], Bre[li, dr, gp], writes=[b_bst[0]])
            S.dma("sp", bst[1][:], Bim[li, dr, gp], writes=[b_bst[1]])
            S.dma("sp", bst[2][:], Cre[li, dr, gp], writes=[b_bst[2]])
            S.dma("sp", bst[3][:], Cim[li, dr, gp], writes=[b_bst[3]])
            S.op("pool", lambda e: e.tensor_scalar(out=wtmp[:, 0:128], in0=bst[0][:], scalar1=fre[:, cs], scalar2=None, op0=ALU.mult),
                 reads=[b_bst[0], b_fre], writes=[b_wtmp])
            S.op(D, lambda e: e.scalar_tensor_tensor(out=wpre[0][:], in0=bst[1][:], scalar=nfim[:, cs], in1=wtmp[:, 0:128], op0=ALU.mult, op1=ALU.add),
                 reads=[b_bst[1], b_nfim, b_wtmp], writes=[b_wpre[0]])
            S.op("pool", lambda e: e.tensor_scalar(out=wtmp[:, 128:256], in0=bst[0][:], scalar1=fim[:, cs], scalar2=None, op0=ALU.mult),
                 reads=[b_bst[0], b_fim], writes=[b_wtmp])
            S.op(D, lambda e: e.scalar_tensor_tensor(out=wpre[1][:], in0=bst[1][:], scalar=fre[:, cs], in1=wtmp[:, 128:256], op0=ALU.mult, op1=ALU.add),
                 reads=[b_bst[1], b_fre, b_wtmp], writes=[b_wpre[1]])
            for ri in range(2):
                S.op("pe", lambda e, ri=ri: e.transpose(out=pT[:, ri * 128:(ri + 1) * 128], in_=wpre[ri][:], identity=ident[:]),
                     reads=[b_wpre[ri], b_ident], writes=[b_pT])
            S.op("act", lambda e: e.copy(out=lx[:].rearrange("p a b -> p (a b)"), in_=pT[:]), reads=[b_pT], writes=[b_lx])
            S.op("act", lambda e: e.copy(out=ly2[par][:, 0, :], in_=bst[2][:]), reads=[b_bst[2]], writes=[b_ly2[par]])
            S.op("act", lambda e: e.mul(out=ly2[par][:, 1, :], in_=bst[3][:], mul=-1.0), reads=[b_bst[3]], writes=[b_ly2[par]])
        t0 = sg * SEG
        if dr == 0:
            io, off = iota[:, :], float(t0)
        else:
            io, off = iota[:, ::-1], float(4095 - t0 - (SEG - 1))
        S.op(D, lambda e: e.tensor_scalar(out=T1[:], in0=io, scalar1=off, scalar2=thn[:, cs], op0=ALU.add, op1=ALU.mult),
             reads=[b_iota, b_thn], writes=[b_T1])
        S.op("act", lambda e: e.copy(out=KI[:], in_=T1[:]), reads=[b_T1], writes=[b_KI])
        S.op("act", lambda e: e.copy(out=T2[:], in_=KI[:]), reads=[b_KI], writes=[b_T2])

    def f2(k):
        gp, dr, si, sg = tasks[k]
        ct = gp // 4
        col = dr * NGP + gp
        cs = slice(col, col + 1)
        par = (gp * 2 + dr) % 2
        t0 = sg * SEG
        (XR, XI, T1, T2, T3, Ct, St, KI, sre, sim_), (b_XR, b_XI, b_T1, b_T2, b_T3, b_Ct, b_St, b_KI, b_sre, b_sim) = bufs(k)
        hp_, n1_ = halfpi[:, 0:1], halfpi[:, 1:2]
        S.op(D, lambda e: e.tensor_tensor(out=T2[:], in0=T1[:], in1=T2[:], op=ALU.subtract), reads=[b_T1, b_T2], writes=[b_T2])
        S.op("act", lambda e: e.activation(out=T3[:], in_=T2[:], func=AF.Sin, scale=math.pi), reads=[b_T2], writes=[b_T3])
        S.op("act", lambda e: e.activation(out=T2[:], in_=T2[:], func=AF.Abs), reads=[b_T2], writes=[b_T2])
        S.op("act", lambda e: e.activation(out=T2[:], in_=T2[:], func=AF.Sin, scale=-math.pi, bias=hp_), reads=[b_T2, b_hp], writes=[b_T2])
        S.op(D, lambda e: e.scalar_tensor_tensor(out=St[:], in0=T3[:], scalar=2.0, in1=T2[:], op0=ALU.mult, op1=ALU.mult), reads=[b_T2, b_T3], writes=[b_St])
        S.op("act", lambda e: e.activation(out=T3[:], in_=T2[:], func=AF.Square), reads=[b_T2], writes=[b_T3])
        S.op("act", lambda e: e.activation(out=Ct[:], in_=T3[:], func=AF.Identity, scale=2.0, bias=n1_), reads=[b_T3, b_hp], writes=[b_Ct])

    def f3(k):
        gp, dr, si, sg = tasks[k]
        ct = gp // 4
        col = dr * NGP + gp
        cs = slice(col, col + 1)
        par = (gp * 2 + dr) % 2
        t0 = sg * SEG
        (XR, XI, T1, T2, T3, Ct, St, KI, sre, sim_), (b_XR, b_XI, b_T1, b_T2, b_T3, b_Ct, b_St, b_KI, b_sre, b_sim) = bufs(k)
        for blk in range(SEG // 512):
            ts = slice(t0 + blk * 512, t0 + (blk + 1) * 512)
            for ri in range(2):
                pi = (blk % 2) * 2 + ri
                S.op("pe", lambda e, ri=ri, pi=pi, ts=ts: e.matmul(pX[pi][:], lhsT=lx[:, ri, :], rhs=uTb[:, ct, ts], start=True, stop=True),
                     reads=[b_lx, b_uTb], writes=[b_pX[pi]])
                dst, bd = (XR, b_XR) if ri == 0 else (XI, b_XI)
                S.op("act", lambda e, dst=dst, pi=pi, blk=blk: e.copy(out=dst[:, blk * 512:(blk + 1) * 512], in_=pX[pi][:]), reads=[b_pX[pi]], writes=[bd])

    def f4(k):
        gp, dr, si, sg = tasks[k]
        ct = gp // 4
        col = dr * NGP + gp
        cs = slice(col, col + 1)
        par = (gp * 2 + dr) % 2
        t0 = sg * SEG
        (XR, XI, T1, T2, T3, Ct, St, KI, sre, sim_), (b_XR, b_XI, b_T1, b_T2, b_T3, b_Ct, b_St, b_KI, b_sre, b_sim) = bufs(k)
        S.op(D, lambda e: e.tensor_tensor(out=T1[:], in0=Ct[:], in1=XR[:], op=ALU.mult), reads=[b_Ct, b_XR], writes=[b_T1])
        S.op("pool", lambda e: e.tensor_tensor(out=T2[:], in0=St[:], in1=XI[:], op=ALU.mult), reads=[b_St, b_XI], writes=[b_T2])
        S.op(D, lambda e: e.tensor_tensor(out=T1[:], in0=T1[:], in1=T2[:], op=ALU.add), reads=[b_T1, b_T2], writes=[b_T1])
        S.op("pool", lambda e: e.tensor_tensor(out=T3[:], in0=St[:], in1=XR[:], op=ALU.mult), reads=[b_St, b_XR], writes=[b_T3])
        S.op(D, lambda e: e.tensor_tensor(out=T2[:], in0=Ct[:], in1=XI[:], op=ALU.mult), reads=[b_Ct, b_XI], writes=[b_T2])
        S.op(D, lambda e: e.tensor_tensor(out=T2[:], in0=T2[:], in1=T3[:], op=ALU.subtract), reads=[b_T2, b_T3], writes=[b_T2])

    def b1(k):
        gp, dr, si, sg = tasks[k]
        ct = gp // 4
        col = dr * NGP + gp
        cs = slice(col, col + 1)
        par = (gp * 2 + dr) % 2
        ly = ly2[par]; b_ly = b_ly2[par]
        t0 = sg * SEG
        (XR, XI, T1, T2, T3, Ct, St, KI, sre, sim_), (b_XR, b_XI, b_T1, b_T2, b_T3, b_Ct, b_St, b_KI, b_sre, b_sim) = bufs(k)
        rmul = mag[:, cs].broadcast_to([128, SEG])
        for zi, (src, bsrc, dst, bdst) in enumerate(((T1, b_T1, XR, b_XR), (T2, b_T2, XI, b_XI))):
            init = 0.0 if si == 0 else carry[:, zi:zi + 1]
            if dr == 0:
                o_ap, d_ap = dst[:, :], src[:, :]
            else:
                o_ap, d_ap = dst[:, ::-1], src[:, ::-1]
            S.op(D, lambda e, o_ap=o_ap, d_ap=d_ap, init=init: e.tensor_tensor_scan(out=o_ap, data0=rmul, data1=d_ap, initial=init, op0=ALU.mult, op1=ALU.add),
                 reads=[bsrc, b_mag, b_carry], writes=[bdst])
        if si < NSEG - 1:
            ccol = SEG - 1 if dr == 0 else 0
            S.op("pool", lambda e: e.tensor_copy(out=carry[:, 0:1], in_=XR[:, ccol:ccol + 1]), reads=[b_XR], writes=[b_carry])
            S.op("pool", lambda e: e.tensor_copy(out=carry[:, 1:2], in_=XI[:, ccol:ccol + 1]), reads=[b_XI], writes=[b_carry])

    def b2(k):
        gp, dr, si, sg = tasks[k]
        ct = gp // 4
        col = dr * NGP + gp
        cs = slice(col, col + 1)
        par = (gp * 2 + dr) % 2
        ly = ly2[par]; b_ly = b_ly2[par]
        t0 = sg * SEG
        (XR, XI, T1, T2, T3, Ct, St, KI, sre, sim_), (b_XR, b_XI, b_T1, b_T2, b_T3, b_Ct, b_St, b_KI, b_sre, b_sim) = bufs(k)
        S.op(D, lambda e: e.tensor_tensor(out=T1[:], in0=Ct[:], in1=XR[:], op=ALU.mult), reads=[b_Ct, b_XR], writes=[b_T1])
        S.op("pool", lambda e: e.tensor_tensor(out=T3[:], in0=St[:], in1=XI[:], op=ALU.mult), reads=[b_St, b_XI], writes=[b_T3])
        S.op(D, lambda e: e.tensor_tensor(out=sre[:], in0=T1[:], in1=T3[:], op=ALU.subtract), reads=[b_T1, b_T3], writes=[b_sre])

    def b3(k):
        gp, dr, si, sg = tasks[k]
        ct = gp // 4
        col = dr * NGP + gp
        cs = slice(col, col + 1)
        par = (gp * 2 + dr) % 2
        ly = ly2[par]; b_ly = b_ly2[par]
        t0 = sg * SEG
        (XR, XI, T1, T2, T3, Ct, St, KI, sre, sim_), (b_XR, b_XI, b_T1, b_T2, b_T3, b_Ct, b_St, b_KI, b_sre, b_sim) = bufs(k)
        S.op(D, lambda e: e.tensor_tensor(out=T2[:], in0=St[:], in1=XR[:], op=ALU.mult), reads=[b_St, b_XR], writes=[b_T2])
        S.op("pool", lambda e: e.tensor_tensor(out=KI[:].bitcast(F32), in0=Ct[:], in1=XI[:], op=ALU.mult), reads=[b_Ct, b_XI], writes=[b_KI])
        S.op(D, lambda e: e.tensor_tensor(out=sim_[:], in0=T2[:], in1=KI[:].bitcast(F32), op=ALU.add), reads=[b_T2, b_KI], writes=[b_sim])
        first = (gp % 4 == 0 and dr == 0)
        for blk in range(SEG // 512):
            pi = ycnt[0] % 2
            ycnt[0] += 1
            bs_ = slice(blk * 512, (blk + 1) * 512)
            ts = slice(t0 + blk * 512, t0 + (blk + 1) * 512)
            S.op("pe", lambda e, pi=pi, bs_=bs_: e.matmul(pY[pi][:], lhsT=ly[:, 0, :], rhs=sre[:, bs_], start=True, stop=False),
                 reads=[b_ly, b_sre], writes=[b_pY[pi]])
            S.op("pe", lambda e, pi=pi, bs_=bs_: e.matmul(pY[pi][:], lhsT=ly[:, 1, :], rhs=sim_[:, bs_], start=False, stop=True),
                 reads=[b_ly, b_sim], writes=[b_pY[pi]])
            if first:
                S.op("act", lambda e, pi=pi, ts=ts: e.copy(out=yacc[:, ct, ts], in_=pY[pi][:]), reads=[b_pY[pi]], writes=[b_yacc])
            else:
                S.op(D, lambda e, pi=pi, ts=ts: e.tensor_tensor(out=yacc[:, ct, ts], in0=pY[pi][:], in1=yacc[:, ct, ts], op=ALU.add),
                     reads=[b_pY[pi], b_yacc], writes=[b_yacc])

    n_t = len(tasks)
    f1(0); f2(0); f3(0); f4(0)
    for k in range(n_t):
        nx = k + 1 < n_t
        if nx:
            f1(k + 1)
        b1(k)
        if nx:
            f2(k + 1)
        b2(k)
        if nx:
            f3(k + 1)
        b3(k)
        if nx:
            f4(k + 1)
    for ctl in range(2):
        S.dma("sp", SC["y_loc"][ctl], yacc[:, ctl, :], reads=[b_yacc], writes=[PB["y_loc"]])
    F.end()


def emit_AT(F, li, W, SC, PB):
    S, A = F.S, F.begin()
    D = "dve"
    lam_init = 0.8 - 0.6 * math.exp(-0.3 * li)
    ones = A.sb("ones", [128, 128], BF16); b_ones = Buf()
    S.op(D, lambda e: e.memset(ones[:], 1.0), writes=[b_ones])
    epsT = A.sb("epsT", [128, 1], F32); b_eps = Buf()
    S.op(D, lambda e: e.memset(epsT[:], EPS), writes=[b_eps])
    lam_bc = A.sb("lam_bc", [128, 256], F32); b_lam = Buf()
    S.dma("pool", lam_bc[:], W["lamv"][li].partition_broadcast(128), writes=[b_lam])
    lt = A.sb("lt", [128, 128], F32); b_lt = Buf()
    ls = A.sb("ls", [128, 8], F32); b_ls = Buf()
    S.op(D, lambda e: e.tensor_tensor(out=lt[:, 0:64], in0=lam_bc[:, 0:64], in1=lam_bc[:, 64:128], op=ALU.mult), reads=[b_lam], writes=[b_lt])
    S.op(D, lambda e: e.tensor_tensor(out=lt[:, 64:128], in0=lam_bc[:, 128:192], in1=lam_bc[:, 192:256], op=ALU.mult), reads=[b_lam], writes=[b_lt])
    S.op(D, lambda e: e.tensor_reduce(out=ls[:, 0:2], in_=lt[:].rearrange("p (a d) -> p a d", a=2), axis=AX.X, op=ALU.add), reads=[b_lt], writes=[b_ls])
    S.op("act", lambda e: e.activation(out=ls[:, 2:4], in_=ls[:, 0:2], func=AF.Exp), reads=[b_ls], writes=[b_ls])
    S.op(D, lambda e: e.tensor_tensor(out=ls[:, 4:5], in0=ls[:, 3:4], in1=ls[:, 2:3], op=ALU.subtract), reads=[b_ls], writes=[b_ls])
    S.op(D, lambda e: e.tensor_scalar_add(out=ls[:, 5:6], in0=ls[:, 4:5], scalar1=-lam_init), reads=[b_ls], writes=[b_ls])
    nlam = ls[:, 5:6]
    subcol = A.sb("subcol", [128, 1], F32); b_sub = Buf()
    S.dma("sp", subcol[:], W["subln"][li].rearrange("(p o) -> p o", o=1), writes=[b_sub])
    S.op("act", lambda e: e.mul(out=subcol[:], in_=subcol[:], mul=1.0 - lam_init), reads=[b_sub], writes=[b_sub])

    qt = [A.sb(f"qt{i}", [128, NT], BF16) for i in range(2)]; b_qt = [Buf() for _ in range(2)]
    kt_ = [A.sb(f"kt{i}", [128, 4096], BF16) for i in range(2)]; b_kt = [Buf() for _ in range(2)]
    for i in range(2):
        S.op("pool", lambda e, i=i: e.memset(qt[i][:], 0.0), writes=[b_qt[i]])
        S.op("pool", lambda e, i=i: e.memset(kt_[i][:], 0.0), writes=[b_kt[i]])
    vt = [A.sb(f"vt{i}", [128, 32, 128], BF16) for i in range(2)]; b_vt = [Buf() for _ in range(2)]
    tab = A.sb("tab", [128, TABW2], F32); b_tab = Buf()
    NP = 5
    e32 = [A.sb(f"e32{i}", [128, 512], F32) for i in range(NP)]; b_e32 = [Buf() for _ in range(NP)]
    pb = [A.sb(f"pb{i}", [128, 512], BF16) for i in range(NP)]; b_pb = [Buf() for _ in range(NP)]
    rz = [A.sb(f"rz{i}", [128, 512], F32) for i in range(2)]; b_rz = [Buf() for _ in range(2)]
    oc32 = [A.sb(f"oc32{i}", [128, 512], F32) for i in range(2)]; b_oc32 = [Buf() for _ in range(2)]
    ot16 = [A.sb(f"ot16{i}", [128, 512], BF16) for i in range(2)]; b_ot16 = [Buf() for _ in range(2)]
    cmb = A.sb("cmb", [128, 512], F32); b_cmb = Buf()
    sqb = A.sb("sqb", [128, 512], BF16); b_sqb = Buf()
    rr = A.sb("rr", [128, 512], F32); b_rr = Buf()
    psS = [A.ps(f"psS{i}", [128, 512], F32) for i in range(NP)]; b_psS = [Buf() for _ in range(NP)]
    psO = [A.ps(f"psO{i}", [128, 512], F32) for i in range(2)]; b_psO = [Buf() for _ in range(2)]
    psZ = [A.ps(f"psZ{i}", [128, 512], F32) for i in range(1)] * 2; b_psZ = [Buf()] * 2

    QT, kall, vall = SC["QT"], SC["kt_all"], SC["v_all"]
    cnt = [0]
    oq = [0]

    def load_q(slot, m, d):
        S.dma("sp", qt[slot][0:d, :], QT[m, 0:d, :], writes=[b_qt[slot]])

    def load_k(slot, k, d):
        j, ko = kchunk(k)
        nrows = (KCH[j][1] - KCH[j][0]) * 96
        for r in range(2):
            row0 = r * nrows + ko * 96
            S.dma("sp", kt_[slot][0:d, r * 2048:(r + 1) * 2048], kall[j][row0:row0 + d, :], reads=[PB["kt_all"]], writes=[b_kt[slot]])

    def load_v(slot, kind, vi):
        for r in range(2):
            if kind == "A":
                src = vall[0][r * 128:(r + 1) * 128, vi * 2048:(vi + 1) * 2048].rearrange("p (k d) -> p k d", k=16)
                S.dma("pool", vt[slot][:, r * 16:(r + 1) * 16, 0:128], src, reads=[PB["v_all"]], writes=[b_vt[slot]])
            else:
                vj, vo = (1, vi) if vi < 5 else (2, vi - 5)
                src = vall[vj][r * 128:(r + 1) * 128, vo * 1024:(vo + 1) * 1024].rearrange("p (k d) -> p k d", k=16)
                S.dma("pool", vt[slot][:, r * 16:(r + 1) * 16, 0:64], src, reads=[PB["v_all"]], writes=[b_vt[slot]])

    def attn_block(sq, sk, sv, d, dv, scale, masked, qb, ob_, zsep=True):
        qs = slice(qb * 512, (qb + 1) * 512)

        def emit_S(kt):
            j = (cnt[0] + kt) % NP
            S.op("pe", lambda e, j=j, kt=kt: e.matmul(psS[j][:], lhsT=kt_[sk][:, kt * 128:(kt + 1) * 128], rhs=qt[sq][:, qs], start=True, stop=True),
                 reads=[b_kt[sk], b_qt[sq]], writes=[b_psS[j]])

        def emit_rest(kt):
            j = (cnt[0] + kt) % NP
            if masked:
                S.op("act", lambda e, j=j: e.activation(out=e32[j][:], in_=psS[j][:], func=AF.Exp, scale=scale), reads=[b_psS[j]], writes=[b_e32[j]])
                w = 512 * qb - 128 * kt + TABOFF2
                meng = D
                S.op(meng, lambda e, j=j, w=w: e.tensor_tensor(out=pb[j][:], in0=e32[j][:], in1=tab[:, w:w + 512], op=ALU.mult),
                     reads=[b_e32[j], b_tab], writes=[b_pb[j]])
            else:
                S.op("act", lambda e, j=j: e.activation(out=pb[j][:], in_=psS[j][:], func=AF.Exp, scale=scale), reads=[b_psS[j]], writes=[b_pb[j]])
            S.op("pe", lambda e, j=j, kt=kt: e.matmul(psO[ob_][0:dv, :], lhsT=vt[sv][:, kt, 0:dv], rhs=pb[j][:], start=(kt == 0), stop=(kt == 31)),
                 reads=[b_vt[sv], b_pb[j]], writes=[b_psO[ob_]])
            if zsep:
                S.op("pe", lambda e, j=j, kt=kt: e.matmul(psZ[ob_][0:dv, :], lhsT=ones[:, 0:dv], rhs=pb[j][:], start=(kt == 0), stop=(kt == 31)),
                     reads=[b_ones, b_pb[j]], writes=[b_psZ[ob_]])

        LOOK = NP - 1
        for kt in range(LOOK):
            emit_S(kt)
        for kt in range(32):
            if kt + LOOK < 32:
                emit_S(kt + LOOK)
            emit_rest(kt)
        cnt[0] += 32
        if zsep:
            S.op(D, lambda e: e.reciprocal(out=rz[ob_][0:dv, :], in_=psZ[ob_][0:dv, :]), reads=[b_psZ[ob_]], writes=[b_rz[ob_]])
        else:
            S.op(D, lambda e: e.reciprocal(out=rz[ob_][0:64, :], in_=psO[ob_][64:128, :]), reads=[b_psO[ob_]], writes=[b_rz[ob_]])

    yaT = SC["yaT"]
    vs = 0
    for hh in range(4):
        for c in range(2):
            load_q(c, 2 * hh + c, 64)
            load_k(c, 2 * hh + c, 64)
        vs = (vs + 1) % 2
        load_v(vs, "A", hh)
        S.dma("pool", tab[:], W["tab"][hh], writes=[b_tab])
        for qb in range(4):
            qs = slice(qb * 512, (qb + 1) * 512)
            for c in range(2):
                ob_ = oq[0] % 2
                oq[0] += 1
                attn_block(c, c, vs, 64, 128, 64 ** -0.5, True, qb, ob_)
                S.op(D, lambda e, ob_=ob_, c=c: e.tensor_tensor(out=oc32[c][:], in0=psO[ob_][:], in1=rz[ob_][:], op=ALU.mult),
                     reads=[b_psO[ob_], b_rz[ob_]], writes=[b_oc32[c]])
            S.op(D, lambda e: e.scalar_tensor_tensor(out=cmb[:], in0=oc32[1][:], scalar=nlam, in1=oc32[0][:], op0=ALU.mult, op1=ALU.add),
                 reads=[b_oc32[0], b_oc32[1], b_ls], writes=[b_cmb])
            S.op("act", lambda e: e.activation(out=sqb[:], in_=cmb[:], func=AF.Square), reads=[b_cmb], writes=[b_sqb])
            jr = cnt[0] % NP
            cnt[0] += 1
            S.op("pe", lambda e, jr=jr: e.matmul(psS[jr][:], lhsT=ones[:], rhs=sqb[:], start=True, stop=True), reads=[b_ones, b_sqb], writes=[b_psS[jr]])
            S.op("act", lambda e, jr=jr: e.activation(out=rr[:], in_=psS[jr][:], func=AF.Sqrt, bias=epsT[:, 0:1], scale=1.0 / 128), reads=[b_psS[jr], b_eps], writes=[b_rr])
            S.op(D, lambda e: e.reciprocal(out=rr[:], in_=rr[:]), reads=[b_rr], writes=[b_rr])
            o16 = oq[0] % 2
            S.op(D, lambda e, o16=o16: e.scalar_tensor_tensor(out=ot16[o16][:], in0=cmb[:], scalar=subcol[:, 0:1], in1=rr[:], op0=ALU.mult, op1=ALU.mult),
                 reads=[b_cmb, b_sub, b_rr], writes=[b_ot16[o16]])
            S.dma("sp", yaT[hh, :, qs], ot16[o16][:], reads=[b_ot16[o16]])
    ybc = SC["ybcT"]
    for i in range(2):
        S.op("pool", lambda e, i=i: e.memset(vt[i][:, :, 64:128], 1.0), writes=[b_vt[i]])
    cur = dict(k=None, v=None)
    sl = dict(q=0, k=0, v=vs)
    for j in range(16):
        if j < 8:
            qm, km, vi, d, scale = 8 + j, 8 + j // 4, j // 4, 64, 64 ** -0.5
        else:
            qm, km, vi, d, scale = 16 + (j - 8), 10 + (j - 8), 2 + (j - 8), 96, 96 ** -0.5
        sl["q"] = (sl["q"] + 1) % 2
        load_q(sl["q"], qm, d)
        if cur["k"] != km:
            sl["k"] = (sl["k"] + 1) % 2
            load_k(sl["k"], km, d)
            cur["k"] = km
        if cur["v"] != vi:
            sl["v"] = (sl["v"] + 1) % 2
            load_v(sl["v"], "BC", vi)
            cur["v"] = vi
        for qb in range(4):
            qs = slice(qb * 512, (qb + 1) * 512)
            ob_ = oq[0] % 2
            oq[0] += 1
            attn_block(sl["q"], sl["k"], sl["v"], d, 128, scale, False, qb, ob_, zsep=False)
            S.op(D, lambda e, ob_=ob_: e.tensor_tensor(out=ot16[ob_][0:64, :], in0=psO[ob_][0:64, :], in1=rz[ob_][0:64, :], op=ALU.mult),
                 reads=[b_psO[ob_], b_rz[ob_]], writes=[b_ot16[ob_]])
            S.dma("sp", ybc[j, :, qs], ot16[ob_][0:64, :], reads=[b_ot16[ob_]])
    F.end()


def emit_MG(F, li, h_src, h1_dst, W, SC, PB):
    S, A = F.S, F.begin()
    h = h_src; h1 = h1_dst
    g_pre = W["norm_pre_mix"][li]; g_post = W["norm_post_mix"][li]
    ybcv = SC["ybcT"].rearrange("(b k two) d t -> b k (two d) t", b=2, two=2)
    ybT = ybcv[0]; ycT = ybcv[1]
    yaTs = SC["yaT"]
    dcol = W["dcol"][li]
    w_glu = W["s5_w_glu"][li]; w_gate = W["w_gate"][li]; w_br = W["w_br"][li]; w_out = W["w_out"][li]
    identd = W["identd"]

    HT = 1024
    HTI = HT // 128
    D = "dve"
    ident = A.sb("ident", [128, 128], BF16); b_ident = Buf()
    S.dma("sp", ident[:], identd[:, :], writes=[b_ident])
    epsT = A.sb("epsT", [128, 1], F32); b_eps = Buf()
    S.op(D, lambda e: e.memset(epsT[:], EPS), writes=[b_eps])
    g_bc = A.sb("g_bc", [128, 1024], F32); b_g = Buf()
    S.dma("pool", g_bc[:], g_pre.partition_broadcast(128), writes=[b_g])
    gp_bc = A.sb("gp_bc", [128, 1024], F32); b_gp = Buf()
    S.dma("pool", gp_bc[:], g_post.partition_broadcast(128), writes=[b_gp])
    sel = A.sb("sel", [128, 2], F32); b_sel = Buf()
    S.dma("sp", sel[:], W["sel"][:, :], writes=[b_sel])
    dc = A.sb("dc", [128, 4], F32); b_dc = Buf()
    S.dma("sp", dc[:], dcol[:, :], writes=[b_dc])

    stg = [A.sb(f"stg{i}", [128, 1024], F32) for i in range(2)]; b_stg = [Buf() for _ in range(2)]
    sidx = [0]
    stq = [A.sb(f"stq{i}", [128, 512], F32) for i in range(4)]; b_stq = [Buf() for _ in range(4)]
    qidx = [0]

    def load_wp(dst, b_dst, src, ktiles):
        for kt in range(ktiles):
            i = qidx[0] % 4
            qidx[0] += 1
            S.dma("sp", stq[i][:], src[kt * 128:(kt + 1) * 128, :], writes=[b_stq[i]])
            S.op("pool", lambda e, i=i, kt=kt: e.tensor_copy(out=dst[:, kt, :], in_=stq[i][:]), reads=[b_stq[i]], writes=[b_dst])

    def load_w(dst, b_dst, src, ktiles, cols):
        for kt in range(ktiles):
            i = sidx[0] % 2
            sidx[0] += 1
            S.dma("sp", stg[i][:, 0:cols], src[kt * 128:(kt + 1) * 128, :], writes=[b_stg[i]])
            eng = "pool" if i else "act"
            if eng == "act":
                S.op("act", lambda e, i=i, kt=kt: e.copy(out=dst[:, kt, 0:cols], in_=stg[i][:, 0:cols]), reads=[b_stg[i]], writes=[b_dst])
            else:
                S.op("pool", lambda e, i=i, kt=kt: e.tensor_copy(out=dst[:, kt, 0:cols], in_=stg[i][:, 0:cols]), reads=[b_stg[i]], writes=[b_dst])

    wgh = [A.sb(f"wgh{i}", [128, 8, 512], BF16) for i in range(2)]; b_wgh = [Buf() for _ in range(2)]
    wbh = [A.sb(f"wbh{i}", [128, 4, 512], BF16) for i in range(2)]; b_wbh = [Buf() for _ in range(2)]
    wb = A.sb("wgl", [128, 4, 1024], BF16); b_wb = Buf()
    mixed = A.sb("mixed", [128, HTI, 1024], F32); b_mixed = [Buf() for _ in range(HTI)]
    xnT = A.sb("xnT", [128, 8, HT], BF16); b_xnT = Buf()
    yaT = A.sb("yaT", [128, 4, HT], BF16); b_yaT = Buf()
    gT = A.sb("gT", [128, 4, HT], BF16); b_gT = Buf()
    sgT = A.sb("sgT", [128, 4, HT], BF16); b_sgT = Buf()
    ydT = A.sb("ydT", [128, 4, HT], BF16); b_ydT = Buf()
    ht = [A.sb(f"ht{i}", [128, 1024], F32) for i in range(2)]; b_ht = [Buf() for _ in range(2)]
    junk = A.sb("junk", [128, 1024], F32); b_junk = Buf()
    st = [A.sb(f"st{i}", [128, 8], F32) for i in range(2)]; b_st = [Buf() for _ in range(2)]
    xn = [A.sb(f"xn{i}", [128, 1024], BF16) for i in range(2)]; b_xn = [Buf() for _ in range(2)]
    e1 = A.sb("e1", [128, HT], F32); b_e1 = Buf()
    e2 = A.sb("e2", [128, HT], F32); b_e2 = Buf()
    e3 = A.sb("e3", [128, HT], F32); b_e3 = Buf()
    eu = A.sb("eu", [128, HT], BF16); b_eu = Buf()
    sg = [A.sb(f"sg{i}", [128, 512], F32) for i in range(2)]; b_sg = [Buf() for _ in range(2)]
    tm = [A.sb(f"tm{i}", [128, 512], F32) for i in range(2)]; b_tm = [Buf() for _ in range(2)]
    osb = A.sb("osb", [128, 1024], F32); b_osb = Buf()
    mT = A.sb("mT", [128, 8, 128], BF16); b_mT = Buf()
    pT = A.ps("pT", [128, 1024], BF16); b_pT = Buf()
    pA = [A.ps(f"pA{i}", [128, 512], F32) for i in range(2)]; b_pA = [Buf() for _ in range(2)]
    pB = [A.ps(f"pB{i}", [128, 512], F32) for i in range(2)]; b_pB = [Buf() for _ in range(2)]
    pG = [A.ps(f"pG{i}", [128, 512], F32) for i in range(2)]; b_pG = [Buf() for _ in range(2)]

    for th in range(NT // HT):
        tb0 = th * HT
        for t in range(HTI):
            i = t % 2
            r0 = tb0 + t * 128
            S.dma("sp", ht[i][:], h[r0:r0 + 128, :], writes=[b_ht[i]])
            rms_rstd(S, ht[i][:], b_ht[i], junk[:], b_junk, st[i], b_st[i], epsT[:, 0:1], b_eps, 1024)
            S.op(D, lambda e, i=i: e.scalar_tensor_tensor(out=xn[i][:], in0=ht[i][:], scalar=st[i][:, 2:3], in1=g_bc[:], op0=ALU.mult, op1=ALU.mult),
                 reads=[b_ht[i], b_st[i], b_g], writes=[b_xn[i]])
            for kt in range(8):
                S.op("pe", lambda e, i=i, kt=kt: e.transpose(out=pT[:, kt * 128:(kt + 1) * 128], in_=xn[i][:, kt * 128:(kt + 1) * 128], identity=ident[:]),
                     reads=[b_xn[i], b_ident], writes=[b_pT])
            S.op("act", lambda e, t=t: e.copy(out=xnT[:, :, t * 128:(t + 1) * 128], in_=pT[:].rearrange("p (k t) -> p k t", k=8)), reads=[b_pT], writes=[b_xnT])
        for kt in range(4):
            S.dma("sp", yaT[:, kt, :], yaTs[kt, :, tb0:tb0 + HT], writes=[b_yaT])
        load_w(wb, b_wb, w_glu, 4, 1024)
        for ct in range(4):
            ya_ = SC["y_all"][ct % 2]
            yrow = (ct // 2) * 128
            S.dma("sp", e1[:], ya_[yrow:yrow + 128, tb0:tb0 + HT], reads=[PB["y_all"]], writes=[b_e1])
            S.dma("sp", e3[:], ya_[yrow:yrow + 128, 2048 + tb0:2048 + tb0 + HT], reads=[PB["y_all"]], writes=[b_e3])
            S.dma("sp", eu[:], SC["uT_loc"][ct * 128:(ct + 1) * 128, tb0:tb0 + HT], writes=[b_eu])
            S.op("pool", lambda e: e.tensor_scalar(out=e1[:], in0=e1[:], scalar1=sel[:, 0:1], scalar2=None, op0=ALU.mult), reads=[b_e1, b_sel], writes=[b_e1])
            S.op(D, lambda e: e.scalar_tensor_tensor(out=e1[:], in0=e3[:], scalar=sel[:, 1:2], in1=e1[:], op0=ALU.mult, op1=ALU.add),
                 reads=[b_e1, b_e3, b_sel], writes=[b_e1])
            S.op(D, lambda e, ct=ct: e.scalar_tensor_tensor(out=e1[:], in0=eu[:], scalar=dc[:, ct:ct + 1], in1=e1[:], op0=ALU.mult, op1=ALU.add),
                 reads=[b_e1, b_eu, b_dc], writes=[b_e1])
            S.op("pool", lambda e: e.tensor_tensor(out=e2[:], in0=e1[:], in1=e1[:], op=ALU.mult), reads=[b_e1], writes=[b_e2])
            S.op("pool", lambda e: e.tensor_scalar(out=e2[:], in0=e2[:], scalar1=0.044715, scalar2=1.0, op0=ALU.mult, op1=ALU.add), reads=[b_e2], writes=[b_e2])
            S.op(D, lambda e: e.tensor_tensor(out=e2[:], in0=e2[:], in1=e1[:], op=ALU.mult), reads=[b_e1, b_e2], writes=[b_e2])
            S.op("act", lambda e: e.activation(out=e3[:], in_=e2[:], func=AF.Sigmoid, scale=2.0 * math.sqrt(2.0 / math.pi)), reads=[b_e2], writes=[b_e3])
            S.op(D, lambda e, ct=ct: e.tensor_tensor(out=gT[:, ct, :], in0=e1[:], in1=e3[:], op=ALU.mult), reads=[b_e1, b_e3], writes=[b_gT])
        gcnt = 0
        for och in (4, 5, 6, 7, 0, 1, 2, 3):
            for tb in range(HT // 512):
                pi = gcnt % 2
                gcnt += 1
                ts = slice(tb * 512, (tb + 1) * 512)
                for kt in range(4):
                    S.op("pe", lambda e, pi=pi, kt=kt, och=och, ts=ts: e.matmul(pG[pi][:], lhsT=wb[:, kt, och * 128:(och + 1) * 128], rhs=gT[:, kt, ts],
                                                                                start=(kt == 0), stop=(kt == 3)), reads=[b_wb, b_gT], writes=[b_pG[pi]])
                if och >= 4:
                    S.op("act", lambda e, pi=pi, och=och, ts=ts: e.activation(out=sgT[:, och - 4, ts], in_=pG[pi][:], func=AF.Sigmoid), reads=[b_pG[pi]], writes=[b_sgT])
                else:
                    S.op(D, lambda e, pi=pi, och=och, ts=ts: e.tensor_tensor(out=ydT[:, och, ts], in0=pG[pi][:], in1=sgT[:, och, ts], op=ALU.mult),
                         reads=[b_pG[pi], b_sgT], writes=[b_ydT])
        units = [(b, hc) for b in range(4) for hc in range(2)]

        def load_unit(u):
            b, hc = units[u]
            load_wp(wgh[u % 2], b_wgh[u % 2], w_gate[:, b * 1024 + hc * 512:b * 1024 + (hc + 1) * 512], 8)
            load_wp(wbh[u % 2], b_wbh[u % 2], w_br[b][:, hc * 512:(hc + 1) * 512], 4)

        for kt in range(4):
            S.dma("sp", gT[:, kt, :], ybT[kt, :, tb0:tb0 + HT], writes=[b_gT])
            S.dma("sp", sgT[:, kt, :], ycT[kt, :, tb0:tb0 + HT], writes=[b_sgT])
        load_unit(0)
        load_unit(1)
        pcnt = 0
        for u, (b, hc) in enumerate(units):
            yT_, b_yT = ((yaT, b_yaT), (gT, b_gT), (sgT, b_sgT), (ydT, b_ydT))[b]
            wgu, b_wgu, wbu, b_wbu = wgh[u % 2], b_wgh[u % 2], wbh[u % 2], b_wbh[u % 2]
            cs = slice(hc * 512, (hc + 1) * 512)
            for t in range(HTI):
                tsl = slice(t * 128, (t + 1) * 128)
                pi = pcnt % 2
                pcnt += 1
                for kt in range(4):
                    S.op("pe", lambda e, kt=kt, pi=pi: e.matmul(pA[pi][:], lhsT=yT_[:, kt, tsl], rhs=wbu[:, kt, :], start=(kt == 0), stop=(kt == 3)),
                         reads=[b_yT, b_wbu], writes=[b_pA[pi]])
                for kt in range(8):
                    S.op("pe", lambda e, kt=kt, pi=pi: e.matmul(pB[pi][:], lhsT=xnT[:, kt, tsl], rhs=wgu[:, kt, :], start=(kt == 0), stop=(kt == 7)),
                         reads=[b_xnT, b_wgu], writes=[b_pB[pi]])
                S.op("act", lambda e, pi=pi: e.activation(out=sg[pi][:], in_=pB[pi][:], func=AF.Sigmoid), reads=[b_pB[pi]], writes=[b_sg[pi]])
                if b == 0:
                    S.op(D, lambda e, pi=pi, t=t: e.tensor_tensor(out=mixed[:, t, cs], in0=pA[pi][:], in1=sg[pi][:], op=ALU.mult),
                         reads=[b_pA[pi], b_sg[pi]], writes=[b_mixed[t]])
                else:
                    S.op(D, lambda e, pi=pi: e.tensor_tensor(out=tm[pi][:], in0=pA[pi][:], in1=sg[pi][:], op=ALU.mult),
                         reads=[b_pA[pi], b_sg[pi]], writes=[b_tm[pi]])
                    S.op(D, lambda e, pi=pi, t=t: e.tensor_tensor(out=mixed[:, t, cs], in0=mixed[:, t, cs], in1=tm[pi][:], op=ALU.add),
                         reads=[b_tm[pi], b_mixed[t]], writes=[b_mixed[t]])
            if u + 2 < len(units):
                load_unit(u + 2)
        for hc in range(2):
            load_wp(wgh[hc], b_wgh[hc], w_out[:, hc * 512:(hc + 1) * 512], 8)
        for t in range(HTI):
            i = t % 2
            r0 = tb0 + t * 128
            S.dma("sp", ht[i][:], h[r0:r0 + 128, :], writes=[b_ht[i]])
            S.op("act", lambda e, i=i, t=t: e.copy(out=xn[i][:], in_=mixed[:, t, :]), reads=[b_mixed[t]], writes=[b_xn[i]])
            for kt in range(8):
                S.op("pe", lambda e, i=i, kt=kt: e.transpose(out=pT[:, kt * 128:(kt + 1) * 128], in_=xn[i][:, kt * 128:(kt + 1) * 128], identity=ident[:]),
                     reads=[b_xn[i], b_ident], writes=[b_pT])
            S.op("act", lambda e: e.copy(out=mT[:].rearrange("p k t -> p (k t)"), in_=pT[:]), reads=[b_pT], writes=[b_mT])
            for hc in range(2):
                cs = slice(hc * 512, (hc + 1) * 512)
                for kt in range(8):
                    S.op("pe", lambda e, kt=kt, hc=hc: e.matmul(pA[hc][:], lhsT=mT[:, kt, :], rhs=wgh[hc][:, kt, :], start=(kt == 0), stop=(kt == 7)),
                         reads=[b_mT, b_wgh[hc]], writes=[b_pA[hc]])
                if hc == 0:
                    S.op("act", lambda e, hc=hc, cs=cs: e.copy(out=osb[:, cs], in_=pA[hc][:]), reads=[b_pA[hc]], writes=[b_osb])
                else:
                    S.op(D, lambda e, hc=hc, cs=cs: e.tensor_copy(out=osb[:, cs], in_=pA[hc][:]), reads=[b_pA[hc]], writes=[b_osb])
            rms_rstd(S, osb[:], b_osb, junk[:], b_junk, st[i], b_st[i], epsT[:, 0:1], b_eps, 1024)
            S.op(D, lambda e, i=i: e.scalar_tensor_tensor(out=osb[:], in0=osb[:], scalar=st[i][:, 2:3], in1=gp_bc[:], op0=ALU.mult, op1=ALU.mult),
                 reads=[b_osb, b_st[i], b_gp], writes=[b_osb])
            S.op("pool", lambda e, i=i: e.tensor_tensor(out=ht[i][:], in0=ht[i][:], in1=osb[:], op=ALU.add), reads=[b_ht[i], b_osb], writes=[b_ht[i]])
            S.dma("sp", h1[r0:r0 + 128, :], ht[i][:], reads=[b_ht[i]])
    F.end()


def emit_FF(F, li, h1_src, h2_dst, W):
    S, A = F.S, F.begin()
    h1 = h1_src; h2 = h2_dst
    g_pre = W["norm_pre_ffn"][li]; g_post = W["norm_post_ffn"][li]
    w_fi = W["w_ffn_in"][li]; w_fo = W["w_ffn_out"][li]
    identd = W["identd"]
    D = "dve"
    ident = A.sb("ident", [128, 128], BF16); b_ident = Buf()
    S.dma("sp", ident[:], identd[:, :], writes=[b_ident])
    epsT = A.sb("epsT", [128, 1], F32); b_eps = Buf()
    S.op(D, lambda e: e.memset(epsT[:], EPS), writes=[b_eps])
    g_bc = A.sb("g_bc", [128, 1024], F32); b_g = Buf()
    S.dma("pool", g_bc[:], g_pre.partition_broadcast(128), writes=[b_g])
    gp_bc = A.sb("gp_bc", [128, 1024], F32); b_gp = Buf()
    S.dma("pool", gp_bc[:], g_post.partition_broadcast(128), writes=[b_gp])
    stg = [A.sb(f"stg{i}", [128, 1024], F32) for i in range(2)]; b_stg = [Buf() for _ in range(2)]
    sidx = [0]

    def load_w(dst, b_dst, src, ktiles, cols):
        for kt in range(ktiles):
            i = sidx[0] % 2
            sidx[0] += 1
            S.dma("sp", stg[i][:, 0:cols], src[kt * 128:(kt + 1) * 128, :], writes=[b_stg[i]])
            S.op("pool", lambda e, i=i, kt=kt: e.tensor_copy(out=dst[:, kt, 0:cols], in_=stg[i][:, 0:cols]), reads=[b_stg[i]], writes=[b_dst])

    fnT = A.sb("fnT", [128, 8, NT], BF16); b_fnT = Buf()
    facc = A.sb("facc", [128, NTI, 1024], F32); b_facc = [Buf() for _ in range(NTI)]
    hidT = A.sb("hidT", [128, 4, NT], BF16); b_hidT = Buf()
    wfis = [A.sb(f"wfi{i}", [128, 8, 512], BF16) for i in range(2)]; b_wfis = [Buf() for _ in range(2)]
    wfos = [A.sb(f"wfo{i}", [128, 4, 1024], BF16) for i in range(2)]; b_wfos = [Buf() for _ in range(2)]
    ht = [A.sb(f"ht{i}", [128, 1024], F32) for i in range(2)]; b_ht = [Buf() for _ in range(2)]
    junk = A.sb("junk", [128, 1024], F32); b_junk = Buf()
    st = [A.sb(f"st{i}", [128, 8], F32) for i in range(2)]; b_st = [Buf() for _ in range(2)]
    xn = [A.sb(f"xn{i}", [128, 1024], BF16) for i in range(2)]; b_xn = [Buf() for _ in range(2)]
    r32 = [A.sb(f"r32{i}", [128, 512], F32) for i in range(2)]; b_r32 = [Buf() for _ in range(2)]
    pT = A.ps("pT", [128, 1024], BF16); b_pT = Buf()
    pH = [A.ps(f"pH{i}", [128, 512], F32) for i in range(2)]; b_pH = [Buf() for _ in range(2)]
    pF = [A.ps(f"pF{i}", [128, 512], F32) for i in range(4)]; b_pF = [Buf() for _ in range(4)]

    for t in range(NTI):
        i = t % 2
        r0 = t * 128
        S.dma("sp", ht[i][:], h1[r0:r0 + 128, :], writes=[b_ht[i]])
        rms_rstd(S, ht[i][:], b_ht[i], junk[:], b_junk, st[i], b_st[i], epsT[:, 0:1], b_eps, 1024)
        S.op(D, lambda e, i=i: e.scalar_tensor_tensor(out=xn[i][:], in0=ht[i][:], scalar=st[i][:, 2:3], in1=g_bc[:], op0=ALU.mult, op1=ALU.mult),
             reads=[b_ht[i], b_st[i], b_g], writes=[b_xn[i]])
        for kt in range(8):
            S.op("pe", lambda e, i=i, kt=kt: e.transpose(out=pT[:, kt * 128:(kt + 1) * 128], in_=xn[i][:, kt * 128:(kt + 1) * 128], identity=ident[:]),
                 reads=[b_xn[i], b_ident], writes=[b_pT])
        S.op("act", lambda e, t=t: e.copy(out=fnT[:, :, t * 128:(t + 1) * 128], in_=pT[:].rearrange("p (k t) -> p k t", k=8)), reads=[b_pT], writes=[b_fnT])
    hcnt = 0
    fcnt = 0
    def load_chunk(c):
        load_w(wfis[c % 2], b_wfis[c % 2], w_fi[:, c * 512:(c + 1) * 512], 8, 512)
        load_w(wfos[c % 2], b_wfos[c % 2], w_fo[c * 512:(c + 1) * 512, :], 4, 1024)

    load_chunk(0)
    load_chunk(1)
    for c in range(8):
        wfi, b_wfi, wfo, b_wfo = wfis[c % 2], b_wfis[c % 2], wfos[c % 2], b_wfos[c % 2]
        for j in range(4):
            for tb in range(4):
                pi = hcnt % 2
                hcnt += 1
                ts = slice(tb * 512, (tb + 1) * 512)
                for kt in range(8):
                    S.op("pe", lambda e, pi=pi, kt=kt, j=j, ts=ts: e.matmul(pH[pi][:], lhsT=wfi[:, kt, j * 128:(j + 1) * 128], rhs=fnT[:, kt, ts],
                                                                            start=(kt == 0), stop=(kt == 7)), reads=[b_wfi, b_fnT], writes=[b_pH[pi]])
                S.op("act", lambda e, pi=pi: e.activation(out=r32[pi][:], in_=pH[pi][:], func=AF.Relu), reads=[b_pH[pi]], writes=[b_r32[pi]])
                S.op(D, lambda e, pi=pi, j=j, ts=ts: e.tensor_tensor(out=hidT[:, j, ts], in0=r32[pi][:], in1=r32[pi][:], op=ALU.mult),
                     reads=[b_r32[pi]], writes=[b_hidT])
        for t in range(NTI):
            tsl = slice(t * 128, (t + 1) * 128)
            for hc in range(2):
                pi = fcnt % 4
                fcnt += 1
                cs = slice(hc * 512, (hc + 1) * 512)
                for j in range(4):
                    S.op("pe", lambda e, pi=pi, j=j, tsl=tsl, cs=cs: e.matmul(pF[pi][:], lhsT=hidT[:, j, tsl], rhs=wfo[:, j, cs], start=(j == 0), stop=(j == 3)),
                         reads=[b_hidT, b_wfo], writes=[b_pF[pi]])
                if c == 0:
                    S.op("act", lambda e, pi=pi, t=t, cs=cs: e.copy(out=facc[:, t, cs], in_=pF[pi][:]), reads=[b_pF[pi]], writes=[b_facc[t]])
                else:
                    S.op(D, lambda e, pi=pi, t=t, cs=cs: e.tensor_tensor(out=facc[:, t, cs], in0=pF[pi][:], in1=facc[:, t, cs], op=ALU.add),
                         reads=[b_pF[pi], b_facc[t]], writes=[b_facc[t]])
        if c + 2 < 8:
            load_chunk(c + 2)
    for t in range(NTI):
        i = t % 2
        r0 = t * 128
        S.dma("sp", ht[i][:], h1[r0:r0 + 128, :], writes=[b_ht[i]])
        rms_rstd(S, facc[:, t, :], b_facc[t], junk[:], b_junk, st[i], b_st[i], epsT[:, 0:1], b_eps, 1024)
        S.op(D, lambda e, i=i, t=t: e.scalar_tensor_tensor(out=facc[:, t, :], in0=facc[:, t, :], scalar=st[i][:, 2:3], in1=gp_bc[:], op0=ALU.mult, op1=ALU.mult),
             reads=[b_facc[t], b_st[i], b_gp], writes=[b_facc[t]])
        S.op("pool", lambda e, i=i, t=t: e.tensor_tensor(out=ht[i][:], in0=ht[i][:], in1=facc[:, t, :], op=ALU.add), reads=[b_ht[i], b_facc[t]], writes=[b_ht[i]])
        S.dma("sp", h2[r0:r0 + 128, :], ht[i][:], reads=[b_ht[i]])
    F.end()


def make_scratch(F):
    SC = dict(
        QT=F.scratch("QT", [24, 96, NT], BF16),
        kt_loc=[F.scratch(f"kt_loc{j}", [(b - a) * 96, NT], BF16) for j, (a, b) in enumerate(KCH)],
        kt_all=[F.scratch(f"kt_all{j}", [2 * (b - a) * 96, NT], BF16) for j, (a, b) in enumerate(KCH)],
        v_loc=[F.scratch(f"v_loc{j}", [128, w], BF16) for j, w in enumerate((8192, 5120, 5120))],
        v_all=[F.scratch(f"v_all{j}", [256, w], BF16) for j, w in enumerate((8192, 5120, 5120))],
        uT_loc=F.scratch("uT_loc", [512, NT], BF16), uT_all=F.scratch("uT_all", [1024, NT], BF16),
        y_loc=[F.scratch(f"y_loc{j}", [128, 4096], F32) for j in range(2)],
        y_all=[F.scratch(f"y_all{j}", [256, 4096], F32) for j in range(2)],
        yaT=F.scratch("yaT", [4, 128, NT], BF16), ybcT=F.scratch("ybcT", [16, 64, NT], BF16),
        h1=F.scratch("h1buf", [NT, 1024], F32), hmid=F.scratch("hmid", [NT, 1024], F32),
    )
    PB = {k: Buf(k) for k in ("kt_all", "v_all", "uT_all", "y_loc", "y_all")}
    return SC, PB


def build_fused():
    F = Fused()
    C, nc, S = F.C, F.nc, F.S
    W = {}
    def I(name, shape, dt=F32):
        W[name] = C.inp(name, shape, dt)
    I("x_own", [NT, 1024])
    for k in ("norm_pre_mix", "norm_post_mix", "norm_pre_ffn", "norm_post_ffn"):
        I(k, [2, 1024])
    I("w_in", [2, 1024, 3232]); I("w_gate", [2, 1024, 4096]); I("lamv", [2, 256]); I("subln", [2, 128])
    I("gB", [2, 640]); I("gC", [2, 384]); I("mla_w_uq", [2, 256, 768]); I("mla_w_ukv", [2, 128, 1024])
    I("areT", [2, 128, 2 * NGP]); I("aimT", [2, 128, 2 * NGP]); I("ldtT", [2, 128, 2 * NGP])
    for k in ("Bre", "Bim", "Cre", "Cim"):
        I(k, [2, 2, NGP, 128, 128])
    I("dcol", [2, 128, 4]); I("s5_w_glu", [2, 512, 1024]); I("w_br", [2, 4, 512, 1024]); I("w_out", [2, 1024, 1024])
    I("w_ffn_in", [2, 1024, 4096]); I("w_ffn_out", [2, 4096, 1024])
    I("cosb", [NT, 64]); I("sinb", [NT, 64]); I("cosc", [NT, 32]); I("sinc", [NT, 32])
    I("tab", [4, 128, TABW2]); I("iota", [128, SEG]); I("sel", [128, 2]); I("identd", [128, 128], BF16)
    out = C.out("out", [NT, 1024])
    SC, PB = make_scratch(F)
    h_src = W["x_own"]
    for li in range(2):
        h_dst = SC["hmid"] if li == 0 else out
        emit_P(F, li, h_src, W, SC)
        F.begin()
        S.cc("AllGather", [SC["uT_loc"].opt()], [SC["uT_all"].opt()], PAIRS, writes=[PB["uT_all"]])
        for j in range(4):
            S.cc("AllGather", [SC["kt_loc"][j].opt()], [SC["kt_all"][j].opt()], PAIRS, writes=[PB["kt_all"]])
        for j in range(3):
            S.cc("AllGather", [SC["v_loc"][j].opt()], [SC["v_all"][j].opt()], PAIRS, writes=[PB["v_all"]])
        F.end()
        emit_S5(F, li, W, SC, PB)
        F.begin()
        for j in range(2):
            S.cc("AllGather", [SC["y_loc"][j].opt()], [SC["y_all"][j].opt()], PAIRS, reads=[PB["y_loc"]], writes=[PB["y_all"]])
        F.end()
        emit_AT(F, li, W, SC, PB)
        emit_MG(F, li, h_src, SC["h1"], W, SC, PB)
        emit_FF(F, li, SC["h1"], h_dst, W)
        h_src = h_dst
    F.begin()
    F.end(include_cc=True)
    return nc


def kernel(**inputs):
    x = np.asarray(inputs["x"], dtype=np.float32)
    P = {k: np.asarray(v, dtype=np.float32) for k, v in inputs.items() if k != "x"}
    NCORE = 8
    ident = np.eye(128, dtype=BF)
    iota = _c(np.tile(np.arange(SEG, dtype=np.float32)[None, :], (128, 1)))
    toks = [np.arange(hf * NT, (hf + 1) * NT) for hf in range(2)]
    rope = [rope_tables(t // 64, t % 64, t) for t in toks]
    tabs = [alibi_table_core(hf) for hf in range(2)]
    shared = {k: P[k] for k in ("norm_pre_mix", "norm_post_mix", "norm_pre_ffn", "norm_post_ffn", "w_in", "w_gate", "mla_w_uq", "mla_w_ukv",
                                "s5_w_glu", "w_out", "w_ffn_in", "w_ffn_out")}
    shared["subln"] = P["diff_subln"]
    shared["lamv"] = _c(np.concatenate([P["diff_lam_q1"], P["diff_lam_k1"], P["diff_lam_q2"], P["diff_lam_k2"]], axis=1))
    shared["gB"] = _c(np.concatenate([np.tile(P["gqa_q_norm"], (1, 8)), np.tile(P["gqa_k_norm"], (1, 2))], axis=1))
    shared["gC"] = _c(np.concatenate([P["mla_q_norm"], P["mla_kv_norm"]], axis=1))
    shared["w_br"] = _c(np.stack([P["w_br_a"], P["w_br_b"], P["w_br_c"], P["w_br_d"]], axis=1))
    shared["dcol"] = _c(P["s5_d"].reshape(2, 4, 128).transpose(0, 2, 1))
    shared["iota"] = iota
    shared["identd"] = ident
    s5keys = ("s5_a_re", "s5_a_im", "s5_log_dt", "s5_b_re", "s5_b_im", "s5_c_re", "s5_c_im")
    s5l = []
    for hf in range(2):
        per_layer = [s5_host_layout({k: P[k] for k in s5keys}, li, hf) for li in range(2)]
        s5l.append({k: _c(np.stack([per_layer[0][k], per_layer[1][k]])) for k in per_layer[0]})
    maps = []
    for c in range(NCORE):
        b, hf = c // 2, c % 2
        m = dict(shared)
        m["x_own"] = _c(x[b, toks[hf]])
        cb, sb_, cc, sc = rope[hf]
        m.update(cosb=cb, sinb=sb_, cosc=cc, sinc=sc, tab=tabs[hf])
        sel = np.zeros((128, 2), np.float32); sel[:, hf] = 1.0
        m["sel"] = sel
        m.update(s5l[hf])
        maps.append(m)
    res = _run(build_fused(), maps)
    out = np.zeros_like(x)
    for c in range(NCORE):
        out[c // 2, toks[c % 2]] = res[c]["out"]
    return out
```

```python
import math
import ml_dtypes
import numpy as np
import concourse.bass as bass
import concourse.mybir as mybir
from concourse.bass_utils import run_bass_kernel_spmd

F32 = mybir.dt.float32
BF16 = mybir.dt.bfloat16
AF = mybir.ActivationFunctionType
ALU = mybir.AluOpType
AX = mybir.AxisListType


class Buf:
    __slots__ = ("name", "w", "r")

    def __init__(self, name=""):
        self.name = name
        self.w = None
        self.r = []


class _Rec:
    def __getattr__(self, name):
        def f(*a, **k):
            self.call = (name, a, k)
            return self
        return f


class Sched:
    ENGS = ("pe", "act", "dve", "pool", "sp")
    PHASE = 12000

    def __init__(self, nc, n_dma_sems=32):
        self.nc = nc
        self.q = {e: [] for e in self.ENGS}
        self.eng_sem = {}
        self.eng_cnt = {e: 0 for e in self.ENGS}
        self.seen = {e: {} for e in self.ENGS}
        self._ctx = []
        self.n_dma = n_dma_sems
        self.dma_sems = []
        self.dma_val = []
        self.dma_i = 0
        self.n_inst = {e: 0 for e in self.ENGS}
        self.cc_toks = []
        self.actual = {}
        self.dma_rr = {}
        self.pe_sem_ids = set()
        for i in range(n_dma_sems):
            cm = nc.semaphore(f"dq{i}")
            self.dma_sems.append(cm.__enter__())
            self._ctx.append(cm)
            self.dma_val.append(0)
        for e in self.ENGS:
            self._new_eng_sem(e)

    def _new_eng_sem(self, e):
        cm = self.nc.semaphore(f"s_{e}_{len(self._ctx)}")
        self.eng_sem[e] = cm.__enter__()
        self._ctx.append(cm)
        self.eng_cnt[e] = 0
        if e == "pe":
            self.pe_sem_ids.add(id(self.eng_sem[e]))

    def close(self):
        for cm in reversed(self._ctx):
            cm.__exit__(None, None, None)

    def _eng(self, e):
        nc = self.nc
        return {"pe": nc.tensor, "act": nc.scalar, "dve": nc.vector,
                "pool": nc.gpsimd, "sp": nc.sync}[e]

    def _wait(self, e, tok):
        sem, val = tok
        key = id(sem)
        if e == "pe" and key in self.pe_sem_ids:
            return
        if self.seen[e].get(key, 0) >= val:
            return
        self.seen[e][key] = val
        self.q[e].append(("wait", sem, val))

    def _deps(self, e, reads, writes):
        for b in reads:
            if b.w is not None:
                self._wait(e, b.w)
        for b in writes:
            if b.w is not None:
                self._wait(e, b.w)
            for t in b.r:
                self._wait(e, t)

    def _commit(self, tok, reads, writes):
        for b in reads:
            b.r.append(tok)
            if len(b.r) > 12:
                best = {}
                for s, v in b.r:
                    if id(s) not in best or best[id(s)][1] < v:
                        best[id(s)] = (s, v)
                b.r = list(best.values())
        for b in writes:
            b.w = tok
            b.r = []

    def op(self, e, fn, reads=(), writes=()):
        if self.eng_cnt[e] >= self.PHASE:
            self._new_eng_sem(e)
        self._deps(e, reads, writes)
        self.eng_cnt[e] += 1
        tok = (self.eng_sem[e], self.eng_cnt[e])
        rec = _Rec()
        fn(rec)
        self.q[e].append(("op", rec.call, tok[0], tok[1]))
        self._commit(tok, reads, writes)
        self.n_inst[e] += 1
        return tok

    def dma(self, e, out, in_, reads=(), writes=(), **kw):
        lo, hi = (0, self.n_dma - 8) if e != "pool" else (self.n_dma - 8, self.n_dma)
        key = "sw" if e == "pool" else "hw"
        i = self.dma_rr.get(key, lo)
        self.dma_rr[key] = lo + ((i - lo + 1) % (hi - lo))
        sem = self.dma_sems[i]
        if self.dma_val[i] > 0:
            self._wait(e, (sem, self.dma_val[i]))
        self._deps(e, reads, writes)
        self.dma_val[i] += 16
        tok = (sem, self.dma_val[i])
        self.q[e].append(("dma", out, in_, sem, kw))
        self._commit(tok, reads, writes)
        self.n_inst[e] += 1
        return tok

    def wait_all(self, e, toks):
        for t in toks:
            self._wait(e, t)

    def cc(self, kind, ins, outs, groups, reads=(), writes=()):
        e = "pool"
        cm = self.nc.semaphore(f"cc{len(self._ctx)}")
        sem = cm.__enter__()
        self._ctx.append(cm)
        self._deps(e, reads, writes)
        tok = (sem, 1)
        self.q[e].append(("cc", kind, ins, outs, groups, sem))
        self._commit(tok, reads, writes)
        self.cc_toks.append(tok)
        return tok

    def flush(self, include_cc=True):
        nc = self.nc
        toks = []
        for i, sem in enumerate(self.dma_sems):
            if self.dma_val[i] > 0:
                toks.append((sem, self.dma_val[i]))
        for e in self.ENGS:
            if self.eng_cnt[e] > 0:
                toks.append((self.eng_sem[e], self.eng_cnt[e]))
        if include_cc:
            toks += self.cc_toks
        for e in self.ENGS:
            for t in toks:
                self._wait(e, t)
        eng_sem_ids = set()
        needed = {}
        for e in self.ENGS:
            for it in self.q[e]:
                if it[0] == "op":
                    eng_sem_ids.add(id(it[2]))
        for e in self.ENGS:
            for it in self.q[e]:
                if it[0] == "wait" and id(it[1]) in eng_sem_ids:
                    needed.setdefault(id(it[1]), set()).add(it[2])
        vmap = {}
        for e in self.ENGS:
            for it in self.q[e]:
                if it[0] == "op":
                    sid, v = id(it[2]), it[3]
                    if v in needed.get(sid, ()):
                        self.actual[sid] = self.actual.get(sid, 0) + 1
                        vmap[(sid, v)] = self.actual[sid]
        with nc.Block() as block:
            def emit(e, eng):
                for it in self.q[e]:
                    if it[0] == "wait":
                        sid = id(it[1])
                        if sid in eng_sem_ids:
                            eng.wait_ge(it[1], vmap[(sid, it[2])])
                        else:
                            eng.wait_ge(it[1], it[2])
                    elif it[0] == "op":
                        name, a, k = it[1]
                        ins = getattr(eng, name)(*a, **k)
                        if (id(it[2]), it[3]) in vmap:
                            ins.then_inc(it[2], 1)
                    elif it[0] == "cc":
                        _, kind, ins, outs, groups, sem = it
                        eng.collective_compute(kind, ALU.bypass, replica_groups=groups, ins=ins, outs=outs).then_inc(sem, 1)
                    else:
                        _, out, in_, sem, kw = it
                        eng.dma_start(out=out, in_=in_, **kw).then_inc(sem, 16)

            @block.tensor
            def _(eng):
                emit("pe", eng)

            @block.scalar
            def _(eng):
                emit("act", eng)

            @block.vector
            def _(eng):
                emit("dve", eng)

            @block.gpsimd
            def _(eng):
                emit("pool", eng)

            @block.sync
            def _(eng):
                emit("sp", eng)
        self.q = {e: [] for e in self.ENGS}


class TileAlloc:
    _uid = [0]

    def __init__(self, nc):
        self.nc = nc
        self._ctx = []
        TileAlloc._uid[0] += 1
        self.tag = f"a{TileAlloc._uid[0]}_"

    def sb(self, name, shape, dtype):
        cm = self.nc.sbuf_tensor("sb_" + self.tag + name, list(shape), dtype)
        t = cm.__enter__()
        self._ctx.append(cm)
        return t

    def ps(self, name, shape, dtype):
        cm = self.nc.psum_tensor("ps_" + self.tag + name, list(shape), dtype)
        t = cm.__enter__()
        self._ctx.append(cm)
        return t

    def close(self):
        for cm in reversed(self._ctx):
            cm.__exit__(None, None, None)


BF = ml_dtypes.bfloat16
NT = 2048
NTI = NT // 128
EPS = 1e-6


def _dram(nc, name, shape, dt, kind):
    return nc.dram_tensor(name, list(shape), dt, kind=kind).ap()


class Ctx:
    def __init__(self):
        self.nc = bass.Bass("TRN2", target_bir_lowering=False)
        self.S = Sched(self.nc)
        self.A = TileAlloc(self.nc)
        self.ins = {}

    def inp(self, name, shape, dt=F32):
        return _dram(self.nc, name, shape, dt, "ExternalInput")

    def out(self, name, shape, dt=F32):
        return _dram(self.nc, name, shape, dt, "ExternalOutput")


def load_weight_bf16(C, dst, dstbuf, src, rows, cols, stage, stagebufs, ktiles, idx=[0]):
    S = C.S
    for kt in range(ktiles):
        i = idx[0] % len(stage)
        idx[0] += 1
        st, sb = stage[i], stagebufs[i]
        r0 = kt * 128
        nr = min(128, rows - r0)
        S.dma("sp", st[0:nr, 0:cols], src[r0:r0 + nr, :], writes=[sb])
        eng = "dve" if i % 2 == 0 else "pool"
        S.op(eng, lambda e, st=st, kt=kt, nr=nr: e.tensor_copy(out=dst[0:nr, kt, 0:cols], in_=st[0:nr, 0:cols]),
             reads=[sb], writes=[dstbuf])


def build_P():
    C = Ctx()
    nc, S, A = C.nc, C.S, C.A
    h = C.inp("h", [NT, 1024])
    g_pre = C.inp("g_pre", [1024])
    w_in = C.inp("w_in", [1024, 3232])
    gB = C.inp("gB", [640])
    gC = C.inp("gC", [384])
    w_uq = C.inp("w_uq", [256, 768])
    w_ukv = C.inp("w_ukv", [128, 1024])
    cosb = C.inp("cosb", [NT, 64])
    sinb = C.inp("sinb", [NT, 64])
    cosc = C.inp("cosc", [NT, 32])
    sinc = C.inp("sinc", [NT, 32])
    identd = C.inp("identd", [128, 128], BF16)
    oa = C.out("oa", [NT, 1536], BF16)
    ob = C.out("ob", [NT, 768], BF16)
    ocq = C.out("ocq", [NT, 768], BF16)
    ockv = C.out("ockv", [NT, 1024], BF16)
    ockr = C.out("ockr", [NT, 32], BF16)
    ou = C.out("ou", [NT, 512], F32)

    ident = A.sb("ident", [128, 128], BF16); b_ident = Buf()
    S.dma("sp", ident[:], identd[:, :], writes=[b_ident])
    g_bc = A.sb("g_bc", [128, 1024], F32); b_g = Buf()
    S.dma("pool", g_bc[:], g_pre.partition_broadcast(128), writes=[b_g])
    gB_bc = A.sb("gB_bc", [128, 640], F32); b_gB = Buf()
    S.dma("pool", gB_bc[:], gB.partition_broadcast(128), writes=[b_gB])
    gC_bc = A.sb("gC_bc", [128, 384], F32); b_gC = Buf()
    S.dma("pool", gC_bc[:], gC.partition_broadcast(128), writes=[b_gC])
    epsT = A.sb("epsT", [128, 1], F32); b_eps = Buf()
    S.op("dve", lambda e: e.memset(epsT[:], EPS), writes=[b_eps])

    stage = [A.sb(f"wst{i}", [128, 3232], F32) for i in range(2)]
    stageb = [Buf() for _ in range(2)]
    wb = A.sb("wb", [128, 8, 3232], BF16); b_wb = Buf()
    load_weight_bf16(C, wb, b_wb, w_in, 1024, 3232, stage, stageb, 8)
    wuq = A.sb("wuq", [128, 2, 768], BF16); b_wuq = Buf()
    load_weight_bf16(C, wuq, b_wuq, w_uq, 256, 768, stage, stageb, 2)
    wukv = A.sb("wukv", [128, 1, 1024], BF16); b_wukv = Buf()
    load_weight_bf16(C, wukv, b_wukv, w_ukv, 128, 1024, stage, stageb, 1)

    NB = 2
    ht = [A.sb(f"ht{i}", [128, 1024], F32) for i in range(NB)]; b_ht = [Buf() for _ in range(NB)]
    junk = A.sb("junk", [128, 1024], F32); b_junk = Buf()
    st1 = [A.sb(f"st1{i}", [128, 16], F32) for i in range(NB)]; b_st1 = [Buf() for _ in range(NB)]
    xn = [A.sb(f"xn{i}", [128, 1024], BF16) for i in range(NB)]; b_xn = [Buf() for _ in range(NB)]
    xnT = [A.sb(f"xnT{i}", [128, 8, 128], BF16) for i in range(NB)]; b_xnT = [Buf() for _ in range(NB)]
    pj = [A.sb(f"pj{i}", [128, 3232], F32) for i in range(NB)]; b_pj = [Buf() for _ in range(NB)]
    tA = [A.sb(f"tA{i}", [128, 1536], BF16) for i in range(NB)]; b_tA = [Buf() for _ in range(NB)]
    tB = [A.sb(f"tB{i}", [128, 768], BF16) for i in range(NB)]; b_tB = [Buf() for _ in range(NB)]
    wk1 = A.sb("wk1", [128, 768], F32); b_wk1 = Buf()
    wk2 = A.sb("wk2", [128, 768], F32); b_wk2 = Buf()
    wk3 = A.sb("wk3", [128, 768], F32); b_wk3 = Buf()
    lat = A.sb("lat", [128, 384], BF16); b_lat = Buf()
    latT = A.sb("latT", [128, 3, 128], BF16); b_latT = Buf()
    qc = A.sb("qc", [128, 768], F32); b_qc = Buf()
    tCq = [A.sb(f"tCq{i}", [128, 768], BF16) for i in range(NB)]; b_tCq = [Buf() for _ in range(NB)]
    tCkv = [A.sb(f"tCkv{i}", [128, 1024], BF16) for i in range(NB)]; b_tCkv = [Buf() for _ in range(NB)]
    tCkr = [A.sb(f"tCkr{i}", [128, 32], BF16) for i in range(NB)]; b_tCkr = [Buf() for _ in range(NB)]
    rb = [A.sb(f"rb{i}", [128, 192], F32) for i in range(NB)]; b_rb = [Buf() for _ in range(NB)]

    pT = A.ps("pT", [128, 1024], BF16); b_pT = Buf()
    pP = [A.ps(f"pP{i}", [128, 512], F32) for i in range(2)]; b_pP = [Buf() for _ in range(2)]
    pU = [A.ps(f"pU{i}", [128, 512], F32) for i in range(2)]; b_pU = [Buf() for _ in range(2)]

    chunks = [(c0, min(512, 3232 - c0)) for c0 in range(0, 3232, 512)]
    for t in range(NTI):
        i = t % NB
        r0 = t * 128
        S.dma("sp", ht[i][:], h[r0:r0 + 128, :], writes=[b_ht[i]])
        S.dma("sp", rb[i][:, 0:64], cosb[r0:r0 + 128, :], writes=[b_rb[i]])
        S.dma("sp", rb[i][:, 64:128], sinb[r0:r0 + 128, :], writes=[b_rb[i]])
        S.dma("sp", rb[i][:, 128:160], cosc[r0:r0 + 128, :], writes=[b_rb[i]])
        S.dma("sp", rb[i][:, 160:192], sinc[r0:r0 + 128, :], writes=[b_rb[i]])
        s1 = st1[i]
        S.op("act", lambda e, i=i, s1=s1: e.activation(out=junk[:], in_=ht[i][:], func=AF.Square, accum_out=s1[:, 0:1]),
             reads=[b_ht[i]], writes=[b_junk, b_st1[i]])
        S.op("act", lambda e, s1=s1: e.activation(out=s1[:, 1:2], in_=s1[:, 0:1], func=AF.Sqrt, bias=epsT[:, 0:1], scale=1.0 / 1024),
             reads=[b_st1[i], b_eps], writes=[b_st1[i]])
        S.op("dve", lambda e, s1=s1: e.reciprocal(out=s1[:, 2:3], in_=s1[:, 1:2]), reads=[b_st1[i]], writes=[b_st1[i]])
        S.op("dve", lambda e, i=i, s1=s1: e.scalar_tensor_tensor(out=xn[i][:], in0=ht[i][:], scalar=s1[:, 2:3], in1=g_bc[:],
                                                               op0=ALU.mult, op1=ALU.mult),
             reads=[b_ht[i], b_st1[i], b_g], writes=[b_xn[i]])
        for kt in range(8):
            S.op("pe", lambda e, i=i, kt=kt: e.transpose(out=pT[:, kt * 128:(kt + 1) * 128], in_=xn[i][:, kt * 128:(kt + 1) * 128], identity=ident[:]),
                 reads=[b_xn[i], b_ident], writes=[b_pT])
        S.op("act", lambda e, i=i: e.copy(out=xnT[i][:].rearrange("p k t -> p (k t)"), in_=pT[:]), reads=[b_pT], writes=[b_xnT[i]])
        for ci, (c0, cw) in enumerate(chunks):
            pb = ci % 2
            for kt in range(8):
                S.op("pe", lambda e, i=i, kt=kt, c0=c0, cw=cw, pb=pb: e.matmul(pP[pb][:, 0:cw], lhsT=xnT[i][:, kt, :], rhs=wb[:, kt, c0:c0 + cw],
                                                                                start=(kt == 0), stop=(kt == 7)),
                     reads=[b_xnT[i], b_wb], writes=[b_pP[pb]])
            eng = "act" if ci % 2 == 0 else "dve"
            if eng == "act":
                S.op("act", lambda e, i=i, c0=c0, cw=cw, pb=pb: e.copy(out=pj[i][:, c0:c0 + cw], in_=pP[pb][:, 0:cw]), reads=[b_pP[pb]], writes=[b_pj[i]])
            else:
                S.op("dve", lambda e, i=i, c0=c0, cw=cw, pb=pb: e.tensor_copy(out=pj[i][:, c0:c0 + cw], in_=pP[pb][:, 0:cw]), reads=[b_pP[pb]], writes=[b_pj[i]])
        p = pj[i]
        S.op("pool", lambda e, i=i, p=p: e.tensor_copy(out=tA[i][:], in_=p[:, 0:1536]), reads=[b_pj[i]], writes=[b_tA[i]])
        S.dma("sp", oa[r0:r0 + 128, :], tA[i][:], reads=[b_tA[i]])
        S.dma("sp", ou[r0:r0 + 128, :], p[:, 2720:3232], reads=[b_pj[i]])
        xB = p[:, 1536:2176]
        S.op("dve", lambda e, xB=xB: e.tensor_tensor(out=wk1[:, 0:640], in0=xB, in1=xB, op=ALU.mult), reads=[b_pj[i]], writes=[b_wk1])
        S.op("dve", lambda e, s1=s1: e.tensor_reduce(out=s1[:, 4:14], in_=wk1[:, 0:640].rearrange("p (h d) -> p h d", h=10), axis=AX.X, op=ALU.add),
             reads=[b_wk1], writes=[b_st1[i]])
        S.op("act", lambda e, s1=s1: e.activation(out=s1[:, 4:14], in_=s1[:, 4:14], func=AF.Sqrt, bias=epsT[:, 0:1], scale=1.0 / 64),
             reads=[b_st1[i], b_eps], writes=[b_st1[i]])
        S.op("dve", lambda e, s1=s1: e.reciprocal(out=s1[:, 4:14], in_=s1[:, 4:14]), reads=[b_st1[i]], writes=[b_st1[i]])
        S.op("dve", lambda e, xB=xB, s1=s1: e.tensor_tensor(out=wk1[:, 0:640].rearrange("p (h d) -> p h d", h=10), in0=xB.rearrange("p (h d) -> p h d", h=10),
                                                           in1=s1[:, 4:14].unsqueeze(2).broadcast_to([128, 10, 64]), op=ALU.mult),
             reads=[b_pj[i], b_st1[i]], writes=[b_wk1])
        S.op("pool", lambda e: e.tensor_tensor(out=wk1[:, 0:640], in0=wk1[:, 0:640], in1=gB_bc[:], op=ALU.mult), reads=[b_wk1, b_gB], writes=[b_wk1])
        cosB = rb[i][:, 0:64].unsqueeze(1).broadcast_to([128, 10, 64])
        S.op("pool", lambda e, cosB=cosB: e.tensor_tensor(out=wk2[:, 0:640].rearrange("p (h d) -> p h d", h=10), in0=wk1[:, 0:640].rearrange("p (h d) -> p h d", h=10),
                                                         in1=cosB, op=ALU.mult), reads=[b_wk1, b_rb[i]], writes=[b_wk2])
        x5 = wk1[:, 0:640].rearrange("p (h a b d) -> p h a b d", h=10, a=2, b=2)
        o5 = wk3[:, 0:640].rearrange("p (h a b d) -> p h a b d", h=10, a=2, b=2)
        s4 = rb[i][:, 64:128].rearrange("p (a b d) -> p a b d", a=2, b=2)
        for bb in range(2):
            S.op("dve", lambda e, bb=bb, x5=x5, o5=o5, s4=s4: e.tensor_tensor(out=o5[:, :, :, bb, :], in0=x5[:, :, :, 1 - bb, :],
                                                                             in1=s4[:, :, bb, :].unsqueeze(1).broadcast_to([128, 10, 2, 16]), op=ALU.mult),
                 reads=[b_wk1, b_rb[i]], writes=[b_wk3])
        S.op("dve", lambda e, i=i: e.tensor_tensor(out=tB[i][:, 0:640], in0=wk2[:, 0:640], in1=wk3[:, 0:640], op=ALU.add), reads=[b_wk2, b_wk3], writes=[b_tB[i]])
        S.op("pool", lambda e, i=i, p=p: e.tensor_copy(out=tB[i][:, 640:768], in_=p[:, 2176:2304]), reads=[b_pj[i]], writes=[b_tB[i]])
        S.dma("sp", ob[r0:r0 + 128, :], tB[i][:], reads=[b_tB[i]])
        xC = p[:, 2304:2688]
        S.op("dve", lambda e, xC=xC: e.tensor_tensor(out=wk2[:, 0:384], in0=xC, in1=xC, op=ALU.mult), reads=[b_pj[i]], writes=[b_wk2])
        S.op("dve", lambda e, s1=s1: e.tensor_reduce(out=s1[:, 14:15], in_=wk2[:, 0:256], axis=AX.X, op=ALU.add), reads=[b_wk2], writes=[b_st1[i]])
        S.op("dve", lambda e, s1=s1: e.tensor_reduce(out=s1[:, 15:16], in_=wk2[:, 256:384], axis=AX.X, op=ALU.add), reads=[b_wk2], writes=[b_st1[i]])
        S.op("act", lambda e, s1=s1: e.activation(out=s1[:, 14:15], in_=s1[:, 14:15], func=AF.Sqrt, bias=epsT[:, 0:1], scale=1.0 / 256),
             reads=[b_st1[i], b_eps], writes=[b_st1[i]])
        S.op("act", lambda e, s1=s1: e.activation(out=s1[:, 15:16], in_=s1[:, 15:16], func=AF.Sqrt, bias=epsT[:, 0:1], scale=1.0 / 128),
             reads=[b_st1[i], b_eps], writes=[b_st1[i]])
        S.op("dve", lambda e, s1=s1: e.reciprocal(out=s1[:, 14:16], in_=s1[:, 14:16]), reads=[b_st1[i]], writes=[b_st1[i]])
        S.op("dve", lambda e, p=p, s1=s1: e.scalar_tensor_tensor(out=lat[:, 0:256], in0=p[:, 2304:2560], scalar=s1[:, 14:15], in1=gC_bc[:, 0:256],
                                                                op0=ALU.mult, op1=ALU.mult), reads=[b_pj[i], b_st1[i], b_gC], writes=[b_lat])
        S.op("dve", lambda e, p=p, s1=s1: e.scalar_tensor_tensor(out=lat[:, 256:384], in0=p[:, 2560:2688], scalar=s1[:, 15:16], in1=gC_bc[:, 256:384],
                                                                op0=ALU.mult, op1=ALU.mult), reads=[b_pj[i], b_st1[i], b_gC], writes=[b_lat])
        for kt in range(3):
            S.op("pe", lambda e, kt=kt: e.transpose(out=pT[:, kt * 128:(kt + 1) * 128], in_=lat[:, kt * 128:(kt + 1) * 128], identity=ident[:]),
                 reads=[b_lat, b_ident], writes=[b_pT])
        S.op("act", lambda e: e.copy(out=latT[:].rearrange("p k t -> p (k t)"), in_=pT[:, 0:384]), reads=[b_pT], writes=[b_latT])
        for cj in range(2):
            for kt in range(2):
                S.op("pe", lambda e, cj=cj, kt=kt: e.matmul(pU[cj][:, 0:384], lhsT=latT[:, kt, :], rhs=wuq[:, kt, cj * 384:(cj + 1) * 384],
                                                           start=(kt == 0), stop=(kt == 1)), reads=[b_latT, b_wuq], writes=[b_pU[cj]])
            S.op("act", lambda e, cj=cj: e.copy(out=qc[:, cj * 384:(cj + 1) * 384], in_=pU[cj][:, 0:384]), reads=[b_pU[cj]], writes=[b_qc])
        q3 = qc[:].rearrange("p (h d) -> p h d", h=8)
        o3 = tCq[i][:].rearrange("p (h d) -> p h d", h=8)
        S.op("pool", lambda e, q3=q3, o3=o3: e.tensor_copy(out=o3[:, :, 0:64], in_=q3[:, :, 0:64]), reads=[b_qc], writes=[b_tCq[i]])
        cosC = rb[i][:, 128:160]
        sinC = rb[i][:, 160:192]
        w2 = wk2[:, 0:256].rearrange("p (h d) -> p h d", h=8)
        w3 = wk3[:, 0:256].rearrange("p (h d) -> p h d", h=8)
        S.op("dve", lambda e, q3=q3, w2=w2, cosC=cosC: e.tensor_tensor(out=w2, in0=q3[:, :, 64:96], in1=cosC.unsqueeze(1).broadcast_to([128, 8, 32]), op=ALU.mult),
             reads=[b_qc, b_rb[i]], writes=[b_wk2])
        for bb in range(2):
            S.op("dve", lambda e, bb=bb, q3=q3, w3=w3, sinC=sinC: e.tensor_tensor(out=w3[:, :, bb * 16:(bb + 1) * 16], in0=q3[:, :, 64 + (1 - bb) * 16:64 + (2 - bb) * 16],
                                                                               in1=sinC[:, bb * 16:(bb + 1) * 16].unsqueeze(1).broadcast_to([128, 8, 16]), op=ALU.mult),
                 reads=[b_qc, b_rb[i]], writes=[b_wk3])
        S.op("dve", lambda e, o3=o3, w2=w2, w3=w3: e.tensor_tensor(out=o3[:, :, 64:96], in0=w2, in1=w3, op=ALU.add), reads=[b_wk2, b_wk3], writes=[b_tCq[i]])
        S.dma("sp", ocq[r0:r0 + 128, :], tCq[i][:], reads=[b_tCq[i]])
        for cj in range(2):
            S.op("pe", lambda e, cj=cj: e.matmul(pU[cj][:, 0:512], lhsT=latT[:, 2, :], rhs=wukv[:, 0, cj * 512:(cj + 1) * 512], start=True, stop=True),
                 reads=[b_latT, b_wukv], writes=[b_pU[cj]])
            S.op("act", lambda e, cj=cj, i=i: e.copy(out=tCkv[i][:, cj * 512:(cj + 1) * 512], in_=pU[cj][:, 0:512]), reads=[b_pU[cj]], writes=[b_tCkv[i]])
        S.dma("sp", ockv[r0:r0 + 128, :], tCkv[i][:], reads=[b_tCkv[i]])
        kr = p[:, 2688:2720]
        S.op("dve", lambda e, kr=kr, cosC=cosC: e.tensor_tensor(out=wk2[:, 256:288], in0=kr, in1=cosC, op=ALU.mult), reads=[b_pj[i], b_rb[i]], writes=[b_wk2])
        for bb in range(2):
            S.op("dve", lambda e, bb=bb, kr=kr, sinC=sinC: e.tensor_tensor(out=wk3[:, 256 + bb * 16:256 + (bb + 1) * 16], in0=kr[:, (1 - bb) * 16:(2 - bb) * 16],
                                                                         in1=sinC[:, bb * 16:(bb + 1) * 16], op=ALU.mult), reads=[b_pj[i], b_rb[i]], writes=[b_wk3])
        S.op("dve", lambda e, i=i: e.tensor_tensor(out=tCkr[i][:], in0=wk2[:, 256:288], in1=wk3[:, 256:288], op=ALU.add), reads=[b_wk2, b_wk3], writes=[b_tCkr[i]])
        S.dma("sp", ockr[r0:r0 + 128, :], tCkr[i][:], reads=[b_tCkr[i]])
    S.flush()
    return nc


def rope_tables(pos_row, pos_col, pos_lin):
    def ang(pos, dim):
        inv = 10000.0 ** (-np.arange(0, dim, 2, dtype=np.float64) / dim)
        a = pos.astype(np.float64)[:, None] * inv[None, :]
        return np.cos(a), np.sin(a)
    rc, rs = ang(pos_row, 32)
    cc, cs = ang(pos_col, 32)
    c1, s1 = ang(pos_lin, 32)
    cosb = np.concatenate([rc, rc, cc, cc], 1).astype(np.float32)
    sinb = np.concatenate([-rs, rs, -cs, cs], 1).astype(np.float32)
    cosc = np.concatenate([c1, c1], 1).astype(np.float32)
    sinc = np.concatenate([-s1, s1], 1).astype(np.float32)
    return cosb, sinb, cosc, sinc


SLOPES = [2.0 ** (-8.0 * (i + 1) / 4) for i in range(4)]
TABW = 3968
TABOFF = 1920


def attn_maps():
    maps = []
    for hh in range(4):
        for c in range(2):
            maps.append(dict(q=hh * 2 + c, k=hh * 2 + c, v=("A", hh), d=64, dv=128, scale=64 ** -0.5, slope=hh, oa=hh * 2 + c))
    for j in range(8):
        maps.append(dict(q=8 + j, k=8 + j // 4, v=("BC", j // 4), d=64, dv=64, scale=64 ** -0.5, slope=None, obc=j))
    for j in range(8):
        maps.append(dict(q=16 + j, k=10 + j, v=("BC", 2 + j), d=96, dv=64, scale=96 ** -0.5, slope=None, obc=8 + j))
    return maps


def build_Q1(maps=None):
    C = Ctx()
    nc, S, A = C.nc, C.S, C.A
    if maps is None:
        maps = attn_maps()
    QT = C.inp("QT", [24, 96, NT], BF16)
    KT = C.inp("KT", [18, 96, 4096], BF16)
    VA = C.inp("VA", [4, 128, 32, 128], BF16)
    VBC = C.inp("VBC", [10, 128, 32, 64], BF16)
    tabOwn = C.inp("tabOwn", [4, 128, TABW])
    tabPar = C.inp("tabPar", [4, 128, TABW])
    OA = C.out("OA", [8, 128, NT], F32)
    OBC = C.out("OBC", [16, 64, NT], BF16)

    ones = A.sb("ones", [128, 128], BF16); b_ones = Buf()
    S.op("dve", lambda e: e.memset(ones[:], 1.0), writes=[b_ones])
    qt = [A.sb(f"qt{i}", [96, NT], BF16) for i in range(2)]; b_qt = [Buf() for _ in range(2)]
    kt_ = [A.sb(f"kt{i}", [96, 4096], BF16) for i in range(2)]; b_kt = [Buf() for _ in range(2)]
    vt = [A.sb(f"vt{i}", [128, 32, 128], BF16) for i in range(2)]; b_vt = [Buf() for _ in range(2)]
    tO = A.sb("tO", [128, TABW], F32); b_tO = Buf()
    tP = A.sb("tP", [128, TABW], F32); b_tP = Buf()
    NP = 3
    e32 = [A.sb(f"e32{i}", [128, 512], F32) for i in range(NP)]; b_e32 = [Buf() for _ in range(NP)]
    pb = [A.sb(f"pb{i}", [128, 512], BF16) for i in range(NP)]; b_pb = [Buf() for _ in range(NP)]
    rz = [A.sb(f"rz{i}", [128, 512], F32) for i in range(2)]; b_rz = [Buf() for _ in range(2)]
    ot32 = [A.sb(f"ot32{i}", [128, 512], F32) for i in range(2)]; b_ot32 = [Buf() for _ in range(2)]
    ot16 = [A.sb(f"ot16{i}", [128, 512], BF16) for i in range(2)]; b_ot16 = [Buf() for _ in range(2)]
    psS = [A.ps(f"psS{i}", [128, 512], F32) for i in range(NP)]; b_psS = [Buf() for _ in range(NP)]
    psO = [A.ps(f"psO{i}", [128, 512], F32) for i in range(2)]; b_psO = [Buf() for _ in range(2)]
    psZ = [A.ps(f"psZ{i}", [128, 512], F32) for i in range(2)]; b_psZ = [Buf() for _ in range(2)]

    cur = dict(q=None, k=None, v=None, slope=None)
    slot = dict(q=-1, k=-1, v=-1)

    def ensure_loaded(m):
        if cur["q"] != m["q"]:
            slot["q"] = (slot["q"] + 1) % 2
            S.dma("sp", qt[slot["q"]][0:m["d"], :], QT[m["q"], 0:m["d"], :], writes=[b_qt[slot["q"]]])
            cur["q"] = m["q"]
        if cur["k"] != m["k"]:
            slot["k"] = (slot["k"] + 1) % 2
            S.dma("sp", kt_[slot["k"]][0:m["d"], :], KT[m["k"], 0:m["d"], :], writes=[b_kt[slot["k"]]])
            cur["k"] = m["k"]
        if cur["v"] != m["v"]:
            slot["v"] = (slot["v"] + 1) % 2
            kind, vi = m["v"]
            src = VA[vi] if kind == "A" else VBC[vi]
            dv = m["dv"]
            S.dma("pool", vt[slot["v"]][:, :, 0:dv], src, writes=[b_vt[slot["v"]]])
            cur["v"] = m["v"]
        if m["slope"] is not None and cur["slope"] != m["slope"]:
            S.dma("pool", tO[:], tabOwn[m["slope"]], writes=[b_tO])
            S.dma("pool", tP[:], tabPar[m["slope"]], writes=[b_tP])
            cur["slope"] = m["slope"]
        return slot["q"], slot["k"], slot["v"]

    cnt = [0]
    oq = [0]
    for m in maps:
        sq, sk, sv = ensure_loaded(m)
        d, dv, scale = m["d"], m["dv"], m["scale"]
        for qb in range(4):
            ob_ = oq[0] % 2
            oq[0] += 1
            qs = slice(qb * 512, (qb + 1) * 512)

            def emit_S(kt):
                j = (cnt[0] + kt) % NP
                S.op("pe", lambda e, j=j, kt=kt: e.matmul(psS[j][:], lhsT=kt_[sk][0:d, kt * 128:(kt + 1) * 128], rhs=qt[sq][0:d, qs], start=True, stop=True),
                     reads=[b_kt[sk], b_qt[sq]], writes=[b_psS[j]])

            def emit_rest(kt):
                j = (cnt[0] + kt) % NP
                if m["slope"] is not None:
                    S.op("act", lambda e, j=j: e.activation(out=e32[j][:], in_=psS[j][:], func=AF.Exp, scale=scale), reads=[b_psS[j]], writes=[b_e32[j]])
                    if kt < 16:
                        w = 512 * qb - 128 * kt + TABOFF
                        tab, tb = tO, b_tO
                    else:
                        w = 512 * qb - 128 * (kt - 16) + TABOFF
                        tab, tb = tP, b_tP
                    S.op("dve", lambda e, j=j, w=w, tab=tab: e.tensor_tensor(out=pb[j][:], in0=e32[j][:], in1=tab[:, w:w + 512], op=ALU.mult),
                         reads=[b_e32[j], tb], writes=[b_pb[j]])
                else:
                    S.op("act", lambda e, j=j: e.activation(out=pb[j][:], in_=psS[j][:], func=AF.Exp, scale=scale), reads=[b_psS[j]], writes=[b_pb[j]])
                S.op("pe", lambda e, j=j, kt=kt: e.matmul(psO[ob_][0:dv, :], lhsT=vt[sv][:, kt, 0:dv], rhs=pb[j][:], start=(kt == 0), stop=(kt == 31)),
                     reads=[b_vt[sv], b_pb[j]], writes=[b_psO[ob_]])
                S.op("pe", lambda e, j=j, kt=kt: e.matmul(psZ[ob_][0:dv, :], lhsT=ones[:, 0:dv], rhs=pb[j][:], start=(kt == 0), stop=(kt == 31)),
                     reads=[b_ones, b_pb[j]], writes=[b_psZ[ob_]])

            LOOK = 2
            for kt in range(min(LOOK, 32)):
                emit_S(kt)
            for kt in range(32):
                if kt + LOOK < 32:
                    emit_S(kt + LOOK)
                emit_rest(kt)
            cnt[0] += 32
            S.op("dve", lambda e, ob_=ob_: e.reciprocal(out=rz[ob_][0:dv, :], in_=psZ[ob_][0:dv, :]), reads=[b_psZ[ob_]], writes=[b_rz[ob_]])
            if "oa" in m:
                S.op("dve", lambda e, ob_=ob_: e.tensor_tensor(out=ot32[ob_][0:dv, :], in0=psO[ob_][0:dv, :], in1=rz[ob_][0:dv, :], op=ALU.mult),
                     reads=[b_psO[ob_], b_rz[ob_]], writes=[b_ot32[ob_]])
                S.dma("sp", OA[m["oa"], :, qs], ot32[ob_][0:dv, :], reads=[b_ot32[ob_]])
            else:
                S.op("dve", lambda e, ob_=ob_: e.tensor_tensor(out=ot16[ob_][0:dv, :], in0=psO[ob_][0:dv, :], in1=rz[ob_][0:dv, :], op=ALU.mult),
                     reads=[b_psO[ob_], b_rz[ob_]], writes=[b_ot16[ob_]])
                S.dma("sp", OBC[m["obc"], :, qs], ot16[ob_][0:dv, :], reads=[b_ot16[ob_]])
    S.flush()
    return nc


def alibi_tables(half):
    x = np.arange(TABW, dtype=np.float64)[None, :]
    ki = np.arange(128, dtype=np.float64)[:, None]
    delta = 2048.0 if half == 1 else -2048.0
    own = np.stack([np.exp(-m * np.abs(x - TABOFF - ki)) for m in SLOPES]).astype(np.float32)
    par = np.stack([np.exp(-m * np.abs(delta + x - TABOFF - ki)) for m in SLOPES]).astype(np.float32)
    return own, par


def emit_sincos(S, T, bT, ki, bki, t1, bt1, t2, bt2, out_cos, bcos, out_sin, bsin, halfpi, bhp, neg1=None, engs=("dve", "pool")):
    e0, e1 = engs
    S.op(e1, lambda e: e.tensor_copy(out=ki, in_=T), reads=[bT], writes=[bki])
    S.op(e1, lambda e: e.tensor_copy(out=t1, in_=ki), reads=[bki], writes=[bt1])
    S.op(e0, lambda e: e.tensor_tensor(out=t1, in0=T, in1=t1, op=ALU.subtract), reads=[bT, bt1], writes=[bt1])
    S.op("act", lambda e: e.activation(out=t2, in_=t1, func=AF.Sin, scale=math.pi), reads=[bt1], writes=[bt2])
    S.op("act", lambda e: e.activation(out=t1, in_=t1, func=AF.Abs), reads=[bt1], writes=[bt1])
    S.op("act", lambda e: e.activation(out=t1, in_=t1, func=AF.Sin, scale=-math.pi, bias=halfpi), reads=[bt1, bhp], writes=[bt1])
    S.op("dve", lambda e: e.scalar_tensor_tensor(out=out_sin, in0=t2, scalar=2.0, in1=t1, op0=ALU.mult, op1=ALU.mult),
         reads=[bt1, bt2], writes=[bsin])
    S.op("act", lambda e: e.activation(out=t2, in_=t1, func=AF.Square), reads=[bt1], writes=[bt2])
    S.op("act", lambda e: e.activation(out=out_cos, in_=t2, func=AF.Identity, scale=2.0, bias=neg1), reads=[bt2, bhp], writes=[bcos])


NGP = 8
SEG = 2048


def build_Q2():
    C = Ctx()
    nc, S, A = C.nc, C.S, C.A
    I32 = mybir.dt.int32
    uT = C.inp("uT", [2, 128, 4096])
    areT = C.inp("areT", [128, 2 * NGP]); aimT = C.inp("aimT", [128, 2 * NGP]); ldtT = C.inp("ldtT", [128, 2 * NGP])
    Bre = C.inp("Bre", [2, NGP, 128, 128]); Bim = C.inp("Bim", [2, NGP, 128, 128])
    Cre = C.inp("Cre", [2, NGP, 128, 128]); Cim = C.inp("Cim", [2, NGP, 128, 128])
    iotad = C.inp("iota", [128, SEG])
    identd = C.inp("identd", [128, 128], BF16)
    yT = C.out("yT", [2, 128, 4096])

    NC_ = 2 * NGP
    ident = A.sb("ident", [128, 128], BF16); b_ident = Buf()
    S.dma("sp", ident[:], identd[:, :], writes=[b_ident])
    iota = A.sb("iota", [128, SEG], F32); b_iota = Buf()
    S.dma("sp", iota[:], iotad[:, :], writes=[b_iota])
    halfpi = A.sb("halfpi", [128, 1], F32); b_hp = Buf()
    S.op("dve", lambda e: e.memset(halfpi[:], math.pi / 2), writes=[b_hp])

    def small(name, dt=F32):
        return A.sb(name, [128, NC_], dt), Buf()
    are, b_are = small("are"); aim, b_aim = small("aim"); ldt, b_ldt = small("ldt")
    S.dma("sp", are[:], areT[:, :], writes=[b_are]); S.dma("sp", aim[:], aimT[:, :], writes=[b_aim]); S.dma("sp", ldt[:], ldtT[:, :], writes=[b_ldt])
    dt_, b_dt = small("dt"); mag, b_mag = small("mag"); thn, b_thn = small("thn")
    c1, b_c1 = small("c1"); s1, b_s1 = small("s1"); ski, b_ski = small("ski", I32); sa, b_sa = small("sa"); sb_, b_sb = small("sb")
    lbre, b_lbre = small("lbre"); lbim, b_lbim = small("lbim"); rden, b_rden = small("rden"); nre, b_nre = small("nre")
    fre, b_fre = small("fre"); fim, b_fim = small("fim"); nfim, b_nfim = small("nfim"); w1, b_w1 = small("w1"); w2, b_w2 = small("w2")
    D = "dve"
    S.op(D, lambda e: e.tensor_scalar_min(out=are[:], in0=are[:], scalar1=-1e-4), reads=[b_are], writes=[b_are])
    S.op("act", lambda e: e.activation(out=dt_[:], in_=ldt[:], func=AF.Exp), reads=[b_ldt], writes=[b_dt])
    S.op(D, lambda e: e.tensor_tensor(out=w1[:], in0=are[:], in1=dt_[:], op=ALU.mult), reads=[b_are, b_dt], writes=[b_w1])
    S.op("act", lambda e: e.activation(out=mag[:], in_=w1[:], func=AF.Exp), reads=[b_w1], writes=[b_mag])
    S.op(D, lambda e: e.tensor_tensor(out=w2[:], in0=aim[:], in1=dt_[:], op=ALU.mult), reads=[b_aim, b_dt], writes=[b_w2])
    S.op(D, lambda e: e.tensor_scalar(out=thn[:], in0=w2[:], scalar1=1.0 / (2 * math.pi), scalar2=None, op0=ALU.mult), reads=[b_w2], writes=[b_thn])
    emit_sincos(S, thn[:], b_thn, ski[:], b_ski, sa[:], b_sa, sb_[:], b_sb, c1[:], b_c1, s1[:], b_s1, halfpi[:, 0:1], b_hp)
    S.op(D, lambda e: e.tensor_tensor(out=lbre[:], in0=mag[:], in1=c1[:], op=ALU.mult), reads=[b_mag, b_c1], writes=[b_lbre])
    S.op(D, lambda e: e.tensor_tensor(out=lbim[:], in0=mag[:], in1=s1[:], op=ALU.mult), reads=[b_mag, b_s1], writes=[b_lbim])
    S.op(D, lambda e: e.tensor_tensor(out=w1[:], in0=are[:], in1=are[:], op=ALU.mult), reads=[b_are], writes=[b_w1])
    S.op(D, lambda e: e.tensor_tensor(out=w2[:], in0=aim[:], in1=aim[:], op=ALU.mult), reads=[b_aim], writes=[b_w2])
    S.op(D, lambda e: e.tensor_tensor(out=w1[:], in0=w1[:], in1=w2[:], op=ALU.add), reads=[b_w1, b_w2], writes=[b_w1])
    S.op(D, lambda e: e.reciprocal(out=rden[:], in_=w1[:]), reads=[b_w1], writes=[b_rden])
    S.op(D, lambda e: e.tensor_scalar_add(out=nre[:], in0=lbre[:], scalar1=-1.0), reads=[b_lbre], writes=[b_nre])
    S.op(D, lambda e: e.tensor_tensor(out=w1[:], in0=nre[:], in1=are[:], op=ALU.mult), reads=[b_nre, b_are], writes=[b_w1])
    S.op(D, lambda e: e.tensor_tensor(out=w2[:], in0=lbim[:], in1=aim[:], op=ALU.mult), reads=[b_lbim, b_aim], writes=[b_w2])
    S.op(D, lambda e: e.tensor_tensor(out=w1[:], in0=w1[:], in1=w2[:], op=ALU.add), reads=[b_w1, b_w2], writes=[b_w1])
    S.op(D, lambda e: e.tensor_tensor(out=fre[:], in0=w1[:], in1=rden[:], op=ALU.mult), reads=[b_w1, b_rden], writes=[b_fre])
    S.op(D, lambda e: e.tensor_tensor(out=w1[:], in0=lbim[:], in1=are[:], op=ALU.mult), reads=[b_lbim, b_are], writes=[b_w1])
    S.op(D, lambda e: e.tensor_tensor(out=w2[:], in0=nre[:], in1=aim[:], op=ALU.mult), reads=[b_nre, b_aim], writes=[b_w2])
    S.op(D, lambda e: e.tensor_tensor(out=w1[:], in0=w1[:], in1=w2[:], op=ALU.subtract), reads=[b_w1, b_w2], writes=[b_w1])
    S.op(D, lambda e: e.tensor_tensor(out=fim[:], in0=w1[:], in1=rden[:], op=ALU.mult), reads=[b_w1, b_rden], writes=[b_fim])
    S.op(D, lambda e: e.tensor_scalar(out=nfim[:], in0=fim[:], scalar1=-1.0, scalar2=None, op0=ALU.mult), reads=[b_fim], writes=[b_nfim])

    uTb = A.sb("uTb", [128, 2, 4096], BF16); b_uTb = Buf()
    yacc = A.sb("yacc", [128, 2, 4096], F32); b_yacc = Buf()
    XR = A.sb("XR", [128, SEG], F32); b_XR = Buf()
    XI = A.sb("XI", [128, SEG], F32); b_XI = Buf()
    T1 = A.sb("T1", [128, SEG], F32); b_T1 = Buf()
    T2 = A.sb("T2", [128, SEG], F32); b_T2 = Buf()
    T3 = A.sb("T3", [128, SEG], F32); b_T3 = Buf()
    Ct = A.sb("Ct", [128, SEG], F32); b_Ct = Buf()
    St = A.sb("St", [128, SEG], F32); b_St = Buf()
    KI = A.sb("KI", [128, SEG], I32); b_KI = Buf()
    sre = A.sb("sre", [128, SEG], BF16); b_sre = Buf()
    sim_ = A.sb("sim", [128, SEG], BF16); b_sim = Buf()
    carry = A.sb("carry", [128, 2], F32); b_carry = Buf()
    bst = [A.sb(f"bst{i}", [128, 128], F32) for i in range(4)]; b_bst = [Buf() for _ in range(4)]
    wpre = [A.sb(f"wpre{i}", [128, 128], BF16) for i in range(2)]; b_wpre = [Buf() for _ in range(2)]
    lx = A.sb("lx", [128, 2, 128], BF16); b_lx = Buf()
    ly = A.sb("ly", [128, 2, 128], BF16); b_ly = Buf()
    pT = A.ps("pT", [128, 256], BF16); b_pT = Buf()
    pX = [A.ps(f"pX{i}", [128, 512], F32) for i in range(4)]; b_pX = [Buf() for _ in range(4)]
    pY = [A.ps(f"pY{i}", [128, 512], F32) for i in range(2)]; b_pY = [Buf() for _ in range(2)]

    for ct in range(2):
        for sg in range(2):
            st, bs = (T1, b_T1) if sg == 0 else (T2, b_T2)
            S.dma("sp", st[:], uT[ct, :, sg * SEG:(sg + 1) * SEG], writes=[bs])
            S.op("pool", lambda e, st=st, ct=ct, sg=sg: e.tensor_copy(out=uTb[:, ct, sg * SEG:(sg + 1) * SEG], in_=st[:]), reads=[bs], writes=[b_uTb])

    ycnt = [0]
    for gp in range(NGP):
        ct = gp // 4
        for dr in range(2):
            col = dr * NGP + gp
            cs = slice(col, col + 1)
            S.dma("sp", bst[0][:], Bre[dr, gp], writes=[b_bst[0]])
            S.dma("sp", bst[1][:], Bim[dr, gp], writes=[b_bst[1]])
            S.dma("sp", bst[2][:], Cre[dr, gp], writes=[b_bst[2]])
            S.dma("sp", bst[3][:], Cim[dr, gp], writes=[b_bst[3]])
            S.op("pool", lambda e, cs=cs: e.tensor_scalar(out=T3[:, 0:128], in0=bst[0][:], scalar1=fre[:, cs], scalar2=None, op0=ALU.mult),
                 reads=[b_bst[0], b_fre], writes=[b_T3])
            S.op(D, lambda e, cs=cs: e.scalar_tensor_tensor(out=wpre[0][:], in0=bst[1][:], scalar=nfim[:, cs], in1=T3[:, 0:128], op0=ALU.mult, op1=ALU.add),
                 reads=[b_bst[1], b_nfim, b_T3], writes=[b_wpre[0]])
            S.op("pool", lambda e, cs=cs: e.tensor_scalar(out=T3[:, 128:256], in0=bst[0][:], scalar1=fim[:, cs], scalar2=None, op0=ALU.mult),
                 reads=[b_bst[0], b_fim], writes=[b_T3])
            S.op(D, lambda e, cs=cs: e.scalar_tensor_tensor(out=wpre[1][:], in0=bst[1][:], scalar=fre[:, cs], in1=T3[:, 128:256], op0=ALU.mult, op1=ALU.add),
                 reads=[b_bst[1], b_fre, b_T3], writes=[b_wpre[1]])
            for ri in range(2):
                S.op("pe", lambda e, ri=ri: e.transpose(out=pT[:, ri * 128:(ri + 1) * 128], in_=wpre[ri][:], identity=ident[:]),
                     reads=[b_wpre[ri], b_ident], writes=[b_pT])
            S.op("act", lambda e: e.copy(out=lx[:].rearrange("p a b -> p (a b)"), in_=pT[:]), reads=[b_pT], writes=[b_lx])
            S.op("act", lambda e: e.copy(out=ly[:, 0, :], in_=bst[2][:]), reads=[b_bst[2]], writes=[b_ly])
            S.op("act", lambda e: e.mul(out=ly[:, 1, :], in_=bst[3][:], mul=-1.0), reads=[b_bst[3]], writes=[b_ly])
            segs = [0, 1] if dr == 0 else [1, 0]
            for si, sg in enumerate(segs):
                t0 = sg * SEG
                if dr == 0:
                    io, off = iota[:, :], float(t0)
                else:
                    io, off = iota[:, ::-1], float(4095 - t0 - (SEG - 1))
                S.op(D, lambda e, io=io, off=off, cs=cs: e.tensor_scalar(out=T1[:], in0=io, scalar1=off, scalar2=thn[:, cs], op0=ALU.add, op1=ALU.mult),
                     reads=[b_iota, b_thn], writes=[b_T1])
                emit_sincos(S, T1[:], b_T1, KI[:], b_KI, T2[:], b_T2, T3[:], b_T3, Ct[:], b_Ct, St[:], b_St, halfpi[:, 0:1], b_hp)
                for blk in range(4):
                    ts = slice(t0 + blk * 512, t0 + (blk + 1) * 512)
                    for ri in range(2):
                        pi = (blk % 2) * 2 + ri
                        S.op("pe", lambda e, ri=ri, pi=pi, ts=ts: e.matmul(pX[pi][:], lhsT=lx[:, ri, :], rhs=uTb[:, ct, ts], start=True, stop=True),
                             reads=[b_lx, b_uTb], writes=[b_pX[pi]])
                        dst, bd = (XR, b_XR) if ri == 0 else (XI, b_XI)
                        S.op("act", lambda e, dst=dst, pi=pi, blk=blk: e.copy(out=dst[:, blk * 512:(blk + 1) * 512], in_=pX[pi][:]), reads=[b_pX[pi]], writes=[bd])
                S.op(D, lambda e: e.tensor_tensor(out=T1[:], in0=Ct[:], in1=XR[:], op=ALU.mult), reads=[b_Ct, b_XR], writes=[b_T1])
                S.op("pool", lambda e: e.tensor_tensor(out=T2[:], in0=St[:], in1=XI[:], op=ALU.mult), reads=[b_St, b_XI], writes=[b_T2])
                S.op(D, lambda e: e.tensor_tensor(out=T1[:], in0=T1[:], in1=T2[:], op=ALU.add), reads=[b_T1, b_T2], writes=[b_T1])
                S.op("pool", lambda e: e.tensor_tensor(out=T2[:], in0=Ct[:], in1=XI[:], op=ALU.mult), reads=[b_Ct, b_XI], writes=[b_T2])
                S.op("pool", lambda e: e.tensor_tensor(out=T3[:], in0=St[:], in1=XR[:], op=ALU.mult), reads=[b_St, b_XR], writes=[b_T3])
                S.op("pool", lambda e: e.tensor_tensor(out=T2[:], in0=T2[:], in1=T3[:], op=ALU.subtract), reads=[b_T2, b_T3], writes=[b_T2])
                rmul = mag[:, cs].broadcast_to([128, SEG])
                for zi, (src, bsrc, dst, bdst) in enumerate(((T1, b_T1, XR, b_XR), (T2, b_T2, XI, b_XI))):
                    init = 0.0 if si == 0 else carry[:, zi:zi + 1]
                    if dr == 0:
                        o_ap, d_ap = dst[:, :], src[:, :]
                    else:
                        o_ap, d_ap = dst[:, ::-1], src[:, ::-1]
                    S.op(D, lambda e, o_ap=o_ap, d_ap=d_ap, init=init, rmul=rmul: e.tensor_tensor_scan(out=o_ap, data0=rmul, data1=d_ap, initial=init,
                                                                                                       op0=ALU.mult, op1=ALU.add),
                         reads=[bsrc, b_mag, b_carry], writes=[bdst])
                if si == 0:
                    ccol = SEG - 1 if dr == 0 else 0
                    S.op("pool", lambda e, ccol=ccol: e.tensor_copy(out=carry[:, 0:1], in_=XR[:, ccol:ccol + 1]), reads=[b_XR], writes=[b_carry])
                    S.op("pool", lambda e, ccol=ccol: e.tensor_copy(out=carry[:, 1:2], in_=XI[:, ccol:ccol + 1]), reads=[b_XI], writes=[b_carry])
                S.op(D, lambda e: e.tensor_tensor(out=T1[:], in0=Ct[:], in1=XR[:], op=ALU.mult), reads=[b_Ct, b_XR], writes=[b_T1])
                S.op("pool", lambda e: e.tensor_tensor(out=T3[:], in0=St[:], in1=XI[:], op=ALU.mult), reads=[b_St, b_XI], writes=[b_T3])
                S.op(D, lambda e: e.tensor_tensor(out=sre[:], in0=T1[:], in1=T3[:], op=ALU.subtract), reads=[b_T1, b_T3], writes=[b_sre])
                S.op("pool", lambda e: e.tensor_tensor(out=T2[:], in0=St[:], in1=XR[:], op=ALU.mult), reads=[b_St, b_XR], writes=[b_T2])
                S.op("pool", lambda e: e.tensor_tensor(out=T3[:], in0=Ct[:], in1=XI[:], op=ALU.mult), reads=[b_Ct, b_XI], writes=[b_T3])
                S.op("pool", lambda e: e.tensor_tensor(out=sim_[:], in0=T2[:], in1=T3[:], op=ALU.add), reads=[b_T2, b_T3], writes=[b_sim])
                first = (gp % 4 == 0 and dr == 0)
                for blk in range(4):
                    pi = ycnt[0] % 2
                    ycnt[0] += 1
                    bs_ = slice(blk * 512, (blk + 1) * 512)
                    ts = slice(t0 + blk * 512, t0 + (blk + 1) * 512)
                    S.op("pe", lambda e, pi=pi, bs_=bs_: e.matmul(pY[pi][:], lhsT=ly[:, 0, :], rhs=sre[:, bs_], start=True, stop=False),
                         reads=[b_ly, b_sre], writes=[b_pY[pi]])
                    S.op("pe", lambda e, pi=pi, bs_=bs_: e.matmul(pY[pi][:], lhsT=ly[:, 1, :], rhs=sim_[:, bs_], start=False, stop=True),
                         reads=[b_ly, b_sim], writes=[b_pY[pi]])
                    if first:
                        S.op("act", lambda e, pi=pi, ts=ts: e.copy(out=yacc[:, ct, ts], in_=pY[pi][:]), reads=[b_pY[pi]], writes=[b_yacc])
                    else:
                        S.op(D, lambda e, pi=pi, ts=ts: e.tensor_tensor(out=yacc[:, ct, ts], in0=pY[pi][:], in1=yacc[:, ct, ts], op=ALU.add),
                             reads=[b_pY[pi], b_yacc], writes=[b_yacc])
    for ct in range(2):
        S.dma("sp", yT[ct], yacc[:, ct, :], reads=[b_yacc])
    S.flush()
    return nc


def s5_host_layout(inputs, li, half):
    g0 = 16 * half
    def colmat(a):
        out = np.zeros((128, 2 * NGP), np.float32)
        for dr in range(2):
            for gp in range(NGP):
                for g2 in range(2):
                    out[g2 * 64:(g2 + 1) * 64, dr * NGP + gp] = a[dr, g0 + 2 * gp + g2]
        return out
    are = colmat(inputs["s5_a_re"][li]); aim = colmat(inputs["s5_a_im"][li])
    ldt = colmat(np.repeat(inputs["s5_log_dt"][li][:, :, None], 64, axis=2))
    Bre = np.zeros((2, NGP, 128, 128), np.float32); Bim = np.zeros_like(Bre); Cre = np.zeros_like(Bre); Cim = np.zeros_like(Bre)
    for dr in range(2):
        for gp in range(NGP):
            for g2 in range(2):
                g = g0 + 2 * gp + g2
                c0 = 32 * (gp % 4) + 16 * g2
                Bre[dr, gp, g2 * 64:(g2 + 1) * 64, c0:c0 + 16] = inputs["s5_b_re"][li, dr, g]
                Bim[dr, gp, g2 * 64:(g2 + 1) * 64, c0:c0 + 16] = inputs["s5_b_im"][li, dr, g]
                Cre[dr, gp, g2 * 64:(g2 + 1) * 64, c0:c0 + 16] = inputs["s5_c_re"][li, dr, g].T
                Cim[dr, gp, g2 * 64:(g2 + 1) * 64, c0:c0 + 16] = inputs["s5_c_im"][li, dr, g].T
    return dict(areT=are, aimT=aim, ldtT=ldt, Bre=Bre, Bim=Bim, Cre=Cre, Cim=Cim)


def rms_rstd(S, src, b_src, junk, b_junk, st, b_st, epsT, b_eps, width):
    S.op("act", lambda e: e.activation(out=junk, in_=src, func=AF.Square, accum_out=st[:, 0:1]), reads=[b_src], writes=[b_junk, b_st])
    S.op("act", lambda e: e.activation(out=st[:, 1:2], in_=st[:, 0:1], func=AF.Sqrt, bias=epsT, scale=1.0 / width), reads=[b_st, b_eps], writes=[b_st])
    S.op("dve", lambda e: e.reciprocal(out=st[:, 2:3], in_=st[:, 1:2]), reads=[b_st], writes=[b_st])


def build_Q3(lam_init):
    C = Ctx()
    nc, S, A = C.nc, C.S, C.A
    h = C.inp("h", [NT, 1024]); g_pre = C.inp("g_pre", [1024]); g_post = C.inp("g_post", [1024])
    OAtok = C.inp("OAtok", [NT, 8, 128]); lamv = C.inp("lamv", [256]); subln = C.inp("subln", [128])
    ybT = C.inp("ybT", [4, 128, NT], BF16); ycT = C.inp("ycT", [4, 128, NT], BF16)
    ysT = C.inp("ysT", [4, 128, NT]); uTo = C.inp("uTo", [4, 128, NT]); dcol = C.inp("dcol", [128, 4])
    w_glu = C.inp("w_glu", [512, 1024]); w_gate = C.inp("w_gate", [1024, 4096]); w_br = C.inp("w_br", [4, 512, 1024]); w_out = C.inp("w_out", [1024, 1024])
    identd = C.inp("identd", [128, 128], BF16)
    h1 = C.out("h1", [NT, 1024])

    HT = 1024
    HTI = HT // 128
    D = "dve"
    ident = A.sb("ident", [128, 128], BF16); b_ident = Buf()
    S.dma("sp", ident[:], identd[:, :], writes=[b_ident])
    epsT = A.sb("epsT", [128, 1], F32); b_eps = Buf()
    S.op(D, lambda e: e.memset(epsT[:], EPS), writes=[b_eps])
    g_bc = A.sb("g_bc", [128, 1024], F32); b_g = Buf()
    S.dma("pool", g_bc[:], g_pre.partition_broadcast(128), writes=[b_g])
    gp_bc = A.sb("gp_bc", [128, 1024], F32); b_gp = Buf()
    S.dma("pool", gp_bc[:], g_post.partition_broadcast(128), writes=[b_gp])
    lam_bc = A.sb("lam_bc", [128, 256], F32); b_lam = Buf()
    S.dma("pool", lam_bc[:], lamv.partition_broadcast(128), writes=[b_lam])
    sub_bc = A.sb("sub_bc", [128, 128], F32); b_sub = Buf()
    S.dma("pool", sub_bc[:], subln.partition_broadcast(128), writes=[b_sub])
    S.op("act", lambda e: e.mul(out=sub_bc[:], in_=sub_bc[:], mul=1.0 - lam_init), reads=[b_sub], writes=[b_sub])
    dc = A.sb("dc", [128, 4], F32); b_dc = Buf()
    S.dma("sp", dc[:], dcol[:, :], writes=[b_dc])
    lt = A.sb("lt", [128, 128], F32); b_lt = Buf()
    ls = A.sb("ls", [128, 8], F32); b_ls = Buf()
    S.op(D, lambda e: e.tensor_tensor(out=lt[:, 0:64], in0=lam_bc[:, 0:64], in1=lam_bc[:, 64:128], op=ALU.mult), reads=[b_lam], writes=[b_lt])
    S.op(D, lambda e: e.tensor_tensor(out=lt[:, 64:128], in0=lam_bc[:, 128:192], in1=lam_bc[:, 192:256], op=ALU.mult), reads=[b_lam], writes=[b_lt])
    S.op(D, lambda e: e.tensor_reduce(out=ls[:, 0:2], in_=lt[:].rearrange("p (a d) -> p a d", a=2), axis=AX.X, op=ALU.add), reads=[b_lt], writes=[b_ls])
    S.op("act", lambda e: e.activation(out=ls[:, 2:4], in_=ls[:, 0:2], func=AF.Exp), reads=[b_ls], writes=[b_ls])
    S.op(D, lambda e: e.tensor_tensor(out=ls[:, 4:5], in0=ls[:, 3:4], in1=ls[:, 2:3], op=ALU.subtract), reads=[b_ls], writes=[b_ls])
    S.op(D, lambda e: e.tensor_scalar_add(out=ls[:, 5:6], in0=ls[:, 4:5], scalar1=-lam_init), reads=[b_ls], writes=[b_ls])
    nlam = ls[:, 5:6]

    stg = [A.sb(f"stg{i}", [128, 1024], F32) for i in range(2)]; b_stg = [Buf() for _ in range(2)]
    sidx = [0]

    def load_w(dst, b_dst, src, ktiles, cols):
        for kt in range(ktiles):
            i = sidx[0] % 2
            sidx[0] += 1
            S.dma("sp", stg[i][:, 0:cols], src[kt * 128:(kt + 1) * 128, :], writes=[b_stg[i]])
            eng = "pool" if i else "act"
            if eng == "act":
                S.op("act", lambda e, i=i, kt=kt: e.copy(out=dst[:, kt, 0:cols], in_=stg[i][:, 0:cols]), reads=[b_stg[i]], writes=[b_dst])
            else:
                S.op("pool", lambda e, i=i, kt=kt: e.tensor_copy(out=dst[:, kt, 0:cols], in_=stg[i][:, 0:cols]), reads=[b_stg[i]], writes=[b_dst])

    wg = A.sb("wg", [128, 8, 1024], BF16); b_wg = Buf()
    wb = A.sb("wb", [128, 4, 1024], BF16); b_wb = Buf()
    mixed = A.sb("mixed", [128, HTI, 1024], F32); b_mixed = [Buf() for _ in range(HTI)]
    xnT = A.sb("xnT", [128, 8, HT], BF16); b_xnT = Buf()
    yaT = A.sb("yaT", [128, 4, HT], BF16); b_yaT = Buf()
    gT = A.sb("gT", [128, 4, HT], BF16); b_gT = Buf()
    sgT = A.sb("sgT", [128, 4, HT], BF16); b_sgT = Buf()
    ydT = A.sb("ydT", [128, 4, HT], BF16); b_ydT = Buf()
    ht = [A.sb(f"ht{i}", [128, 1024], F32) for i in range(2)]; b_ht = [Buf() for _ in range(2)]
    junk = A.sb("junk", [128, 1024], F32); b_junk = Buf()
    st = [A.sb(f"st{i}", [128, 8], F32) for i in range(2)]; b_st = [Buf() for _ in range(2)]
    xn = [A.sb(f"xn{i}", [128, 1024], BF16) for i in range(2)]; b_xn = [Buf() for _ in range(2)]
    oat = [A.sb(f"oat{i}", [128, 8, 128], F32) for i in range(2)]; b_oat = [Buf() for _ in range(2)]
    cmb = A.sb("cmb", [128, 512], F32); b_cmb = Buf()
    cm2 = A.sb("cm2", [128, 512], F32); b_cm2 = Buf()
    yab = A.sb("yab", [128, 512], BF16); b_yab = Buf()
    e1 = A.sb("e1", [128, HT], F32); b_e1 = Buf()
    e2 = A.sb("e2", [128, HT], F32); b_e2 = Buf()
    e3 = A.sb("e3", [128, HT], F32); b_e3 = Buf()
    sg = [A.sb(f"sg{i}", [128, 512], F32) for i in range(2)]; b_sg = [Buf() for _ in range(2)]
    tm = [A.sb(f"tm{i}", [128, 512], F32) for i in range(2)]; b_tm = [Buf() for _ in range(2)]
    osb = A.sb("osb", [128, 1024], F32); b_osb = Buf()
    mT = A.sb("mT", [128, 8, 128], BF16); b_mT = Buf()
    pT = A.ps("pT", [128, 1024], BF16); b_pT = Buf()
    pA = [A.ps(f"pA{i}", [128, 512], F32) for i in range(2)]; b_pA = [Buf() for _ in range(2)]
    pB = [A.ps(f"pB{i}", [128, 512], F32) for i in range(2)]; b_pB = [Buf() for _ in range(2)]
    pG = [A.ps(f"pG{i}", [128, 512], F32) for i in range(2)]; b_pG = [Buf() for _ in range(2)]

    for th in range(NT // HT):
        tb0 = th * HT
        for t in range(HTI):
            i = t % 2
            r0 = tb0 + t * 128
            S.dma("sp", ht[i][:], h[r0:r0 + 128, :], writes=[b_ht[i]])
            S.dma("sp", oat[i][:], OAtok[r0:r0 + 128], writes=[b_oat[i]])
            rms_rstd(S, ht[i][:], b_ht[i], junk[:], b_junk, st[i], b_st[i], epsT[:, 0:1], b_eps, 1024)
            S.op(D, lambda e, i=i: e.scalar_tensor_tensor(out=xn[i][:], in0=ht[i][:], scalar=st[i][:, 2:3], in1=g_bc[:], op0=ALU.mult, op1=ALU.mult),
                 reads=[b_ht[i], b_st[i], b_g], writes=[b_xn[i]])
            for kt in range(8):
                S.op("pe", lambda e, i=i, kt=kt: e.transpose(out=pT[:, kt * 128:(kt + 1) * 128], in_=xn[i][:, kt * 128:(kt + 1) * 128], identity=ident[:]),
                     reads=[b_xn[i], b_ident], writes=[b_pT])
            S.op("act", lambda e, t=t: e.copy(out=xnT[:, :, t * 128:(t + 1) * 128], in_=pT[:].rearrange("p (k t) -> p k t", k=8)), reads=[b_pT], writes=[b_xnT])
            o4 = oat[i][:].rearrange("p (h c) d -> p h c d", c=2)
            c3 = cmb[:].rearrange("p (h d) -> p h d", h=4)
            S.op(D, lambda e, o4=o4, c3=c3: e.scalar_tensor_tensor(out=c3, in0=o4[:, :, 1, :], scalar=nlam, in1=o4[:, :, 0, :], op0=ALU.mult, op1=ALU.add),
                 reads=[b_oat[i], b_ls], writes=[b_cmb])
            S.op("pool", lambda e: e.tensor_tensor(out=cm2[:], in0=cmb[:], in1=cmb[:], op=ALU.mult), reads=[b_cmb], writes=[b_cm2])
            S.op(D, lambda e, i=i: e.tensor_reduce(out=st[i][:, 4:8], in_=cm2[:].rearrange("p (h d) -> p h d", h=4), axis=AX.X, op=ALU.add),
                 reads=[b_cm2], writes=[b_st[i]])
            S.op("act", lambda e, i=i: e.activation(out=st[i][:, 4:8], in_=st[i][:, 4:8], func=AF.Sqrt, bias=epsT[:, 0:1], scale=1.0 / 128),
                 reads=[b_st[i], b_eps], writes=[b_st[i]])
            S.op(D, lambda e, i=i: e.reciprocal(out=st[i][:, 4:8], in_=st[i][:, 4:8]), reads=[b_st[i]], writes=[b_st[i]])
            S.op(D, lambda e, i=i, c3=c3: e.tensor_tensor(out=cm2[:].rearrange("p (h d) -> p h d", h=4), in0=c3,
                                                         in1=st[i][:, 4:8].unsqueeze(2).broadcast_to([128, 4, 128]), op=ALU.mult),
                 reads=[b_cmb, b_st[i]], writes=[b_cm2])
            S.op("pool", lambda e: e.tensor_tensor(out=yab[:].rearrange("p (h d) -> p h d", h=4), in0=cm2[:].rearrange("p (h d) -> p h d", h=4),
                                                  in1=sub_bc[:].unsqueeze(1).broadcast_to([128, 4, 128]), op=ALU.mult),
                 reads=[b_cm2, b_sub], writes=[b_yab])
            for kt in range(4):
                S.op("pe", lambda e, kt=kt: e.transpose(out=pT[:, kt * 128:(kt + 1) * 128], in_=yab[:, kt * 128:(kt + 1) * 128], identity=ident[:]),
                     reads=[b_yab, b_ident], writes=[b_pT])
            S.op("act", lambda e, t=t: e.copy(out=yaT[:, :, t * 128:(t + 1) * 128], in_=pT[:, 0:512].rearrange("p (k t) -> p k t", k=4)), reads=[b_pT], writes=[b_yaT])
        load_w(wb, b_wb, w_glu, 4, 1024)
        for ct in range(4):
            S.dma("sp", e1[:], ysT[ct, :, tb0:tb0 + HT], writes=[b_e1])
            S.dma("sp", e2[:], uTo[ct, :, tb0:tb0 + HT], writes=[b_e2])
            S.op(D, lambda e, ct=ct: e.scalar_tensor_tensor(out=e1[:], in0=e2[:], scalar=dc[:, ct:ct + 1], in1=e1[:], op0=ALU.mult, op1=ALU.add),
                 reads=[b_e1, b_e2, b_dc], writes=[b_e1])
            S.op("pool", lambda e: e.tensor_tensor(out=e2[:], in0=e1[:], in1=e1[:], op=ALU.mult), reads=[b_e1], writes=[b_e2])
            S.op("pool", lambda e: e.tensor_scalar(out=e2[:], in0=e2[:], scalar1=0.044715, scalar2=1.0, op0=ALU.mult, op1=ALU.add), reads=[b_e2], writes=[b_e2])
            S.op(D, lambda e: e.tensor_tensor(out=e2[:], in0=e2[:], in1=e1[:], op=ALU.mult), reads=[b_e1, b_e2], writes=[b_e2])
            S.op("act", lambda e: e.activation(out=e3[:], in_=e2[:], func=AF.Sigmoid, scale=2.0 * math.sqrt(2.0 / math.pi)), reads=[b_e2], writes=[b_e3])
            S.op(D, lambda e, ct=ct: e.tensor_tensor(out=gT[:, ct, :], in0=e1[:], in1=e3[:], op=ALU.mult), reads=[b_e1, b_e3], writes=[b_gT])
        gcnt = 0
        for och in (4, 5, 6, 7, 0, 1, 2, 3):
            for tb in range(HT // 512):
                pi = gcnt % 2
                gcnt += 1
                ts = slice(tb * 512, (tb + 1) * 512)
                for kt in range(4):
                    S.op("pe", lambda e, pi=pi, kt=kt, och=och, ts=ts: e.matmul(pG[pi][:], lhsT=wb[:, kt, och * 128:(och + 1) * 128], rhs=gT[:, kt, ts],
                                                                                start=(kt == 0), stop=(kt == 3)), reads=[b_wb, b_gT], writes=[b_pG[pi]])
                if och >= 4:
                    S.op("act", lambda e, pi=pi, och=och, ts=ts: e.activation(out=sgT[:, och - 4, ts], in_=pG[pi][:], func=AF.Sigmoid), reads=[b_pG[pi]], writes=[b_sgT])
                else:
                    S.op(D, lambda e, pi=pi, och=och, ts=ts: e.tensor_tensor(out=ydT[:, och, ts], in0=pG[pi][:], in1=sgT[:, och, ts], op=ALU.mult),
                         reads=[b_pG[pi], b_sgT], writes=[b_ydT])
        for b in range(4):
            load_w(wg, b_wg, w_gate[:, b * 1024:(b + 1) * 1024], 8, 1024)
            load_w(wb, b_wb, w_br[b], 4, 1024)
            if b == 0:
                yT_, b_yT = yaT, b_yaT
            elif b == 3:
                yT_, b_yT = ydT, b_ydT
            else:
                src = ybT if b == 1 else ycT
                for kt in range(4):
                    S.dma("sp", gT[:, kt, :], src[kt, :, tb0:tb0 + HT], writes=[b_gT])
                yT_, b_yT = gT, b_gT
            for t in range(HTI):
                tsl = slice(t * 128, (t + 1) * 128)
                for hc in range(2):
                    cs = slice(hc * 512, (hc + 1) * 512)
                    for kt in range(4):
                        S.op("pe", lambda e, kt=kt, hc=hc, tsl=tsl, cs=cs, yT_=yT_: e.matmul(pA[hc][:], lhsT=yT_[:, kt, tsl], rhs=wb[:, kt, cs],
                                                                                             start=(kt == 0), stop=(kt == 3)), reads=[b_yT, b_wb], writes=[b_pA[hc]])
                    for kt in range(8):
                        S.op("pe", lambda e, kt=kt, hc=hc, tsl=tsl, cs=cs: e.matmul(pB[hc][:], lhsT=xnT[:, kt, tsl], rhs=wg[:, kt, cs],
                                                                                    start=(kt == 0), stop=(kt == 7)), reads=[b_xnT, b_wg], writes=[b_pB[hc]])
                    S.op("act", lambda e, hc=hc: e.activation(out=sg[hc][:], in_=pB[hc][:], func=AF.Sigmoid), reads=[b_pB[hc]], writes=[b_sg[hc]])
                    if b == 0:
                        S.op(D, lambda e, hc=hc, t=t, cs=cs: e.tensor_tensor(out=mixed[:, t, cs], in0=pA[hc][:], in1=sg[hc][:], op=ALU.mult),
                             reads=[b_pA[hc], b_sg[hc]], writes=[b_mixed[t]])
                    else:
                        S.op(D, lambda e, hc=hc: e.tensor_tensor(out=tm[hc][:], in0=pA[hc][:], in1=sg[hc][:], op=ALU.mult),
                             reads=[b_pA[hc], b_sg[hc]], writes=[b_tm[hc]])
                        S.op("pool", lambda e, hc=hc, t=t, cs=cs: e.tensor_tensor(out=mixed[:, t, cs], in0=mixed[:, t, cs], in1=tm[hc][:], op=ALU.add),
                             reads=[b_tm[hc], b_mixed[t]], writes=[b_mixed[t]])
        load_w(wg, b_wg, w_out, 8, 1024)
        for t in range(HTI):
            i = t % 2
            r0 = tb0 + t * 128
            S.dma("sp", ht[i][:], h[r0:r0 + 128, :], writes=[b_ht[i]])
            S.op("act", lambda e, i=i, t=t: e.copy(out=xn[i][:], in_=mixed[:, t, :]), reads=[b_mixed[t]], writes=[b_xn[i]])
            for kt in range(8):
                S.op("pe", lambda e, i=i, kt=kt: e.transpose(out=pT[:, kt * 128:(kt + 1) * 128], in_=xn[i][:, kt * 128:(kt + 1) * 128], identity=ident[:]),
                     reads=[b_xn[i], b_ident], writes=[b_pT])
            S.op("act", lambda e: e.copy(out=mT[:].rearrange("p k t -> p (k t)"), in_=pT[:]), reads=[b_pT], writes=[b_mT])
            for hc in range(2):
                cs = slice(hc * 512, (hc + 1) * 512)
                for kt in range(8):
                    S.op("pe", lambda e, kt=kt, hc=hc, cs=cs: e.matmul(pA[hc][:], lhsT=mT[:, kt, :], rhs=wg[:, kt, cs], start=(kt == 0), stop=(kt == 7)),
                         reads=[b_mT, b_wg], writes=[b_pA[hc]])
                if hc == 0:
                    S.op("act", lambda e, hc=hc, cs=cs: e.copy(out=osb[:, cs], in_=pA[hc][:]), reads=[b_pA[hc]], writes=[b_osb])
                else:
                    S.op(D, lambda e, hc=hc, cs=cs: e.tensor_copy(out=osb[:, cs], in_=pA[hc][:]), reads=[b_pA[hc]], writes=[b_osb])
            rms_rstd(S, osb[:], b_osb, junk[:], b_junk, st[i], b_st[i], epsT[:, 0:1], b_eps, 1024)
            S.op(D, lambda e, i=i: e.scalar_tensor_tensor(out=osb[:], in0=osb[:], scalar=st[i][:, 2:3], in1=gp_bc[:], op0=ALU.mult, op1=ALU.mult),
                 reads=[b_osb, b_st[i], b_gp], writes=[b_osb])
            S.op("pool", lambda e, i=i: e.tensor_tensor(out=ht[i][:], in0=ht[i][:], in1=osb[:], op=ALU.add), reads=[b_ht[i], b_osb], writes=[b_ht[i]])
            S.dma("sp", h1[r0:r0 + 128, :], ht[i][:], reads=[b_ht[i]])
    S.flush()
    return nc


def build_Q4():
    C = Ctx()
    nc, S, A = C.nc, C.S, C.A
    h1 = C.inp("h1", [NT, 1024]); g_pre = C.inp("g_pre", [1024]); g_post = C.inp("g_post", [1024])
    w_fi = C.inp("w_fi", [1024, 4096]); w_fo = C.inp("w_fo", [4096, 1024])
    identd = C.inp("identd", [128, 128], BF16)
    h2 = C.out("h2", [NT, 1024])
    D = "dve"
    ident = A.sb("ident", [128, 128], BF16); b_ident = Buf()
    S.dma("sp", ident[:], identd[:, :], writes=[b_ident])
    epsT = A.sb("epsT", [128, 1], F32); b_eps = Buf()
    S.op(D, lambda e: e.memset(epsT[:], EPS), writes=[b_eps])
    g_bc = A.sb("g_bc", [128, 1024], F32); b_g = Buf()
    S.dma("pool", g_bc[:], g_pre.partition_broadcast(128), writes=[b_g])
    gp_bc = A.sb("gp_bc", [128, 1024], F32); b_gp = Buf()
    S.dma("pool", gp_bc[:], g_post.partition_broadcast(128), writes=[b_gp])
    stg = [A.sb(f"stg{i}", [128, 1024], F32) for i in range(2)]; b_stg = [Buf() for _ in range(2)]
    sidx = [0]

    def load_w(dst, b_dst, src, ktiles, cols):
        for kt in range(ktiles):
            i = sidx[0] % 2
            sidx[0] += 1
            S.dma("sp", stg[i][:, 0:cols], src[kt * 128:(kt + 1) * 128, :], writes=[b_stg[i]])
            if i == 0:
                S.op("pool", lambda e, i=i, kt=kt: e.tensor_copy(out=dst[:, kt, 0:cols], in_=stg[i][:, 0:cols]), reads=[b_stg[i]], writes=[b_dst])
            else:
                S.op(D, lambda e, i=i, kt=kt: e.tensor_copy(out=dst[:, kt, 0:cols], in_=stg[i][:, 0:cols]), reads=[b_stg[i]], writes=[b_dst])

    fnT = A.sb("fnT", [128, 8, NT], BF16); b_fnT = Buf()
    facc = A.sb("facc", [128, NTI, 1024], F32); b_facc = [Buf() for _ in range(NTI)]
    hidT = A.sb("hidT", [128, 4, NT], BF16); b_hidT = Buf()
    wfi = A.sb("wfi", [128, 8, 512], BF16); b_wfi = Buf()
    wfo = A.sb("wfo", [128, 4, 1024], BF16); b_wfo = Buf()
    ht = [A.sb(f"ht{i}", [128, 1024], F32) for i in range(2)]; b_ht = [Buf() for _ in range(2)]
    junk = A.sb("junk", [128, 1024], F32); b_junk = Buf()
    st = [A.sb(f"st{i}", [128, 8], F32) for i in range(2)]; b_st = [Buf() for _ in range(2)]
    xn = [A.sb(f"xn{i}", [128, 1024], BF16) for i in range(2)]; b_xn = [Buf() for _ in range(2)]
    r32 = [A.sb(f"r32{i}", [128, 512], F32) for i in range(2)]; b_r32 = [Buf() for _ in range(2)]
    pT = A.ps("pT", [128, 1024], BF16); b_pT = Buf()
    pH = [A.ps(f"pH{i}", [128, 512], F32) for i in range(2)]; b_pH = [Buf() for _ in range(2)]
    pF = [A.ps(f"pF{i}", [128, 512], F32) for i in range(4)]; b_pF = [Buf() for _ in range(4)]

    for t in range(NTI):
        i = t % 2
        r0 = t * 128
        S.dma("sp", ht[i][:], h1[r0:r0 + 128, :], writes=[b_ht[i]])
        rms_rstd(S, ht[i][:], b_ht[i], junk[:], b_junk, st[i], b_st[i], epsT[:, 0:1], b_eps, 1024)
        S.op(D, lambda e, i=i: e.scalar_tensor_tensor(out=xn[i][:], in0=ht[i][:], scalar=st[i][:, 2:3], in1=g_bc[:], op0=ALU.mult, op1=ALU.mult),
             reads=[b_ht[i], b_st[i], b_g], writes=[b_xn[i]])
        for kt in range(8):
            S.op("pe", lambda e, i=i, kt=kt: e.transpose(out=pT[:, kt * 128:(kt + 1) * 128], in_=xn[i][:, kt * 128:(kt + 1) * 128], identity=ident[:]),
                 reads=[b_xn[i], b_ident], writes=[b_pT])
        S.op("act", lambda e, t=t: e.copy(out=fnT[:, :, t * 128:(t + 1) * 128], in_=pT[:].rearrange("p (k t) -> p k t", k=8)), reads=[b_pT], writes=[b_fnT])
    hcnt = 0
    fcnt = 0
    for c in range(8):
        load_w(wfi, b_wfi, w_fi[:, c * 512:(c + 1) * 512], 8, 512)
        load_w(wfo, b_wfo, w_fo[c * 512:(c + 1) * 512, :], 4, 1024)
        for j in range(4):
            for tb in range(4):
                pi = hcnt % 2
                hcnt += 1
                ts = slice(tb * 512, (tb + 1) * 512)
                for kt in range(8):
                    S.op("pe", lambda e, pi=pi, kt=kt, j=j, ts=ts: e.matmul(pH[pi][:], lhsT=wfi[:, kt, j * 128:(j + 1) * 128], rhs=fnT[:, kt, ts],
                                                                            start=(kt == 0), stop=(kt == 7)), reads=[b_wfi, b_fnT], writes=[b_pH[pi]])
                S.op("act", lambda e, pi=pi: e.activation(out=r32[pi][:], in_=pH[pi][:], func=AF.Relu), reads=[b_pH[pi]], writes=[b_r32[pi]])
                S.op("pool", lambda e, pi=pi, j=j, ts=ts: e.tensor_tensor(out=hidT[:, j, ts], in0=r32[pi][:], in1=r32[pi][:], op=ALU.mult),
                     reads=[b_r32[pi]], writes=[b_hidT])
        for t in range(NTI):
            tsl = slice(t * 128, (t + 1) * 128)
            for hc in range(2):
                pi = fcnt % 4
                fcnt += 1
                cs = slice(hc * 512, (hc + 1) * 512)
                for j in range(4):
                    S.op("pe", lambda e, pi=pi, j=j, tsl=tsl, cs=cs: e.matmul(pF[pi][:], lhsT=hidT[:, j, tsl], rhs=wfo[:, j, cs], start=(j == 0), stop=(j == 3)),
                         reads=[b_hidT, b_wfo], writes=[b_pF[pi]])
                if c == 0:
                    S.op("act", lambda e, pi=pi, t=t, cs=cs: e.copy(out=facc[:, t, cs], in_=pF[pi][:]), reads=[b_pF[pi]], writes=[b_facc[t]])
                else:
                    S.op(D, lambda e, pi=pi, t=t, cs=cs: e.tensor_tensor(out=facc[:, t, cs], in0=pF[pi][:], in1=facc[:, t, cs], op=ALU.add),
                         reads=[b_pF[pi], b_facc[t]], writes=[b_facc[t]])
    for t in range(NTI):
        i = t % 2
        r0 = t * 128
        S.dma("sp", ht[i][:], h1[r0:r0 + 128, :], writes=[b_ht[i]])
        rms_rstd(S, facc[:, t, :], b_facc[t], junk[:], b_junk, st[i], b_st[i], epsT[:, 0:1], b_eps, 1024)
        S.op(D, lambda e, i=i, t=t: e.scalar_tensor_tensor(out=facc[:, t, :], in0=facc[:, t, :], scalar=st[i][:, 2:3], in1=gp_bc[:], op0=ALU.mult, op1=ALU.mult),
             reads=[b_facc[t], b_st[i], b_gp], writes=[b_facc[t]])
        S.op("pool", lambda e, i=i, t=t: e.tensor_tensor(out=ht[i][:], in0=ht[i][:], in1=facc[:, t, :], op=ALU.add), reads=[b_ht[i], b_facc[t]], writes=[b_ht[i]])
        S.dma("sp", h2[r0:r0 + 128, :], ht[i][:], reads=[b_ht[i]])
    S.flush()
    return nc


def _run(nc, in_maps):
    res = run_bass_kernel_spmd(nc, in_maps, core_ids=list(range(len(in_maps))))
    return res.results


def _c(a):
    return np.ascontiguousarray(a)


def kernel_unfused(**inputs):
    x = np.asarray(inputs["x"], dtype=np.float32)
    P = {k: np.asarray(v, dtype=np.float32) for k, v in inputs.items() if k != "x"}
    NCORE = 8
    ident = np.eye(128, dtype=BF)
    iota = _c(np.tile(np.arange(SEG, dtype=np.float32)[None, :], (128, 1)))
    toks = [np.arange(hf * NT, (hf + 1) * NT) for hf in range(2)]
    rope = [rope_tables(t // 64, t % 64, t) for t in toks]
    alibi = [alibi_tables(hf) for hf in range(2)]
    h = [_c(x[c // 2, toks[c % 2]]) for c in range(NCORE)]
    for li in range(2):
        lam_init = 0.8 - 0.6 * math.exp(-0.3 * li)
        gB = np.concatenate([np.tile(P["gqa_q_norm"][li], 8), np.tile(P["gqa_k_norm"][li], 2)])
        gC = np.concatenate([P["mla_q_norm"][li], P["mla_kv_norm"][li]])
        maps = []
        for c in range(NCORE):
            cb, sb_, cc, sc = rope[c % 2]
            maps.append(dict(h=h[c], g_pre=P["norm_pre_mix"][li], w_in=P["w_in"][li], gB=gB, gC=gC, w_uq=P["mla_w_uq"][li], w_ukv=P["mla_w_ukv"][li],
                             cosb=cb, sinb=sb_, cosc=cc, sinc=sc, identd=ident))
        rp = _run(build_P(), maps)
        maps = []
        for c in range(NCORE):
            b, hf = c // 2, c % 2
            own, par = rp[2 * b + hf], rp[2 * b + 1 - hf]
            cat = lambda k: np.concatenate([own[k], par[k]], axis=0)
            oa, ob, ockv, ockr = cat("oa"), cat("ob"), cat("ockv"), cat("ockr")
            QT = np.zeros((24, 96, NT), BF); KT = np.zeros((18, 96, 4096), BF)
            QT[0:8, 0:64] = own["oa"][:, 0:512].reshape(NT, 8, 64).transpose(1, 2, 0)
            QT[8:16, 0:64] = own["ob"][:, 0:512].reshape(NT, 8, 64).transpose(1, 2, 0)
            QT[16:24] = own["ocq"].reshape(NT, 8, 96).transpose(1, 2, 0)
            KT[0:8, 0:64] = oa[:, 512:1024].reshape(4096, 8, 64).transpose(1, 2, 0)
            KT[8:10, 0:64] = ob[:, 512:640].reshape(4096, 2, 64).transpose(1, 2, 0)
            kv = ockv.reshape(4096, 8, 128)
            KT[10:18, 0:64] = kv[:, :, 0:64].transpose(1, 2, 0)
            KT[10:18, 64:96] = ockr.T[None, :, :]
            VA = oa[:, 1024:1536].reshape(32, 128, 4, 128).transpose(2, 1, 0, 3)
            vb = ob[:, 640:768].reshape(32, 128, 2, 64).transpose(2, 1, 0, 3)
            vc = kv[:, :, 64:128].reshape(32, 128, 8, 64).transpose(2, 1, 0, 3)
            VBC = np.concatenate([vb, vc], axis=0)
            tO, tP = alibi[hf]
            maps.append(dict(QT=_c(QT), KT=_c(KT), VA=_c(VA), VBC=_c(VBC), tabOwn=tO, tabPar=tP))
        rq1 = _run(build_Q1(), maps)
        maps = []
        for c in range(NCORE):
            b, hf = c // 2, c % 2
            u_full = np.concatenate([rp[2 * b]["ou"], rp[2 * b + 1]["ou"]], axis=0)
            uT = _c(u_full.T.reshape(4, 128, 4096)[2 * hf:2 * hf + 2])
            m = s5_host_layout({k: P[k] for k in ("s5_a_re", "s5_a_im", "s5_log_dt", "s5_b_re", "s5_b_im", "s5_c_re", "s5_c_im")}, li, hf)
            m.update(uT=uT, iota=iota, identd=ident)
            maps.append(m)
        rq2 = _run(build_Q2(), maps)
        maps = []
        w_br = _c(np.stack([P["w_br_a"][li], P["w_br_b"][li], P["w_br_c"][li], P["w_br_d"][li]]))
        lamv = np.concatenate([P["diff_lam_q1"][li], P["diff_lam_k1"][li], P["diff_lam_q2"][li], P["diff_lam_k2"][li]])
        dcol = _c(P["s5_d"][li].reshape(4, 128).T)
        for c in range(NCORE):
            b, hf = c // 2, c % 2
            ys_full = np.concatenate([rq2[2 * b]["yT"], rq2[2 * b + 1]["yT"]], axis=0)
            ysT = _c(ys_full[:, :, toks[hf]])
            uTo = _c(rp[c]["ou"].T.reshape(4, 128, NT))
            OAtok = _c(rq1[c]["OA"].transpose(2, 0, 1))
            obc = rq1[c]["OBC"]
            maps.append(dict(h=h[c], g_pre=P["norm_pre_mix"][li], g_post=P["norm_post_mix"][li], OAtok=OAtok, lamv=lamv, subln=P["diff_subln"][li],
                             ybT=_c(obc[0:8].reshape(4, 128, NT)), ycT=_c(obc[8:16].reshape(4, 128, NT)), ysT=ysT, uTo=uTo, dcol=dcol,
                             w_glu=P["s5_w_glu"][li], w_gate=P["w_gate"][li], w_br=w_br, w_out=P["w_out"][li], identd=ident))
        rq3 = _run(build_Q3(lam_init), maps)
        maps = [dict(h1=rq3[c]["h1"], g_pre=P["norm_pre_ffn"][li], g_post=P["norm_post_ffn"][li], w_fi=P["w_ffn_in"][li], w_fo=P["w_ffn_out"][li], identd=ident)
                for c in range(NCORE)]
        rq4 = _run(build_Q4(), maps)
        h = [rq4[c]["h2"] for c in range(NCORE)]
    out = np.zeros_like(x)
    for c in range(NCORE):
        out[c // 2, toks[c % 2]] = h[c]
    return out


PAIRS = [[0, 1], [2, 3], [4, 5], [6, 7]]
TABW2 = 6016
TABOFF2 = 3968
VCOLS = 4 * 16 * 128 + 10 * 16 * 64
KCH = [(0, 5), (5, 10), (10, 14), (14, 18)]


def kchunk(k):
    for j, (a, b) in enumerate(KCH):
        if a <= k < b:
            return j, k - a


def alibi_table_core(half):
    x = np.arange(TABW2, dtype=np.float64)[None, :]
    ki = np.arange(128, dtype=np.float64)[:, None]
    return np.stack([np.exp(-m * np.abs(2048.0 * half + x - TABOFF2 - ki)) for m in SLOPES]).astype(np.float32)


class Fused:
    def __init__(self):
        self.C = Ctx()
        self.nc, self.S = self.C.nc, self.C.S
        self.A = None

    def begin(self):
        self.A = TileAlloc(self.nc)
        return self.A

    def end(self, include_cc=False):
        self.S.flush(include_cc=include_cc)
        self.A.close()
        self.A = None

    def scratch(self, name, shape, dt):
        return self.nc.dram_tensor(name, list(shape), dt).ap()


def emit_P(F, li, h_src, W, SC):
    S, A = F.S, F.begin()
    D = "dve"
    ident = A.sb("ident", [128, 128], BF16); b_ident = Buf()
    S.dma("sp", ident[:], W["identd"][:, :], writes=[b_ident])
    g_bc = A.sb("g_bc", [128, 1024], F32); b_g = Buf()
    S.dma("pool", g_bc[:], W["norm_pre_mix"][li].partition_broadcast(128), writes=[b_g])
    gB_bc = A.sb("gB_bc", [128, 640], F32); b_gB = Buf()
    S.dma("pool", gB_bc[:], W["gB"][li].partition_broadcast(128), writes=[b_gB])
    gC_bc = A.sb("gC_bc", [128, 384], F32); b_gC = Buf()
    S.dma("pool", gC_bc[:], W["gC"][li].partition_broadcast(128), writes=[b_gC])
    epsT = A.sb("epsT", [128, 1], F32); b_eps = Buf()
    S.op(D, lambda e: e.memset(epsT[:], EPS), writes=[b_eps])

    stg = [A.sb(f"stg{i}", [128, 1024], F32) for i in range(2)]; b_stg = [Buf() for _ in range(2)]
    sidx = [0]

    def load_w(dst, b_dst, src, ktiles, ncols, rows=None):
        for c0 in range(0, ncols, 1024):
            cw = min(1024, ncols - c0)
            bd = b_dst[c0 // 1024] if isinstance(b_dst, list) else b_dst
            for kt in range(ktiles):
                i = sidx[0] % 2
                sidx[0] += 1
                S.dma("sp", stg[i][:, 0:cw], src[kt * 128:(kt + 1) * 128, c0:c0 + cw], writes=[b_stg[i]])
                if i == 0:
                    S.op("pool", lambda e, i=i, kt=kt, c0=c0, cw=cw: e.tensor_copy(out=dst[:, kt, c0:c0 + cw], in_=stg[i][:, 0:cw]), reads=[b_stg[i]], writes=[bd])
                else:
                    S.op(D, lambda e, i=i, kt=kt, c0=c0, cw=cw: e.tensor_copy(out=dst[:, kt, c0:c0 + cw], in_=stg[i][:, 0:cw]), reads=[b_stg[i]], writes=[bd])

    wb = A.sb("wb", [128, 8, 3232], BF16); b_wb = [Buf() for _ in range(4)]
    load_w(wb, b_wb, W["w_in"][li], 8, 3232)
    wuq = A.sb("wuq", [128, 2, 768], BF16); b_wuq = Buf()
    load_w(wuq, b_wuq, W["mla_w_uq"][li], 2, 768)
    wukv = A.sb("wukv", [128, 1, 1024], BF16); b_wukv = Buf()
    load_w(wukv, b_wukv, W["mla_w_ukv"][li], 1, 1024)

    NB = 2
    ht = [A.sb(f"ht{i}", [128, 1024], F32) for i in range(NB)]; b_ht = [Buf() for _ in range(NB)]
    junk = A.sb("junk", [128, 1024], F32); b_junk = Buf()
    st1 = [A.sb(f"st1{i}", [128, 16], F32) for i in range(NB)]; b_st1 = [Buf() for _ in range(NB)]
    xn = [A.sb(f"xn{i}", [128, 1024], BF16) for i in range(NB)]; b_xn = [Buf() for _ in range(NB)]
    xnT = A.sb("xnT", [128, 8, 128], BF16); b_xnT = Buf()
    pj = A.sb("pj", [128, 3232], F32); b_pj = Buf()
    tA = [A.sb(f"tA{i}", [128, 1536], BF16) for i in range(NB)]; b_tA = [Buf() for _ in range(NB)]
    tB = [A.sb(f"tB{i}", [128, 768], BF16) for i in range(NB)]; b_tB = [Buf() for _ in range(NB)]
    tCq = [A.sb(f"tCq{i}", [128, 768], BF16) for i in range(NB)]; b_tCq = [Buf() for _ in range(NB)]
    tCkv = [A.sb(f"tCkv{i}", [128, 1024], BF16) for i in range(NB)]; b_tCkv = [Buf() for _ in range(NB)]
    tCkr = A.sb("tCkr", [128, 32], BF16); b_tCkr = Buf()
    tCk = A.sb("tCk", [128, 8, 96], BF16); b_tCk = Buf()
    ub = A.sb("ub", [128, 512], BF16); b_ub = Buf()
    wk1 = A.sb("wk1", [128, 768], F32); b_wk1 = Buf()
    wk2 = A.sb("wk2", [128, 768], F32); b_wk2 = Buf()
    wk3 = A.sb("wk3", [128, 768], F32); b_wk3 = Buf()
    lat = A.sb("lat", [128, 384], BF16); b_lat = Buf()
    latT = A.sb("latT", [128, 3, 128], BF16); b_latT = Buf()
    qc = A.sb("qc", [128, 768], F32); b_qc = Buf()
    rb = [A.sb(f"rb{i}", [128, 192], F32) for i in range(NB)]; b_rb = [Buf() for _ in range(NB)]
    qS = [A.sb(f"qS{i}", [96, 24, 256], BF16) for i in range(2)]; b_qS = [Buf() for _ in range(2)]
    kS = [A.sb(f"kS{i}", [96, 18, 256], BF16) for i in range(2)]; b_kS = [Buf() for _ in range(2)]
    uS = A.sb("uS", [128, 4, NT], BF16); b_uS = Buf()
    for i in range(2):
        S.op("pool", lambda e, i=i: e.memset(qS[i][:], 0.0), writes=[b_qS[i]])
        S.op("pool", lambda e, i=i: e.memset(kS[i][:], 0.0), writes=[b_kS[i]])

    pTs = [A.ps(f"pT{i}", [128, 1024], BF16) for i in range(3)]; b_pTs = [Buf() for _ in range(3)]
    pP = [A.ps(f"pP{i}", [128, 512], F32) for i in range(2)]; b_pP = [Buf() for _ in range(2)]
    pU = [A.ps(f"pU{i}", [128, 512], F32) for i in range(2)]; b_pU = [Buf() for _ in range(2)]
    pti = [0]

    def transposes(srcs, b_src, rows, dst_ap, b_dst, copy_eng):
        k = pti[0] % 3
        pti[0] += 1
        n = len(srcs)
        for j, sap in enumerate(srcs):
            S.op("pe", lambda e, k=k, j=j, sap=sap: e.transpose(out=pTs[k][0:rows, j * 128:(j + 1) * 128], in_=sap, identity=ident[:]),
                 reads=[b_src, b_ident], writes=[b_pTs[k]])
        src_v = pTs[k][0:rows, 0:n * 128].rearrange("p (n t) -> p n t", n=n)
        if copy_eng == "act":
            S.op("act", lambda e: e.copy(out=dst_ap, in_=src_v), reads=[b_pTs[k]], writes=[b_dst])
        else:
            S.op(D, lambda e: e.tensor_copy(out=dst_ap, in_=src_v), reads=[b_pTs[k]], writes=[b_dst])

    QTv = SC["QT"].rearrange("m d t -> d m t")
    KTv = [SC["kt_loc"][j].rearrange("(m d) t -> d m t", d=96) for j in range(4)]
    vA = SC["v_loc"][0].rearrange("p (h k d) -> p h k d", h=4, k=16)
    vB1 = SC["v_loc"][1].rearrange("p (h k d) -> p h k d", h=5, k=16)
    vB2 = SC["v_loc"][2].rearrange("p (h k d) -> p h k d", h=5, k=16)
    cosb, sinb, cosc, sinc = W["cosb"], W["sinb"], W["cosc"], W["sinc"]
    chunks = [(c0, min(512, 3232 - c0)) for c0 in range(0, 3232, 512)]
    for t in range(NTI):
        i = t % NB
        r0 = t * 128
        S.dma("sp", ht[i][:], h_src[r0:r0 + 128, :], writes=[b_ht[i]])
        S.dma("sp", rb[i][:, 0:64], cosb[r0:r0 + 128, :], writes=[b_rb[i]])
        S.dma("sp", rb[i][:, 64:128], sinb[r0:r0 + 128, :], writes=[b_rb[i]])
        S.dma("sp", rb[i][:, 128:160], cosc[r0:r0 + 128, :], writes=[b_rb[i]])
        S.dma("sp", rb[i][:, 160:192], sinc[r0:r0 + 128, :], writes=[b_rb[i]])
        s1 = st1[i]
        rms_rstd(S, ht[i][:], b_ht[i], junk[:], b_junk, s1, b_st1[i], epsT[:, 0:1], b_eps, 1024)
        S.op(D, lambda e, i=i, s1=s1: e.scalar_tensor_tensor(out=xn[i][:], in0=ht[i][:], scalar=s1[:, 2:3], in1=g_bc[:], op0=ALU.mult, op1=ALU.mult),
             reads=[b_ht[i], b_st1[i], b_g], writes=[b_xn[i]])
        transposes([xn[i][:, kt * 128:(kt + 1) * 128] for kt in range(8)], b_xn[i], 128, xnT[:], b_xnT, "act")
        for ci, (c0, cw) in enumerate(chunks):
            pb = ci % 2
            for kt in range(8):
                S.op("pe", lambda e, kt=kt, c0=c0, cw=cw, pb=pb: e.matmul(pP[pb][:, 0:cw], lhsT=xnT[:, kt, :], rhs=wb[:, kt, c0:c0 + cw], start=(kt == 0), stop=(kt == 7)),
                     reads=[b_xnT, b_wb[c0 // 1024]], writes=[b_pP[pb]])
            if ci % 2 == 0:
                S.op("act", lambda e, c0=c0, cw=cw, pb=pb: e.copy(out=pj[:, c0:c0 + cw], in_=pP[pb][:, 0:cw]), reads=[b_pP[pb]], writes=[b_pj])
            else:
                S.op(D, lambda e, c0=c0, cw=cw, pb=pb: e.tensor_copy(out=pj[:, c0:c0 + cw], in_=pP[pb][:, 0:cw]), reads=[b_pP[pb]], writes=[b_pj])
        p = pj
        S.op("pool", lambda e, i=i: e.tensor_copy(out=tA[i][:], in_=p[:, 0:1536]), reads=[b_pj], writes=[b_tA[i]])
        S.op("pool", lambda e: e.tensor_copy(out=ub[:], in_=p[:, 2720:3232]), reads=[b_pj], writes=[b_ub])
        xB = p[:, 1536:2176]
        S.op(D, lambda e: e.tensor_tensor(out=wk1[:, 0:640], in0=xB, in1=xB, op=ALU.mult), reads=[b_pj], writes=[b_wk1])
        S.op(D, lambda e, s1=s1: e.tensor_reduce(out=s1[:, 4:14], in_=wk1[:, 0:640].rearrange("p (h d) -> p h d", h=10), axis=AX.X, op=ALU.add),
             reads=[b_wk1], writes=[b_st1[i]])
        S.op("act", lambda e, s1=s1: e.activation(out=s1[:, 4:14], in_=s1[:, 4:14], func=AF.Sqrt, bias=epsT[:, 0:1], scale=1.0 / 64),
             reads=[b_st1[i], b_eps], writes=[b_st1[i]])
        S.op(D, lambda e, s1=s1: e.reciprocal(out=s1[:, 4:14], in_=s1[:, 4:14]), reads=[b_st1[i]], writes=[b_st1[i]])
        S.op(D, lambda e, s1=s1: e.tensor_tensor(out=wk1[:, 0:640].rearrange("p (h d) -> p h d", h=10), in0=xB.rearrange("p (h d) -> p h d", h=10),
                                                in1=s1[:, 4:14].unsqueeze(2).broadcast_to([128, 10, 64]), op=ALU.mult),
             reads=[b_pj, b_st1[i]], writes=[b_wk1])
        S.op("pool", lambda e: e.tensor_tensor(out=wk1[:, 0:640], in0=wk1[:, 0:640], in1=gB_bc[:], op=ALU.mult), reads=[b_wk1, b_gB], writes=[b_wk1])
        cosB = rb[i][:, 0:64].unsqueeze(1).broadcast_to([128, 10, 64])
        S.op("pool", lambda e, cosB=cosB: e.tensor_tensor(out=wk2[:, 0:640].rearrange("p (h d) -> p h d", h=10), in0=wk1[:, 0:640].rearrange("p (h d) -> p h d", h=10),
                                                         in1=cosB, op=ALU.mult), reads=[b_wk1, b_rb[i]], writes=[b_wk2])
        x5 = wk1[:, 0:640].rearrange("p (h a b d) -> p h a b d", h=10, a=2, b=2)
        o5 = wk3[:, 0:640].rearrange("p (h a b d) -> p h a b d", h=10, a=2, b=2)
        s4 = rb[i][:, 64:128].rearrange("p (a b d) -> p a b d", a=2, b=2)
        for bb in range(2):
            S.op(D, lambda e, bb=bb: e.tensor_tensor(out=o5[:, :, :, bb, :], in0=x5[:, :, :, 1 - bb, :],
                                                    in1=s4[:, :, bb, :].unsqueeze(1).broadcast_to([128, 10, 2, 16]), op=ALU.mult),
                 reads=[b_wk1, b_rb[i]], writes=[b_wk3])
        S.op(D, lambda e, i=i: e.tensor_tensor(out=tB[i][:, 0:640], in0=wk2[:, 0:640], in1=wk3[:, 0:640], op=ALU.add), reads=[b_wk2, b_wk3], writes=[b_tB[i]])
        S.op("pool", lambda e, i=i: e.tensor_copy(out=tB[i][:, 640:768], in_=p[:, 2176:2304]), reads=[b_pj], writes=[b_tB[i]])
        xC = p[:, 2304:2688]
        S.op(D, lambda e: e.tensor_tensor(out=wk2[:, 0:384], in0=xC, in1=xC, op=ALU.mult), reads=[b_pj], writes=[b_wk2])
        S.op(D, lambda e, s1=s1: e.tensor_reduce(out=s1[:, 14:15], in_=wk2[:, 0:256], axis=AX.X, op=ALU.add), reads=[b_wk2], writes=[b_st1[i]])
        S.op(D, lambda e, s1=s1: e.tensor_reduce(out=s1[:, 15:16], in_=wk2[:, 256:384], axis=AX.X, op=ALU.add), reads=[b_wk2], writes=[b_st1[i]])
        S.op("act", lambda e, s1=s1: e.activation(out=s1[:, 14:15], in_=s1[:, 14:15], func=AF.Sqrt, bias=epsT[:, 0:1], scale=1.0 / 256),
             reads=[b_st1[i], b_eps], writes=[b_st1[i]])
        S.op("act", lambda e, s1=s1: e.activation(out=s1[:, 15:16], in_=s1[:, 15:16], func=AF.Sqrt, bias=epsT[:, 0:1], scale=1.0 / 128),
             reads=[b_st1[i], b_eps], writes=[b_st1[i]])
        S.op(D, lambda e, s1=s1: e.reciprocal(out=s1[:, 14:16], in_=s1[:, 14:16]), reads=[b_st1[i]], writes=[b_st1[i]])
        S.op(D, lambda e, s1=s1: e.scalar_tensor_tensor(out=lat[:, 0:256], in0=p[:, 2304:2560], scalar=s1[:, 14:15], in1=gC_bc[:, 0:256], op0=ALU.mult, op1=ALU.mult),
             reads=[b_pj, b_st1[i], b_gC], writes=[b_lat])
        S.op(D, lambda e, s1=s1: e.scalar_tensor_tensor(out=lat[:, 256:384], in0=p[:, 2560:2688], scalar=s1[:, 15:16], in1=gC_bc[:, 256:384], op0=ALU.mult, op1=ALU.mult),
             reads=[b_pj, b_st1[i], b_gC], writes=[b_lat])
        transposes([lat[:, kt * 128:(kt + 1) * 128] for kt in range(3)], b_lat, 128, latT[:], b_latT, "act")
        for cj in range(2):
            for kt in range(2):
                S.op("pe", lambda e, cj=cj, kt=kt: e.matmul(pU[cj][:, 0:384], lhsT=latT[:, kt, :], rhs=wuq[:, kt, cj * 384:(cj + 1) * 384], start=(kt == 0), stop=(kt == 1)),
                     reads=[b_latT, b_wuq], writes=[b_pU[cj]])
            S.op("act", lambda e, cj=cj: e.copy(out=qc[:, cj * 384:(cj + 1) * 384], in_=pU[cj][:, 0:384]), reads=[b_pU[cj]], writes=[b_qc])
        q3 = qc[:].rearrange("p (h d) -> p h d", h=8)
        o3 = tCq[i][:].rearrange("p (h d) -> p h d", h=8)
        S.op("pool", lambda e: e.tensor_copy(out=o3[:, :, 0:64], in_=q3[:, :, 0:64]), reads=[b_qc], writes=[b_tCq[i]])
        cosC = rb[i][:, 128:160]
        sinC = rb[i][:, 160:192]
        w2 = wk2[:, 0:256].rearrange("p (h d) -> p h d", h=8)
        w3 = wk3[:, 0:256].rearrange("p (h d) -> p h d", h=8)
        S.op(D, lambda e: e.tensor_tensor(out=w2, in0=q3[:, :, 64:96], in1=cosC.unsqueeze(1).broadcast_to([128, 8, 32]), op=ALU.mult),
             reads=[b_qc, b_rb[i]], writes=[b_wk2])
        for bb in range(2):
            S.op(D, lambda e, bb=bb: e.tensor_tensor(out=w3[:, :, bb * 16:(bb + 1) * 16], in0=q3[:, :, 64 + (1 - bb) * 16:64 + (2 - bb) * 16],
                                                    in1=sinC[:, bb * 16:(bb + 1) * 16].unsqueeze(1).broadcast_to([128, 8, 16]), op=ALU.mult),
                 reads=[b_qc, b_rb[i]], writes=[b_wk3])
        S.op(D, lambda e: e.tensor_tensor(out=o3[:, :, 64:96], in0=w2, in1=w3, op=ALU.add), reads=[b_wk2, b_wk3], writes=[b_tCq[i]])
        for cj in range(2):
            S.op("pe", lambda e, cj=cj: e.matmul(pU[cj][:, 0:512], lhsT=latT[:, 2, :], rhs=wukv[:, 0, cj * 512:(cj + 1) * 512], start=True, stop=True),
                 reads=[b_latT, b_wukv], writes=[b_pU[cj]])
            S.op("act", lambda e, cj=cj, i=i: e.copy(out=tCkv[i][:, cj * 512:(cj + 1) * 512], in_=pU[cj][:, 0:512]), reads=[b_pU[cj]], writes=[b_tCkv[i]])
        kr = p[:, 2688:2720]
        S.op(D, lambda e: e.tensor_tensor(out=wk2[:, 256:288], in0=kr, in1=cosC, op=ALU.mult), reads=[b_pj, b_rb[i]], writes=[b_wk2])
        for bb in range(2):
            S.op(D, lambda e, bb=bb: e.tensor_tensor(out=wk3[:, 256 + bb * 16:256 + (bb + 1) * 16], in0=kr[:, (1 - bb) * 16:(2 - bb) * 16],
                                                    in1=sinC[:, bb * 16:(bb + 1) * 16], op=ALU.mult), reads=[b_pj, b_rb[i]], writes=[b_wk3])
        S.op(D, lambda e: e.tensor_tensor(out=tCkr[:], in0=wk2[:, 256:288], in1=wk3[:, 256:288], op=ALU.add), reads=[b_wk2, b_wk3], writes=[b_tCkr])
        kv3 = tCkv[i][:].rearrange("p (h d) -> p h d", h=8)
        S.op("pool", lambda e, kv3=kv3: e.tensor_copy(out=tCk[:, :, 0:64], in_=kv3[:, :, 0:64]), reads=[b_tCkv[i]], writes=[b_tCk])
        S.op("pool", lambda e: e.tensor_copy(out=tCk[:, :, 64:96], in_=tCkr[:].unsqueeze(1).broadcast_to([128, 8, 32])), reads=[b_tCkr], writes=[b_tCk])
        S.dma("sp", vA[:, :, t, :], tA[i][:, 1024:1536].rearrange("p (h d) -> p h d", h=4), reads=[b_tA[i]])
        S.dma("sp", vB1[:, 0:2, t, :], tB[i][:, 640:768].rearrange("p (h d) -> p h d", h=2), reads=[b_tB[i]])
        S.dma("sp", vB1[:, 2:5, t, :], kv3[:, 0:3, 64:128], reads=[b_tCkv[i]])
        S.dma("sp", vB2[:, :, t, :], kv3[:, 3:8, 64:128], reads=[b_tCkv[i]])
        sl = (t // 2) % 2
        tsl = slice((t % 2) * 128, (t % 2 + 1) * 128)
        transposes([tA[i][:, m * 64:(m + 1) * 64] for m in range(8)], b_tA[i], 64, qS[sl][0:64, 0:8, tsl], b_qS[sl], "act")
        transposes([tA[i][:, 512 + m * 64:512 + (m + 1) * 64] for m in range(8)], b_tA[i], 64, kS[sl][0:64, 0:8, tsl], b_kS[sl], "dve")
        transposes([tB[i][:, j * 64:(j + 1) * 64] for j in range(8)], b_tB[i], 64, qS[sl][0:64, 8:16, tsl], b_qS[sl], "act")
        transposes([tB[i][:, 512 + g * 64:512 + (g + 1) * 64] for g in range(2)], b_tB[i], 64, kS[sl][0:64, 8:10, tsl], b_kS[sl], "dve")
        transposes([tCq[i][:, j * 96:(j + 1) * 96] for j in range(8)], b_tCq[i], 96, qS[sl][0:96, 16:24, tsl], b_qS[sl], "act")
        transposes([tCk[:, j, :] for j in range(8)], b_tCk, 96, kS[sl][0:96, 10:18, tsl], b_kS[sl], "dve")
        transposes([ub[:, c * 128:(c + 1) * 128] for c in range(4)], b_ub, 128, uS[:, :, r0:r0 + 128], b_uS, "act")
        if t % 2 == 1:
            g0 = (t - 1) * 128
            S.dma("sp", QTv[:, :, g0:g0 + 256], qS[sl][:], reads=[b_qS[sl]])
            for j, (ka, kb_) in enumerate(KCH):
                S.dma("sp", KTv[j][:, :, g0:g0 + 256], kS[sl][:, ka:kb_, :], reads=[b_kS[sl]])
    S.dma("sp", SC["uT_loc"].rearrange("(c p) t -> p c t", p=128), uS[:], reads=[b_uS])
    F.end()


def emit_S5(F, li, W, SC, PB):
    S, A = F.S, F.begin()
    I32 = mybir.dt.int32
    D = "dve"
    NC_ = 2 * NGP
    SEG = 1024
    NSEG = 4096 // SEG
    ident = A.sb("ident", [128, 128], BF16); b_ident = Buf()
    S.dma("sp", ident[:], W["identd"][:, :], writes=[b_ident])
    iota = A.sb("iota", [128, SEG], F32); b_iota = Buf()
    S.dma("sp", iota[:], W["iota"][:, 0:SEG], writes=[b_iota])
    halfpi = A.sb("halfpi", [128, 2], F32); b_hp = Buf()
    S.op(D, lambda e: e.memset(halfpi[:, 0:1], math.pi / 2), writes=[b_hp])
    S.op(D, lambda e: e.memset(halfpi[:, 1:2], -1.0), writes=[b_hp])
    sel = A.sb("sel", [128, 2], F32); b_sel = Buf()
    S.dma("sp", sel[:], W["sel"][:, :], writes=[b_sel])

    def small(name, dt=F32):
        return A.sb(name, [128, NC_], dt), Buf()
    are, b_are = small("are"); aim, b_aim = small("aim"); ldt, b_ldt = small("ldt")
    S.dma("sp", are[:], W["areT"][li], writes=[b_are]); S.dma("sp", aim[:], W["aimT"][li], writes=[b_aim]); S.dma("sp", ldt[:], W["ldtT"][li], writes=[b_ldt])
    dt_, b_dt = small("dt"); mag, b_mag = small("mag"); thn, b_thn = small("thn")
    c1, b_c1 = small("c1"); s1, b_s1 = small("s1"); ski, b_ski = small("ski", I32); sa, b_sa = small("sa"); sb_, b_sb = small("sb")
    lbre, b_lbre = small("lbre"); lbim, b_lbim = small("lbim"); rden, b_rden = small("rden"); nre, b_nre = small("nre")
    fre, b_fre = small("fre"); fim, b_fim = small("fim"); nfim, b_nfim = small("nfim"); w1, b_w1 = small("w1"); w2, b_w2 = small("w2")
    S.op(D, lambda e: e.tensor_scalar_min(out=are[:], in0=are[:], scalar1=-1e-4), reads=[b_are], writes=[b_are])
    S.op("act", lambda e: e.activation(out=dt_[:], in_=ldt[:], func=AF.Exp), reads=[b_ldt], writes=[b_dt])
    S.op(D, lambda e: e.tensor_tensor(out=w1[:], in0=are[:], in1=dt_[:], op=ALU.mult), reads=[b_are, b_dt], writes=[b_w1])
    S.op("act", lambda e: e.activation(out=mag[:], in_=w1[:], func=AF.Exp), reads=[b_w1], writes=[b_mag])
    S.op(D, lambda e: e.tensor_tensor(out=w2[:], in0=aim[:], in1=dt_[:], op=ALU.mult), reads=[b_aim, b_dt], writes=[b_w2])
    S.op(D, lambda e: e.tensor_scalar(out=thn[:], in0=w2[:], scalar1=1.0 / (2 * math.pi), scalar2=None, op0=ALU.mult), reads=[b_w2], writes=[b_thn])
    emit_sincos(S, thn[:], b_thn, ski[:], b_ski, sa[:], b_sa, sb_[:], b_sb, c1[:], b_c1, s1[:], b_s1, halfpi[:, 0:1], b_hp, neg1=halfpi[:, 1:2])
    S.op(D, lambda e: e.tensor_tensor(out=lbre[:], in0=mag[:], in1=c1[:], op=ALU.mult), reads=[b_mag, b_c1], writes=[b_lbre])
    S.op(D, lambda e: e.tensor_tensor(out=lbim[:], in0=mag[:], in1=s1[:], op=ALU.mult), reads=[b_mag, b_s1], writes=[b_lbim])
    S.op(D, lambda e: e.tensor_tensor(out=w1[:], in0=are[:], in1=are[:], op=ALU.mult), reads=[b_are], writes=[b_w1])
    S.op(D, lambda e: e.tensor_tensor(out=w2[:], in0=aim[:], in1=aim[:], op=ALU.mult), reads=[b_aim], writes=[b_w2])
    S.op(D, lambda e: e.tensor_tensor(out=w1[:], in0=w1[:], in1=w2[:], op=ALU.add), reads=[b_w1, b_w2], writes=[b_w1])
    S.op(D, lambda e: e.reciprocal(out=rden[:], in_=w1[:]), reads=[b_w1], writes=[b_rden])
    S.op(D, lambda e: e.tensor_scalar_add(out=nre[:], in0=lbre[:], scalar1=-1.0), reads=[b_lbre], writes=[b_nre])
    S.op(D, lambda e: e.tensor_tensor(out=w1[:], in0=nre[:], in1=are[:], op=ALU.mult), reads=[b_nre, b_are], writes=[b_w1])
    S.op(D, lambda e: e.tensor_tensor(out=w2[:], in0=lbim[:], in1=aim[:], op=ALU.mult), reads=[b_lbim, b_aim], writes=[b_w2])
    S.op(D, lambda e: e.tensor_tensor(out=w1[:], in0=w1[:], in1=w2[:], op=ALU.add), reads=[b_w1, b_w2], writes=[b_w1])
    S.op(D, lambda e: e.tensor_tensor(out=fre[:], in0=w1[:], in1=rden[:], op=ALU.mult), reads=[b_w1, b_rden], writes=[b_fre])
    S.op(D, lambda e: e.tensor_tensor(out=w1[:], in0=lbim[:], in1=are[:], op=ALU.mult), reads=[b_lbim, b_are], writes=[b_w1])
    S.op(D, lambda e: e.tensor_tensor(out=w2[:], in0=nre[:], in1=aim[:], op=ALU.mult), reads=[b_nre, b_aim], writes=[b_w2])
    S.op(D, lambda e: e.tensor_tensor(out=w1[:], in0=w1[:], in1=w2[:], op=ALU.subtract), reads=[b_w1, b_w2], writes=[b_w1])
    S.op(D, lambda e: e.tensor_tensor(out=fim[:], in0=w1[:], in1=rden[:], op=ALU.mult), reads=[b_w1, b_rden], writes=[b_fim])
    S.op(D, lambda e: e.tensor_scalar(out=nfim[:], in0=fim[:], scalar1=-1.0, scalar2=None, op0=ALU.mult), reads=[b_fim], writes=[b_nfim])

    uTb = A.sb("uTb", [128, 2, 4096], BF16); b_uTb = Buf()
    yacc = A.sb("yacc", [128, 2, 4096], F32); b_yacc = Buf()
    def wset(k):
        d = {}
        for nm, dt in (("XR", F32), ("XI", F32), ("T1", F32), ("T2", F32), ("T3", F32), ("Ct", F32), ("St", F32), ("KI", I32), ("sre", BF16), ("sim", BF16)):
            d[nm] = A.sb(f"{nm}{k}", [128, SEG], dt)
            d["b_" + nm] = Buf()
        return d
    WS = [wset(0), wset(1)]
    carry = A.sb("carry", [128, 2], F32); b_carry = Buf()
    wtmp = A.sb("wtmp", [128, 256], F32); b_wtmp = Buf()
    bst = [A.sb(f"bst{i}", [128, 128], F32) for i in range(4)]; b_bst = [Buf() for _ in range(4)]
    wpre = [A.sb(f"wpre{i}", [128, 128], BF16) for i in range(2)]; b_wpre = [Buf() for _ in range(2)]
    lx = A.sb("lx", [128, 2, 128], BF16); b_lx = Buf()
    ly = A.sb("ly", [128, 2, 128], BF16); b_ly = Buf()
    pT = A.ps("pT", [128, 256], BF16); b_pT = Buf()
    pX = [A.ps(f"pX{i}", [128, 512], F32) for i in range(4)]; b_pX = [Buf() for _ in range(4)]
    pY = [A.ps(f"pY{i}", [128, 512], F32) for i in range(2)]; b_pY = [Buf() for _ in range(2)]

    ca = WS[0]['sre']; cb = WS[1]['sre']; b_sre = WS[0]['b_sre']; b_sim = WS[1]['b_sre']; T1 = WS[0]['T1']; b_T1 = WS[0]['b_T1']
    uall = SC["uT_all"]
    for r in range(2):
        for ctl in range(2):
            r0a = r * 512 + ctl * 128
            r0b = r * 512 + (2 + ctl) * 128
            for hh in range(2048 // SEG):
                c0 = hh * SEG
                S.dma("sp", ca[:], uall[r0a:r0a + 128, c0:c0 + SEG], reads=[PB["uT_all"]], writes=[b_sre])
                S.dma("sp", cb[:], uall[r0b:r0b + 128, c0:c0 + SEG], reads=[PB["uT_all"]], writes=[b_sim])
                S.op("pool", lambda e: e.tensor_scalar(out=T1[:], in0=ca[:], scalar1=sel[:, 0:1], scalar2=None, op0=ALU.mult), reads=[b_sre, b_sel], writes=[b_T1])
                S.op(D, lambda e, r=r, ctl=ctl, c0=c0: e.scalar_tensor_tensor(out=uTb[:, ctl, r * 2048 + c0:r * 2048 + c0 + SEG], in0=cb[:], scalar=sel[:, 1:2], in1=T1[:],
                                                                            op0=ALU.mult, op1=ALU.add),
                     reads=[b_sim, b_sel, b_T1], writes=[b_uTb])

    Bre, Bim, Cre, Cim = W["Bre"], W["Bim"], W["Cre"], W["Cim"]
    ycnt = [0]
    ly2 = [A.sb(f"ly2_{i}", [128, 2, 128], BF16) for i in range(2)]; b_ly2 = [Buf() for _ in range(2)]
    tasks = []
    for gp in range(NGP):
        for dr in range(2):
            segs = list(range(NSEG)) if dr == 0 else list(range(NSEG - 1, -1, -1))
            for si, sg in enumerate(segs):
                tasks.append((gp, dr, si, sg))

    def bufs(k):
        ws = WS[k % 2]
        return ([ws[n] for n in ("XR", "XI", "T1", "T2", "T3", "Ct", "St", "KI", "sre", "sim")],
                [ws["b_" + n] for n in ("XR", "XI", "T1", "T2", "T3", "Ct", "St", "KI", "sre", "sim")])

    def f1(k):
        gp, dr, si, sg = tasks[k]
        ct = gp // 4
        col = dr * NGP + gp
        cs = slice(col, col + 1)
        par = (gp * 2 + dr) % 2
        (XR, XI, T1, T2, T3, Ct, St, KI, sre, sim_), (b_XR, b_XI, b_T1, b_T2, b_T3, b_Ct, b_St, b_KI, b_sre, b_sim) = bufs(k)
        if si == 0:
            S.dma("sp", bst[0][:], Bre[li, dr, gp], writes=[b_bst[0]])
            S.dma("sp", bst[1][:], Bim[li, dr, gp], writes=[b_bst[1]])
            S.dma("sp", bst[2][:], Cre[li, dr, gp], writes=[b_bst[2]])
            S.dma("sp", bst[3][:], Cim[li, dr, gp], writes=[b_bst[3]])
            S.op("pool", lambda e: e.tensor_scalar(out=wtmp[:, 0:128], in0=bst[0][:], scalar1=fre[:, cs], scalar2=None, op0=ALU.mult),
                 reads=[b_bst[0], b_fre], writes=[b_wtmp])
            S.op(D, lambda e: e.scalar_tensor_tensor(out=wpre[0][:], in0=bst[1][:], scalar=nfim[:, cs], in1=wtmp[:, 0:128], op0=ALU.mult, op1=ALU.add),
                 reads=[b_bst[1], b_nfim, b_wtmp], writes=[b_wpre[0]])
            S.op("pool", lambda e: e.tensor_scalar(out=wtmp[:, 128:256], in0=bst[0][:], scalar1=fim[:, cs], scalar2=None, op0=ALU.mult),
                 reads=[b_bst[0], b_fim], writes=[b_wtmp])
            S.op(D, lambda e: e.scalar_tensor_tensor(out=wpre[1][:], in0=bst[1][:], scalar=fre[:, cs], in1=wtmp[:, 128:256], op0=ALU.mult, op1=ALU.add),
                 reads=[b_bst[1], b_fre, b_wtmp], writes=[b_wpre[1]])
            for ri in range(2):
                S.op("pe", lambda e, ri=ri: e.transpose(out=pT[:, ri * 128:(ri + 1) * 128], in_=wpre[ri][:], identity=ident[:]),
                     reads=[b_wpre[ri], b_ident], writes=[b_pT])
            S.op("act", lambda e: e.copy(out=lx[:].rearrange("p a b -> p (a b)"), in_=pT[:]), reads=[b_pT], writes=[b_lx])
            S.op("act", lambda e: e.copy(out=ly2[par][:, 0, :], in_=bst[2][:]), reads=[b_bst[2]], writes=[b_ly2[par]])
            S.op("act", lambda e: e.mul(out=ly2[par][:, 1, :], in_=bst[3][:], mul=-1.0), reads=[b_bst[3]], writes=[b_ly2[par]])
        t0 = sg * SEG
        if dr == 0:
            io, off = iota[:, :], float(t0)
        else:
            io, off = iota[:, ::-1], float(4095 - t0 - (SEG - 1))
        S.op(D, lambda e: e.tensor_scalar(out=T1[:], in0=io, scalar1=off, scalar2=thn[:, cs], op0=ALU.add, op1=ALU.mult),
             reads=[b_iota, b_thn], writes=[b_T1])
        S.op("act", lambda e: e.copy(out=KI[:], in_=T1[:]), reads=[b_T1], writes=[b_KI])
        S.op("act", lambda e: e.copy(out=T2[:], in_=KI[:]), reads=[b_KI], writes=[b_T2])

    def f2(k):
        gp, dr, si, sg = tasks[k]
        ct = gp // 4
        col = dr * NGP + gp
        cs = slice(col, col + 1)
        par = (gp * 2 + dr) % 2
        t0 = sg * SEG
        (XR, XI, T1, T2, T3, Ct, St, KI, sre, sim_), (b_XR, b_XI, b_T1, b_T2, b_T3, b_Ct, b_St, b_KI, b_sre, b_sim) = bufs(k)
        hp_, n1_ = halfpi[:, 0:1], halfpi[:, 1:2]
        S.op(D, lambda e: e.tensor_tensor(out=T2[:], in0=T1[:], in1=T2[:], op=ALU.subtract), reads=[b_T1, b_T2], writes=[b_T2])
        S.op("act", lambda e: e.activation(out=T3[:], in_=T2[:], func=AF.Sin, scale=math.pi), reads=[b_T2], writes=[b_T3])
        S.op("act", lambda e: e.activation(out=T2[:], in_=T2[:], func=AF.Abs), reads=[b_T2], writes=[b_T2])
        S.op("act", lambda e: e.activation(out=T2[:], in_=T2[:], func=AF.Sin, scale=-math.pi, bias=hp_), reads=[b_T2, b_hp], writes=[b_T2])
        S.op(D, lambda e: e.scalar_tensor_tensor(out=St[:], in0=T3[:], scalar=2.0, in1=T2[:], op0=ALU.mult, op1=ALU.mult), reads=[b_T2, b_T3], writes=[b_St])
        S.op("act", lambda e: e.activation(out=T3[:], in_=T2[:], func=AF.Square), reads=[b_T2], writes=[b_T3])
        S.op("act", lambda e: e.activation(out=Ct[:], in_=T3[:], func=AF.Identity, scale=2.0, bias=n1_), reads=[b_T3, b_hp], writes=[b_Ct])

    def f3(k):
        gp, dr, si, sg = tasks[k]
        ct = gp // 4
        col = dr * NGP + gp
        cs = slice(col, col + 1)
        par = (gp * 2 + dr) % 2
        t0 = sg * SEG
        (XR, XI, T1, T2, T3, Ct, St, KI, sre, sim_), (b_XR, b_XI, b_T1, b_T2, b_T3, b_Ct, b_St, b_KI, b_sre, b_sim) = bufs(k)
        for blk in range(SEG // 512):
            ts = slice(t0 + blk * 512, t0 + (blk + 1) * 512)
            for ri in range(2):
                pi = (blk % 2) * 2 + ri
                S.op("pe", lambda e, ri=ri, pi=pi, ts=ts: e.matmul(pX[pi][:], lhsT=lx[:, ri, :], rhs=uTb[:, ct, ts], start=True, stop=True),
                     reads=[b_lx, b_uTb], writes=[b_pX[pi]])
                dst, bd = (XR, b_XR) if ri == 0 else (XI, b_XI)
                S.op("act", lambda e, dst=dst, pi=pi, blk=blk: e.copy(out=dst[:, blk * 512:(blk + 1) * 512], in_=pX[pi][:]), reads=[b_pX[pi]], writes=[bd])

    def f4(k):
        gp, dr, si, sg = tasks[k]
        ct = gp // 4
        col = dr * NGP + gp
        cs = slice(col, col + 1)
        par = (gp * 2 + dr) % 2
        t0 = sg * SEG
        (XR, XI, T1, T2, T3, Ct, St, KI, sre, sim_), (b_XR, b_XI, b_T1, b_T2, b_T3, b_Ct, b_St, b_KI, b_sre, b_sim) = bufs(k)
        S.op(D, lambda e: e.tensor_tensor(out=T1[:], in0=Ct[:], in1=XR[:], op=ALU.mult), reads=[b_Ct, b_XR], writes=[b_T1])
        S.op("pool", lambda e: e.tensor_tensor(out=T2[:], in0=St[:], in1=XI[:], op=ALU.mult), reads=[b_St, b_XI], writes=[b_T2])
        S.op(D, lambda e: e.tensor_tensor(out=T1[:], in0=T1[:], in1=T2[:], op=ALU.add), reads=[b_T1, b_T2], writes=[b_T1])
        S.op("pool", lambda e: e.tensor_tensor(out=T3[:], in0=St[:], in1=XR[:], op=ALU.mult), reads=[b_St, b_XR], writes=[b_T3])
        S.op(D, lambda e: e.tensor_tensor(out=T2[:], in0=Ct[:], in1=XI[:], op=ALU.mult), reads=[b_Ct, b_XI], writes=[b_T2])
        S.op(D, lambda e: e.tensor_tensor(out=T2[:], in0=T2[:], in1=T3[:], op=ALU.subtract), reads=[b_T2, b_T3], writes=[b_T2])

    def b1(k):
        gp, dr, si, sg = tasks[k]
        ct = gp // 4
        col = dr * NGP + gp
        cs = slice(col, col + 1)
        par = (gp * 2 + dr) % 2
        ly = ly2[par]; b_ly = b_ly2[par]
        t0 = sg * SEG
        (XR, XI, T1, T2, T3, Ct, St, KI, sre, sim_), (b_XR, b_XI, b_T1, b_T2, b_T3, b_Ct, b_St, b_KI, b_sre, b_sim) = bufs(k)
        rmul = mag[:, cs].broadcast_to([128, SEG])
        for zi, (src, bsrc, dst, bdst) in enumerate(((T1, b_T1, XR, b_XR), (T2, b_T2, XI, b_XI))):
            init = 0.0 if si == 0 else carry[:, zi:zi + 1]
            if dr == 0:
                o_ap, d_ap = dst[:, :], src[:, :]
            else:
                o_ap, d_ap = dst[:, ::-1], src[:, ::-1]
            S.op(D, lambda e, o_ap=o_ap, d_ap=d_ap, init=init: e.tensor_tensor_scan(out=o_ap, data0=rmul, data1=d_ap, initial=init, op0=ALU.mult, op1=ALU.add),
                 reads=[bsrc, b_mag, b_carry], writes=[bdst])
        if si < NSEG - 1:
            ccol = SEG - 1 if dr == 0 else 0
            S.op("pool", lambda e: e.tensor_copy(out=carry[:, 0:1], in_=XR[:, ccol:ccol + 1]), reads=[b_XR], writes=[b_carry])
            S.op("pool", lambda e: e.tensor_copy(out=carry[:, 1:2], in_=XI[:, ccol:ccol + 1]), reads=[b_XI], writes=[b_carry])

    def b2(k):
        gp, dr, si, sg = tasks[k]
        ct = gp // 4
        col = dr * NGP + gp
        cs = slice(col, col + 1)
        par = (gp * 2 + dr) % 2
        ly = ly2[par]; b_ly = b_ly2[par]
        t0 = sg * SEG
        (XR, XI, T1, T2, T3, Ct, St, KI, sre, sim_), (b_XR, b_XI, b_T1, b_T2, b_T3, b_Ct, b_St, b_KI, b_sre, b_sim) = bufs(k)
        S.op(D, lambda e: e.tensor_tensor(out=T1[:], in0=Ct[:], in1=XR[:], op=ALU.mult), reads=[b_Ct, b_XR], writes=[b_T1])
        S.op("pool", lambda e: e.tensor_tensor(out=T3[:], in0=St[:], in1=XI[:], op=ALU.mult), reads=[b_St, b_XI], writes=[b_T3])
        S.op(D, lambda e: e.tensor_tensor(out=sre[:], in0=T1[:], in1=T3[:], op=ALU.subtract), reads=[b_T1, b_T3], writes=[b_sre])

    def b3(k):
        gp, dr, si, sg = tasks[k]
        ct = gp // 4
        col = dr * NGP + gp
        cs = slice(col, col + 1)
        par = (gp * 2 + dr) % 2
        ly = ly2[par]; b_ly = b_ly2[par]
        t0 = sg * SEG
        (XR, XI, T1, T2, T3, Ct, St, KI, sre, sim_), (b_XR, b_XI, b_T1, b_T2, b_T3, b_Ct, b_St, b_KI, b_sre, b_sim) = bufs(k)
        S.op(D, lambda e: e.tensor_tensor(out=T2[:], in0=St[:], in1=XR[:], op=ALU.mult), reads=[b_St, b_XR], writes=[b_T2])
        S.op("pool", lambda e: e.tensor_tensor(out=KI[:].bitcast(F32), in0=Ct[:], in1=XI[:], op=ALU.mult), reads=[b_Ct, b_XI], writes=[b_KI])
        S.op(D, lambda e: e.tensor_tensor(out=sim_[:], in0=T2[:], in1=KI[:].bitcast(F32), op=ALU.add), reads=[b_T2, b_KI], writes=[b_sim])
        first = (gp % 4 == 0 and dr == 0)
        for blk in range(SEG // 512):
            pi = ycnt[0] % 2
            ycnt[0] += 1
            bs_ = slice(blk * 512, (blk + 1) * 512)
            ts = slice(t0 + blk * 512, t0 + (blk + 1) * 512)
            S.op("pe", lambda e, pi=pi, bs_=bs_: e.matmul(pY[pi][:], lhsT=ly[:, 0, :], rhs=sre[:, bs_], start=True, stop=False),
                 reads=[b_ly, b_sre], writes=[b_pY[pi]])
            S.op("pe", lambda e, pi=pi, bs_=bs_: e.matmul(pY[pi][:], lhsT=ly[:, 1, :], rhs=sim_[:, bs_], start=False, stop=True),
                 reads=[b_ly, b_sim], writes=[b_pY[pi]])
            if first:
                S.op("act", lambda e, pi=pi, ts=ts: e.copy(out=yacc[:, ct, ts], in_=pY[pi][:]), reads=[b_pY[pi]], writes=[b_yacc])
            else:
                S.op(D, lambda e, pi=pi, ts=ts: e.tensor_tensor(out=yacc[:, ct, ts], in0=pY[pi][:], in1=yacc[:, ct, ts], op=ALU.add),
                     reads=[b_pY[pi], b_yacc], writes=[b_yacc])

    n_t = len(tasks)
    f1(0); f2(0); f3(0); f4(0)
    for k in range(n_t):
        nx = k + 1 < n_t
        if nx:
            f1(k + 1)
        b1(k)
        if nx:
            f2(k + 1)
        b2(k)
        if nx:
            f3(k + 1)
        b3(k)
        if nx:
            f4(k + 1)
    for ctl in range(2):
        S.dma("sp", SC["y_loc"][ctl], yacc[:, ctl, :], reads=[b_yacc], writes=[PB["y_loc"]])
    F.end()


def emit_AT(F, li, W, SC, PB):
    S, A = F.S, F.begin()
    D = "dve"
    lam_init = 0.8 - 0.6 * math.exp(-0.3 * li)
    ones = A.sb("ones", [128, 128], BF16); b_ones = Buf()
    S.op(D, lambda e: e.memset(ones[:], 1.0), writes=[b_ones])
    epsT = A.sb("epsT", [128, 1], F32); b_eps = Buf()
    S.op(D, lambda e: e.memset(epsT[:], EPS), writes=[b_eps])
    lam_bc = A.sb("lam_bc", [128, 256], F32); b_lam = Buf()
    S.dma("pool", lam_bc[:], W["lamv"][li].partition_broadcast(128), writes=[b_lam])
    lt = A.sb("lt", [128, 128], F32); b_lt = Buf()
    ls = A.sb("ls", [128, 8], F32); b_ls = Buf()
    S.op(D, lambda e: e.tensor_tensor(out=lt[:, 0:64], in0=lam_bc[:, 0:64], in1=lam_bc[:, 64:128], op=ALU.mult), reads=[b_lam], writes=[b_lt])
    S.op(D, lambda e: e.tensor_tensor(out=lt[:, 64:128], in0=lam_bc[:, 128:192], in1=lam_bc[:, 192:256], op=ALU.mult), reads=[b_lam], writes=[b_lt])
    S.op(D, lambda e: e.tensor_reduce(out=ls[:, 0:2], in_=lt[:].rearrange("p (a d) -> p a d", a=2), axis=AX.X, op=ALU.add), reads=[b_lt], writes=[b_ls])
    S.op("act", lambda e: e.activation(out=ls[:, 2:4], in_=ls[:, 0:2], func=AF.Exp), reads=[b_ls], writes=[b_ls])
    S.op(D, lambda e: e.tensor_tensor(out=ls[:, 4:5], in0=ls[:, 3:4], in1=ls[:, 2:3], op=ALU.subtract), reads=[b_ls], writes=[b_ls])
    S.op(D, lambda e: e.tensor_scalar_add(out=ls[:, 5:6], in0=ls[:, 4:5], scalar1=-lam_init), reads=[b_ls], writes=[b_ls])
    nlam = ls[:, 5:6]
    subcol = A.sb("subcol", [128, 1], F32); b_sub = Buf()
    S.dma("sp", subcol[:], W["subln"][li].rearrange("(p o) -> p o", o=1), writes=[b_sub])
    S.op("act", lambda e: e.mul(out=subcol[:], in_=subcol[:], mul=1.0 - lam_init), reads=[b_sub], writes=[b_sub])

    qt = [A.sb(f"qt{i}", [128, NT], BF16) for i in range(2)]; b_qt = [Buf() for _ in range(2)]
    kt_ = [A.sb(f"kt{i}", [128, 4096], BF16) for i in range(2)]; b_kt = [Buf() for _ in range(2)]
    for i in range(2):
        S.op("pool", lambda e, i=i: e.memset(qt[i][:], 0.0), writes=[b_qt[i]])
        S.op("pool", lambda e, i=i: e.memset(kt_[i][:], 0.0), writes=[b_kt[i]])
    vt = [A.sb(f"vt{i}", [128, 32, 128], BF16) for i in range(2)]; b_vt = [Buf() for _ in range(2)]
    tab = A.sb("tab", [128, TABW2], F32); b_tab = Buf()
    NP = 5
    e32 = [A.sb(f"e32{i}", [128, 512], F32) for i in range(NP)]; b_e32 = [Buf() for _ in range(NP)]
    pb = [A.sb(f"pb{i}", [128, 512], BF16) for i in range(NP)]; b_pb = [Buf() for _ in range(NP)]
    rz = [A.sb(f"rz{i}", [128, 512], F32) for i in range(2)]; b_rz = [Buf() for _ in range(2)]
    oc32 = [A.sb(f"oc32{i}", [128, 512], F32) for i in range(2)]; b_oc32 = [Buf() for _ in range(2)]
    ot16 = [A.sb(f"ot16{i}", [128, 512], BF16) for i in range(2)]; b_ot16 = [Buf() for _ in range(2)]
    cmb = A.sb("cmb", [128, 512], F32); b_cmb = Buf()
    sqb = A.sb("sqb", [128, 512], BF16); b_sqb = Buf()
    rr = A.sb("rr", [128, 512], F32); b_rr = Buf()
    psS = [A.ps(f"psS{i}", [128, 512], F32) for i in range(NP)]; b_psS = [Buf() for _ in range(NP)]
    psO = [A.ps(f"psO{i}", [128, 512], F32) for i in range(2)]; b_psO = [Buf() for _ in range(2)]
    psZ = [A.ps(f"psZ{i}", [128, 512], F32) for i in range(1)] * 2; b_psZ = [Buf()] * 2

    QT, kall, vall = SC["QT"], SC["kt_all"], SC["v_all"]
    cnt = [0]
    oq = [0]

    def load_q(slot, m, d):
        S.dma("sp", qt[slot][0:d, :], QT[m, 0:d, :], writes=[b_qt[slot]])

    def load_k(slot, k, d):
        j, ko = kchunk(k)
        nrows = (KCH[j][1] - KCH[j][0]) * 96
        for r in range(2):
            row0 = r * nrows + ko * 96
            S.dma("sp", kt_[slot][0:d, r * 2048:(r + 1) * 2048], kall[j][row0:row0 + d, :], reads=[PB["kt_all"]], writes=[b_kt[slot]])

    def load_v(slot, kind, vi):
        for r in range(2):
            if kind == "A":
                src = vall[0][r * 128:(r + 1) * 128, vi * 2048:(vi + 1) * 2048].rearrange("p (k d) -> p k d", k=16)
                S.dma("pool", vt[slot][:, r * 16:(r + 1) * 16, 0:128], src, reads=[PB["v_all"]], writes=[b_vt[slot]])
            else:
                vj, vo = (1, vi) if vi < 5 else (2, vi - 5)
                src = vall[vj][r * 128:(r + 1) * 128, vo * 1024:(vo + 1) * 1024].rearrange("p (k d) -> p k d", k=16)
                S.dma("pool", vt[slot][:, r * 16:(r + 1) * 16, 0:64], src, reads=[PB["v_all"]], writes=[b_vt[slot]])

    def attn_block(sq, sk, sv, d, dv, scale, masked, qb, ob_, zsep=True):
        qs = slice(qb * 512, (qb + 1) * 512)

        def emit_S(kt):
            j = (cnt[0] + kt) % NP
            S.op("pe", lambda e, j=j, kt=kt: e.matmul(psS[j][:], lhsT=kt_[sk][:, kt * 128:(kt + 1) * 128], rhs=qt[sq][:, qs], start=True, stop=True),
                 reads=[b_kt[sk], b_qt[sq]], writes=[b_psS[j]])

        def emit_rest(kt):
            j = (cnt[0] + kt) % NP
            if masked:
                S.op("act", lambda e, j=j: e.activation(out=e32[j][:], in_=psS[j][:], func=AF.Exp, scale=scale), reads=[b_psS[j]], writes=[b_e32[j]])
                w = 512 * qb - 128 * kt + TABOFF2
                meng = D
                S.op(meng, lambda e, j=j, w=w: e.tensor_tensor(out=pb[j][:], in0=e32[j][:], in1=tab[:, w:w + 512], op=ALU.mult),
                     reads=[b_e32[j], b_tab], writes=[b_pb[j]])
            else:
                S.op("act", lambda e, j=j: e.activation(out=pb[j][:], in_=psS[j][:], func=AF.Exp, scale=scale), reads=[b_psS[j]], writes=[b_pb[j]])
            S.op("pe", lambda e, j=j, kt=kt: e.matmul(psO[ob_][0:dv, :], lhsT=vt[sv][:, kt, 0:dv], rhs=pb[j][:], start=(kt == 0), stop=(kt == 31)),
                 reads=[b_vt[sv], b_pb[j]], writes=[b_psO[ob_]])
            if zsep:
                S.op("pe", lambda e, j=j, kt=kt: e.matmul(psZ[ob_][0:dv, :], lhsT=ones[:, 0:dv], rhs=pb[j][:], start=(kt == 0), stop=(kt == 31)),
                     reads=[b_ones, b_pb[j]], writes=[b_psZ[ob_]])

        LOOK = NP - 1
        for kt in range(LOOK):
            emit_S(kt)
        for kt in range(32):
            if kt + LOOK < 32:
                emit_S(kt + LOOK)
            emit_rest(kt)
        cnt[0] += 32
        if zsep:
            S.op(D, lambda e: e.reciprocal(out=rz[ob_][0:dv, :], in_=psZ[ob_][0:dv, :]), reads=[b_psZ[ob_]], writes=[b_rz[ob_]])
        else:
            S.op(D, lambda e: e.reciprocal(out=rz[ob_][0:64, :], in_=psO[ob_][64:128, :]), reads=[b_psO[ob_]], writes=[b_rz[ob_]])

    yaT = SC["yaT"]
    vs = 0
    for hh in range(4):
        for c in range(2):
            load_q(c, 2 * hh + c, 64)
            load_k(c, 2 * hh + c, 64)
        vs = (vs + 1) % 2
        load_v(vs, "A", hh)
        S.dma("pool", tab[:], W["tab"][hh], writes=[b_tab])
        for qb in range(4):
            qs = slice(qb * 512, (qb + 1) * 512)
            for c in range(2):
                ob_ = oq[0] % 2
                oq[0] += 1
                attn_block(c, c, vs, 64, 128, 64 ** -0.5, True, qb, ob_)
                S.op(D, lambda e, ob_=ob_, c=c: e.tensor_tensor(out=oc32[c][:], in0=psO[ob_][:], in1=rz[ob_][:], op=ALU.mult),
                     reads=[b_psO[ob_], b_rz[ob_]], writes=[b_oc32[c]])
            S.op(D, lambda e: e.scalar_tensor_tensor(out=cmb[:], in0=oc32[1][:], scalar=nlam, in1=oc32[0][:], op0=ALU.mult, op1=ALU.add),
                 reads=[b_oc32[0], b_oc32[1], b_ls], writes=[b_cmb])
            S.op("act", lambda e: e.activation(out=sqb[:], in_=cmb[:], func=AF.Square), reads=[b_cmb], writes=[b_sqb])
            jr = cnt[0] % NP
            cnt[0] += 1
            S.op("pe", lambda e, jr=jr: e.matmul(psS[jr][:], lhsT=ones[:], rhs=sqb[:], start=True, stop=True), reads=[b_ones, b_sqb], writes=[b_psS[jr]])
            S.op("act", lambda e, jr=jr: e.activation(out=rr[:], in_=psS[jr][:], func=AF.Sqrt, bias=epsT[:, 0:1], scale=1.0 / 128), reads=[b_psS[jr], b_eps], writes=[b_rr])
            S.op(D, lambda e: e.reciprocal(out=rr[:], in_=rr[:]), reads=[b_rr], writes=[b_rr])
            o16 = oq[0] % 2
            S.op(D, lambda e, o16=o16: e.scalar_tensor_tensor(out=ot16[o16][:], in0=cmb[:], scalar=subcol[:, 0:1], in1=rr[:], op0=ALU.mult, op1=ALU.mult),
                 reads=[b_cmb, b_sub, b_rr], writes=[b_ot16[o16]])
            S.dma("sp", yaT[hh, :, qs], ot16[o16][:], reads=[b_ot16[o16]])
    ybc = SC["ybcT"]
    for i in range(2):
        S.op("pool", lambda e, i=i: e.memset(vt[i][:, :, 64:128], 1.0), writes=[b_vt[i]])
    cur = dict(k=None, v=None)
    sl = dict(q=0, k=0, v=vs)
    for j in range(16):
        if j < 8:
            qm, km, vi, d, scale = 8 + j, 8 + j // 4, j // 4, 64, 64 ** -0.5
        else:
            qm, km, vi, d, scale = 16 + (j - 8), 10 + (j - 8), 2 + (j - 8), 96, 96 ** -0.5
        sl["q"] = (sl["q"] + 1) % 2
        load_q(sl["q"], qm, d)
        if cur["k"] != km:
            sl["k"] = (sl["k"] + 1) % 2
            load_k(sl["k"], km, d)
            cur["k"] = km
        if cur["v"] != vi:
            sl["v"] = (sl["v"] + 1) % 2
            load_v(sl["v"], "BC", vi)
            cur["v"] = vi
        for qb in range(4):
            qs = slice(qb * 512, (qb + 1) * 512)
            ob_ = oq[0] % 2
            oq[0] += 1
            attn_block(sl["q"], sl["k"], sl["v"], d, 128, scale, False, qb, ob_, zsep=False)
            S.op(D, lambda e, ob_=ob_: e.tensor_tensor(out=ot16[ob_][0:64, :], in0=psO[ob_][0:64, :], in1=rz[ob_][0:64, :], op=ALU.mult),
                 reads=[b_psO[ob_], b_rz[ob_]], writes=[b_ot16[ob_]])
            S.dma("sp", ybc[j, :, qs], ot16[ob_][0:64, :], reads=[b_ot16[ob_]])
    F.end()


def emit_MG(F, li, h_src, h1_dst, W, SC, PB):
    S, A = F.S, F.begin()
    h = h_src; h1 = h1_dst
    g_pre = W["norm_pre_mix"][li]; g_post = W["norm_post_mix"][li]
    ybcv = SC["ybcT"].rearrange("(b k two) d t -> b k (two d) t", b=2, two=2)
    ybT = ybcv[0]; ycT = ybcv[1]
    yaTs = SC["yaT"]
    dcol = W["dcol"][li]
    w_glu = W["s5_w_glu"][li]; w_gate = W["w_gate"][li]; w_br = W["w_br"][li]; w_out = W["w_out"][li]
    identd = W["identd"]

    HT = 1024
    HTI = HT // 128
    D = "dve"
    ident = A.sb("ident", [128, 128], BF16); b_ident = Buf()
    S.dma("sp", ident[:], identd[:, :], writes=[b_ident])
    epsT = A.sb("epsT", [128, 1], F32); b_eps = Buf()
    S.op(D, lambda e: e.memset(epsT[:], EPS), writes=[b_eps])
    g_bc = A.sb("g_bc", [128, 1024], F32); b_g = Buf()
    S.dma("pool", g_bc[:], g_pre.partition_broadcast(128), writes=[b_g])
    gp_bc = A.sb("gp_bc", [128, 1024], F32); b_gp = Buf()
    S.dma("pool", gp_bc[:], g_post.partition_broadcast(128), writes=[b_gp])
    sel = A.sb("sel", [128, 2], F32); b_sel = Buf()
    S.dma("sp", sel[:], W["sel"][:, :], writes=[b_sel])
    dc = A.sb("dc", [128, 4], F32); b_dc = Buf()
    S.dma("sp", dc[:], dcol[:, :], writes=[b_dc])

    stg = [A.sb(f"stg{i}", [128, 1024], F32) for i in range(2)]; b_stg = [Buf() for _ in range(2)]
    sidx = [0]
    stq = [A.sb(f"stq{i}", [128, 512], F32) for i in range(4)]; b_stq = [Buf() for _ in range(4)]
    qidx = [0]

    def load_wp(dst, b_dst, src, ktiles):
        for kt in range(ktiles):
            i = qidx[0] % 4
            qidx[0] += 1
            S.dma("sp", stq[i][:], src[kt * 128:(kt + 1) * 128, :], writes=[b_stq[i]])
            S.op("pool", lambda e, i=i, kt=kt: e.tensor_copy(out=dst[:, kt, :], in_=stq[i][:]), reads=[b_stq[i]], writes=[b_dst])

    def load_w(dst, b_dst, src, ktiles, cols):
        for kt in range(ktiles):
            i = sidx[0] % 2
            sidx[0] += 1
            S.dma("sp", stg[i][:, 0:cols], src[kt * 128:(kt + 1) * 128, :], writes=[b_stg[i]])
            eng = "pool" if i else "act"
            if eng == "act":
                S.op("act", lambda e, i=i, kt=kt: e.copy(out=dst[:, kt, 0:cols], in_=stg[i][:, 0:cols]), reads=[b_stg[i]], writes=[b_dst])
            else:
                S.op("pool", lambda e, i=i, kt=kt: e.tensor_copy(out=dst[:, kt, 0:cols], in_=stg[i][:, 0:cols]), reads=[b_stg[i]], writes=[b_dst])

    wgh = [A.sb(f"wgh{i}", [128, 8, 512], BF16) for i in range(3)]; b_wgh = [Buf() for _ in range(3)]
    wbh = [A.sb(f"wbh{i}", [128, 4, 512], BF16) for i in range(3)]; b_wbh = [Buf() for _ in range(3)]
    wb = A.sb("wgl", [128, 4, 1024], BF16); b_wb = Buf()
    mixed = A.sb("mixed", [128, HTI, 1024], F32); b_mixed = [Buf() for _ in range(HTI)]
    xnT = A.sb("xnT", [128, 8, HT], BF16); b_xnT = Buf()
    yaT = A.sb("yaT", [128, 4, HT], BF16); b_yaT = Buf()
    gT = A.sb("gT", [128, 4, HT], BF16); b_gT = Buf()
    sgT = A.sb("sgT", [128, 4, HT], BF16); b_sgT = Buf()
    ydT = A.sb("ydT", [128, 4, HT], BF16); b_ydT = Buf()
    ht = [A.sb(f"ht{i}", [128, 1024], F32) for i in range(2)]; b_ht = [Buf() for _ in range(2)]
    junk = A.sb("junk", [128, 1024], F32); b_junk = Buf()
    st = [A.sb(f"st{i}", [128, 8], F32) for i in range(2)]; b_st = [Buf() for _ in range(2)]
    xn = [A.sb(f"xn{i}", [128, 1024], BF16) for i in range(2)]; b_xn = [Buf() for _ in range(2)]
    e1 = A.sb("e1", [128, HT], F32); b_e1 = Buf()
    e2 = A.sb("e2", [128, HT], F32); b_e2 = Buf()
    e3 = A.sb("e3", [128, HT], F32); b_e3 = Buf()
    eu = A.sb("eu", [128, HT], BF16); b_eu = Buf()
    sg = [A.sb(f"sg{i}", [128, 512], F32) for i in range(2)]; b_sg = [Buf() for _ in range(2)]
    tm = [A.sb(f"tm{i}", [128, 512], F32) for i in range(2)]; b_tm = [Buf() for _ in range(2)]
    osb = A.sb("osb", [128, 1024], F32); b_osb = Buf()
    mT = A.sb("mT", [128, 8, 128], BF16); b_mT = Buf()
    pT = A.ps("pT", [128, 1024], BF16); b_pT = Buf()
    pA = [A.ps(f"pA{i}", [128, 512], F32) for i in range(2)]; b_pA = [Buf() for _ in range(2)]
    pB = [A.ps(f"pB{i}", [128, 512], F32) for i in range(2)]; b_pB = [Buf() for _ in range(2)]
    pG = [A.ps(f"pG{i}", [128, 512], F32) for i in range(2)]; b_pG = [Buf() for _ in range(2)]

    for th in range(NT // HT):
        tb0 = th * HT
        for t in range(HTI):
            i = t % 2
            r0 = tb0 + t * 128
            S.dma("sp", ht[i][:], h[r0:r0 + 128, :], writes=[b_ht[i]])
            rms_rstd(S, ht[i][:], b_ht[i], junk[:], b_junk, st[i], b_st[i], epsT[:, 0:1], b_eps, 1024)
            S.op(D, lambda e, i=i: e.scalar_tensor_tensor(out=xn[i][:], in0=ht[i][:], scalar=st[i][:, 2:3], in1=g_bc[:], op0=ALU.mult, op1=ALU.mult),
                 reads=[b_ht[i], b_st[i], b_g], writes=[b_xn[i]])
            for kt in range(8):
                S.op("pe", lambda e, i=i, kt=kt: e.transpose(out=pT[:, kt * 128:(kt + 1) * 128], in_=xn[i][:, kt * 128:(kt + 1) * 128], identity=ident[:]),
                     reads=[b_xn[i], b_ident], writes=[b_pT])
            S.op("act", lambda e, t=t: e.copy(out=xnT[:, :, t * 128:(t + 1) * 128], in_=pT[:].rearrange("p (k t) -> p k t", k=8)), reads=[b_pT], writes=[b_xnT])
        for kt in range(4):
            S.dma("sp", yaT[:, kt, :], yaTs[kt, :, tb0:tb0 + HT], writes=[b_yaT])
        load_w(wb, b_wb, w_glu, 4, 1024)
        for ct in range(4):
            ya_ = SC["y_all"][ct % 2]
            yrow = (ct // 2) * 128
            S.dma("sp", e1[:], ya_[yrow:yrow + 128, tb0:tb0 + HT], reads=[PB["y_all"]], writes=[b_e1])
            S.dma("sp", e3[:], ya_[yrow:yrow + 128, 2048 + tb0:2048 + tb0 + HT], reads=[PB["y_all"]], writes=[b_e3])
            S.dma("sp", eu[:], SC["uT_loc"][ct * 128:(ct + 1) * 128, tb0:tb0 + HT], writes=[b_eu])
            S.op("pool", lambda e: e.tensor_scalar(out=e1[:], in0=e1[:], scalar1=sel[:, 0:1], scalar2=None, op0=ALU.mult), reads=[b_e1, b_sel], writes=[b_e1])
            S.op(D, lambda e: e.scalar_tensor_tensor(out=e1[:], in0=e3[:], scalar=sel[:, 1:2], in1=e1[:], op0=ALU.mult, op1=ALU.add),
                 reads=[b_e1, b_e3, b_sel], writes=[b_e1])
            S.op(D, lambda e, ct=ct: e.scalar_tensor_tensor(out=e1[:], in0=eu[:], scalar=dc[:, ct:ct + 1], in1=e1[:], op0=ALU.mult, op1=ALU.add),
                 reads=[b_e1, b_eu, b_dc], writes=[b_e1])
            S.op("pool", lambda e: e.tensor_tensor(out=e2[:], in0=e1[:], in1=e1[:], op=ALU.mult), reads=[b_e1], writes=[b_e2])
            S.op("pool", lambda e: e.tensor_scalar(out=e2[:], in0=e2[:], scalar1=0.044715, scalar2=1.0, op0=ALU.mult, op1=ALU.add), reads=[b_e2], writes=[b_e2])
            S.op(D, lambda e: e.tensor_tensor(out=e2[:], in0=e2[:], in1=e1[:], op=ALU.mult), reads=[b_e1, b_e2], writes=[b_e2])
            S.op("act", lambda e: e.activation(out=e3[:], in_=e2[:], func=AF.Sigmoid, scale=2.0 * math.sqrt(2.0 / math.pi)), reads=[b_e2], writes=[b_e3])
            S.op(D, lambda e, ct=ct: e.tensor_tensor(out=gT[:, ct, :], in0=e1[:], in1=e3[:], op=ALU.mult), reads=[b_e1, b_e3], writes=[b_gT])
        gcnt = 0
        for och in (4, 5, 6, 7, 0, 1, 2, 3):
            for tb in range(HT // 512):
                pi = gcnt % 2
                gcnt += 1
                ts = slice(tb * 512, (tb + 1) * 512)
                for kt in range(4):
                    S.op("pe", lambda e, pi=pi, kt=kt, och=och, ts=ts: e.matmul(pG[pi][:], lhsT=wb[:, kt, och * 128:(och + 1) * 128], rhs=gT[:, kt, ts],
                                                                                start=(kt == 0), stop=(kt == 3)), reads=[b_wb, b_gT], writes=[b_pG[pi]])
                if och >= 4:
                    S.op("act", lambda e, pi=pi, och=och, ts=ts: e.activation(out=sgT[:, och - 4, ts], in_=pG[pi][:], func=AF.Sigmoid), reads=[b_pG[pi]], writes=[b_sgT])
                else:
                    S.op(D, lambda e, pi=pi, och=och, ts=ts: e.tensor_tensor(out=ydT[:, och, ts], in0=pG[pi][:], in1=sgT[:, och, ts], op=ALU.mult),
                         reads=[b_pG[pi], b_sgT], writes=[b_ydT])
        units = [(b, hc) for b in range(4) for hc in range(2)]

        def load_unit(u):
            b, hc = units[u]
            load_wp(wgh[u % 3], b_wgh[u % 3], w_gate[:, b * 1024 + hc * 512:b * 1024 + (hc + 1) * 512], 8)
            load_wp(wbh[u % 3], b_wbh[u % 3], w_br[b][:, hc * 512:(hc + 1) * 512], 4)

        for kt in range(4):
            S.dma("sp", gT[:, kt, :], ybT[kt, :, tb0:tb0 + HT], writes=[b_gT])
            S.dma("sp", sgT[:, kt, :], ycT[kt, :, tb0:tb0 + HT], writes=[b_sgT])
        load_unit(0)
        load_unit(1)
        load_unit(2)
        pcnt = 0
        for u, (b, hc) in enumerate(units):
            yT_, b_yT = ((yaT, b_yaT), (gT, b_gT), (sgT, b_sgT), (ydT, b_ydT))[b]
            wgu, b_wgu, wbu, b_wbu = wgh[u % 3], b_wgh[u % 3], wbh[u % 3], b_wbh[u % 3]
            cs = slice(hc * 512, (hc + 1) * 512)
            for t in range(HTI):
                tsl = slice(t * 128, (t + 1) * 128)
                pi = pcnt % 2
                pcnt += 1
                for kt in range(4):
                    S.op("pe", lambda e, kt=kt, pi=pi: e.matmul(pA[pi][:], lhsT=yT_[:, kt, tsl], rhs=wbu[:, kt, :], start=(kt == 0), stop=(kt == 3)),
                         reads=[b_yT, b_wbu], writes=[b_pA[pi]])
                for kt in range(8):
                    S.op("pe", lambda e, kt=kt, pi=pi: e.matmul(pB[pi][:], lhsT=xnT[:, kt, tsl], rhs=wgu[:, kt, :], start=(kt == 0), stop=(kt == 7)),
                         reads=[b_xnT, b_wgu], writes=[b_pB[pi]])
                S.op("act", lambda e, pi=pi: e.activation(out=sg[pi][:], in_=pB[pi][:], func=AF.Sigmoid), reads=[b_pB[pi]], writes=[b_sg[pi]])
                if b == 0:
                    S.op(D, lambda e, pi=pi, t=t: e.tensor_tensor(out=mixed[:, t, cs], in0=pA[pi][:], in1=sg[pi][:], op=ALU.mult),
                         reads=[b_pA[pi], b_sg[pi]], writes=[b_mixed[t]])
                else:
                    S.op(D, lambda e, pi=pi: e.tensor_tensor(out=tm[pi][:], in0=pA[pi][:], in1=sg[pi][:], op=ALU.mult),
                         reads=[b_pA[pi], b_sg[pi]], writes=[b_tm[pi]])
                    S.op(D, lambda e, pi=pi, t=t: e.tensor_tensor(out=mixed[:, t, cs], in0=mixed[:, t, cs], in1=tm[pi][:], op=ALU.add),
                         reads=[b_tm[pi], b_mixed[t]], writes=[b_mixed[t]])
            if u + 3 < len(units):
                load_unit(u + 3)
        for hc in range(2):
            load_wp(wgh[hc], b_wgh[hc], w_out[:, hc * 512:(hc + 1) * 512], 8)
        for t in range(HTI):
            i = t % 2
            r0 = tb0 + t * 128
            S.dma("sp", ht[i][:], h[r0:r0 + 128, :], writes=[b_ht[i]])
            S.op("act", lambda e, i=i, t=t: e.copy(out=xn[i][:], in_=mixed[:, t, :]), reads=[b_mixed[t]], writes=[b_xn[i]])
            for kt in range(8):
                S.op("pe", lambda e, i=i, kt=kt: e.transpose(out=pT[:, kt * 128:(kt + 1) * 128], in_=xn[i][:, kt * 128:(kt + 1) * 128], identity=ident[:]),
                     reads=[b_xn[i], b_ident], writes=[b_pT])
            S.op("act", lambda e: e.copy(out=mT[:].rearrange("p k t -> p (k t)"), in_=pT[:]), reads=[b_pT], writes=[b_mT])
            for hc in range(2):
                cs = slice(hc * 512, (hc + 1) * 512)
                for kt in range(8):
                    S.op("pe", lambda e, kt=kt, hc=hc: e.matmul(pA[hc][:], lhsT=mT[:, kt, :], rhs=wgh[hc][:, kt, :], start=(kt == 0), stop=(kt == 7)),
                         reads=[b_mT, b_wgh[hc]], writes=[b_pA[hc]])
                if hc == 0:
                    S.op("act", lambda e, hc=hc, cs=cs: e.copy(out=osb[:, cs], in_=pA[hc][:]), reads=[b_pA[hc]], writes=[b_osb])
                else:
                    S.op(D, lambda e, hc=hc, cs=cs: e.tensor_copy(out=osb[:, cs], in_=pA[hc][:]), reads=[b_pA[hc]], writes=[b_osb])
            rms_rstd(S, osb[:], b_osb, junk[:], b_junk, st[i], b_st[i], epsT[:, 0:1], b_eps, 1024)
            S.op(D, lambda e, i=i: e.scalar_tensor_tensor(out=osb[:], in0=osb[:], scalar=st[i][:, 2:3], in1=gp_bc[:], op0=ALU.mult, op1=ALU.mult),
                 reads=[b_osb, b_st[i], b_gp], writes=[b_osb])
            S.op("pool", lambda e, i=i: e.tensor_tensor(out=ht[i][:], in0=ht[i][:], in1=osb[:], op=ALU.add), reads=[b_ht[i], b_osb], writes=[b_ht[i]])
            S.dma("sp", h1[r0:r0 + 128, :], ht[i][:], reads=[b_ht[i]])
    F.end()


def emit_FF(F, li, h1_src, h2_dst, W):
    S, A = F.S, F.begin()
    h1 = h1_src; h2 = h2_dst
    g_pre = W["norm_pre_ffn"][li]; g_post = W["norm_post_ffn"][li]
    w_fi = W["w_ffn_in"][li]; w_fo = W["w_ffn_out"][li]
    identd = W["identd"]
    D = "dve"
    ident = A.sb("ident", [128, 128], BF16); b_ident = Buf()
    S.dma("sp", ident[:], identd[:, :], writes=[b_ident])
    epsT = A.sb("epsT", [128, 1], F32); b_eps = Buf()
    S.op(D, lambda e: e.memset(epsT[:], EPS), writes=[b_eps])
    g_bc = A.sb("g_bc", [128, 1024], F32); b_g = Buf()
    S.dma("pool", g_bc[:], g_pre.partition_broadcast(128), writes=[b_g])
    gp_bc = A.sb("gp_bc", [128, 1024], F32); b_gp = Buf()
    S.dma("pool", gp_bc[:], g_post.partition_broadcast(128), writes=[b_gp])
    stg = [A.sb(f"stg{i}", [128, 1024], F32) for i in range(2)]; b_stg = [Buf() for _ in range(2)]
    sidx = [0]

    def load_w(dst, b_dst, src, ktiles, cols):
        for kt in range(ktiles):
            i = sidx[0] % 2
            sidx[0] += 1
            S.dma("sp", stg[i][:, 0:cols], src[kt * 128:(kt + 1) * 128, :], writes=[b_stg[i]])
            S.op("pool", lambda e, i=i, kt=kt: e.tensor_copy(out=dst[:, kt, 0:cols], in_=stg[i][:, 0:cols]), reads=[b_stg[i]], writes=[b_dst])

    fnT = A.sb("fnT", [128, 8, NT], BF16); b_fnT = Buf()
    facc = A.sb("facc", [128, NTI, 1024], F32); b_facc = [Buf() for _ in range(NTI)]
    hidT = A.sb("hidT", [128, 4, NT], BF16); b_hidT = Buf()
    wfis = [A.sb(f"wfi{i}", [128, 8, 512], BF16) for i in range(2)]; b_wfis = [Buf() for _ in range(2)]
    wfos = [A.sb(f"wfo{i}", [128, 4, 1024], BF16) for i in range(2)]; b_wfos = [Buf() for _ in range(2)]
    ht = [A.sb(f"ht{i}", [128, 1024], F32) for i in range(2)]; b_ht = [Buf() for _ in range(2)]
    junk = A.sb("junk", [128, 1024], F32); b_junk = Buf()
    st = [A.sb(f"st{i}", [128, 8], F32) for i in range(2)]; b_st = [Buf() for _ in range(2)]
    xn = [A.sb(f"xn{i}", [128, 1024], BF16) for i in range(2)]; b_xn = [Buf() for _ in range(2)]
    r32 = [A.sb(f"r32{i}", [128, 512], F32) for i in range(2)]; b_r32 = [Buf() for _ in range(2)]
    pT = A.ps("pT", [128, 1024], BF16); b_pT = Buf()
    pH = [A.ps(f"pH{i}", [128, 512], F32) for i in range(2)]; b_pH = [Buf() for _ in range(2)]
    pF = [A.ps(f"pF{i}", [128, 512], F32) for i in range(4)]; b_pF = [Buf() for _ in range(4)]

    for t in range(NTI):
        i = t % 2
        r0 = t * 128
        S.dma("sp", ht[i][:], h1[r0:r0 + 128, :], writes=[b_ht[i]])
        rms_rstd(S, ht[i][:], b_ht[i], junk[:], b_junk, st[i], b_st[i], epsT[:, 0:1], b_eps, 1024)
        S.op(D, lambda e, i=i: e.scalar_tensor_tensor(out=xn[i][:], in0=ht[i][:], scalar=st[i][:, 2:3], in1=g_bc[:], op0=ALU.mult, op1=ALU.mult),
             reads=[b_ht[i], b_st[i], b_g], writes=[b_xn[i]])
        for kt in range(8):
            S.op("pe", lambda e, i=i, kt=kt: e.transpose(out=pT[:, kt * 128:(kt + 1) * 128], in_=xn[i][:, kt * 128:(kt + 1) * 128], identity=ident[:]),
                 reads=[b_xn[i], b_ident], writes=[b_pT])
        S.op("act", lambda e, t=t: e.copy(out=fnT[:, :, t * 128:(t + 1) * 128], in_=pT[:].rearrange("p (k t) -> p k t", k=8)), reads=[b_pT], writes=[b_fnT])
    hcnt = 0
    fcnt = 0
    def load_chunk(c):
        load_w(wfis[c % 2], b_wfis[c % 2], w_fi[:, c * 512:(c + 1) * 512], 8, 512)
        load_w(wfos[c % 2], b_wfos[c % 2], w_fo[c * 512:(c + 1) * 512, :], 4, 1024)

    load_chunk(0)
    load_chunk(1)
    for c in range(8):
        wfi, b_wfi, wfo, b_wfo = wfis[c % 2], b_wfis[c % 2], wfos[c % 2], b_wfos[c % 2]
        for j in range(4):
            for tb in range(4):
                pi = hcnt % 2
                hcnt += 1
                ts = slice(tb * 512, (tb + 1) * 512)
                for kt in range(8):
                    S.op("pe", lambda e, pi=pi, kt=kt, j=j, ts=ts: e.matmul(pH[pi][:], lhsT=wfi[:, kt, j * 128:(j + 1) * 128], rhs=fnT[:, kt, ts],
                                                                            start=(kt == 0), stop=(kt == 7)), reads=[b_wfi, b_fnT], writes=[b_pH[pi]])
                S.op("act", lambda e, pi=pi: e.activation(out=r32[pi][:], in_=pH[pi][:], func=AF.Relu), reads=[b_pH[pi]], writes=[b_r32[pi]])
                S.op(D, lambda e, pi=pi, j=j, ts=ts: e.tensor_tensor(out=hidT[:, j, ts], in0=r32[pi][:], in1=r32[pi][:], op=ALU.mult),
                     reads=[b_r32[pi]], writes=[b_hidT])
        for t in range(NTI):
            tsl = slice(t * 128, (t + 1) * 128)
            for hc in range(2):
                pi = fcnt % 4
                fcnt += 1
                cs = slice(hc * 512, (hc + 1) * 512)
                for j in range(4):
                    S.op("pe", lambda e, pi=pi, j=j, tsl=tsl, cs=cs: e.matmul(pF[pi][:], lhsT=hidT[:, j, tsl], rhs=wfo[:, j, cs], start=(j == 0), stop=(j == 3)),
                         reads=[b_hidT, b_wfo], writes=[b_pF[pi]])
                if c == 0:
                    S.op("act", lambda e, pi=pi, t=t, cs=cs: e.copy(out=facc[:, t, cs], in_=pF[pi][:]), reads=[b_pF[pi]], writes=[b_facc[t]])
                else:
                    S.op(D, lambda e, pi=pi, t=t, cs=cs: e.tensor_tensor(out=facc[:, t, cs], in0=pF[pi][:], in1=facc[:, t, cs], op=ALU.add),
                         reads=[b_pF[pi], b_facc[t]], writes=[b_facc[t]])
        if c + 2 < 8:
            load_chunk(c + 2)
    for t in range(NTI):
        i = t % 2
        r0 = t * 128
        S.dma("sp", ht[i][:], h1[r0:r0 + 128, :], writes=[b_ht[i]])
        rms_rstd(S, facc[:, t, :], b_facc[t], junk[:], b_junk, st[i], b_st[i], epsT[:, 0:1], b_eps, 1024)
        S.op(D, lambda e, i=i, t=t: e.scalar_tensor_tensor(out=facc[:, t, :], in0=facc[:, t, :], scalar=st[i][:, 2:3], in1=gp_bc[:], op0=ALU.mult, op1=ALU.mult),
             reads=[b_facc[t], b_st[i], b_gp], writes=[b_facc[t]])
        S.op("pool", lambda e, i=i, t=t: e.tensor_tensor(out=ht[i][:], in0=ht[i][:], in1=facc[:, t, :], op=ALU.add), reads=[b_ht[i], b_facc[t]], writes=[b_ht[i]])
        S.dma("sp", h2[r0:r0 + 128, :], ht[i][:], reads=[b_ht[i]])
    F.end()


def make_scratch(F):
    SC = dict(
        QT=F.scratch("QT", [24, 96, NT], BF16),
        kt_loc=[F.scratch(f"kt_loc{j}", [(b - a) * 96, NT], BF16) for j, (a, b) in enumerate(KCH)],
        kt_all=[F.scratch(f"kt_all{j}", [2 * (b - a) * 96, NT], BF16) for j, (a, b) in enumerate(KCH)],
        v_loc=[F.scratch(f"v_loc{j}", [128, w], BF16) for j, w in enumerate((8192, 5120, 5120))],
        v_all=[F.scratch(f"v_all{j}", [256, w], BF16) for j, w in enumerate((8192, 5120, 5120))],
        uT_loc=F.scratch("uT_loc", [512, NT], BF16), uT_all=F.scratch("uT_all", [1024, NT], BF16),
        y_loc=[F.scratch(f"y_loc{j}", [128, 4096], F32) for j in range(2)],
        y_all=[F.scratch(f"y_all{j}", [256, 4096], F32) for j in range(2)],
        yaT=F.scratch("yaT", [4, 128, NT], BF16), ybcT=F.scratch("ybcT", [16, 64, NT], BF16),
        h1=F.scratch("h1buf", [NT, 1024], F32), hmid=F.scratch("hmid", [NT, 1024], F32),
    )
    PB = {k: Buf(k) for k in ("kt_all", "v_all", "uT_all", "y_loc", "y_all")}
    return SC, PB


def build_fused():
    F = Fused()
    C, nc, S = F.C, F.nc, F.S
    W = {}
    def I(name, shape, dt=F32):
        W[name] = C.inp(name, shape, dt)
    I("x_own", [NT, 1024])
    for k in ("norm_pre_mix", "norm_post_mix", "norm_pre_ffn", "norm_post_ffn"):
        I(k, [2, 1024])
    I("w_in", [2, 1024, 3232]); I("w_gate", [2, 1024, 4096]); I("lamv", [2, 256]); I("subln", [2, 128])
    I("gB", [2, 640]); I("gC", [2, 384]); I("mla_w_uq", [2, 256, 768]); I("mla_w_ukv", [2, 128, 1024])
    I("areT", [2, 128, 2 * NGP]); I("aimT", [2, 128, 2 * NGP]); I("ldtT", [2, 128, 2 * NGP])
    for k in ("Bre", "Bim", "Cre", "Cim"):
        I(k, [2, 2, NGP, 128, 128])
    I("dcol", [2, 128, 4]); I("s5_w_glu", [2, 512, 1024]); I("w_br", [2, 4, 512, 1024]); I("w_out", [2, 1024, 1024])
    I("w_ffn_in", [2, 1024, 4096]); I("w_ffn_out", [2, 4096, 1024])
    I("cosb", [NT, 64]); I("sinb", [NT, 64]); I("cosc", [NT, 32]); I("sinc", [NT, 32])
    I("tab", [4, 128, TABW2]); I("iota", [128, SEG]); I("sel", [128, 2]); I("identd", [128, 128], BF16)
    out = C.out("out", [NT, 1024])
    SC, PB = make_scratch(F)
    h_src = W["x_own"]
    for li in range(2):
        h_dst = SC["hmid"] if li == 0 else out
        emit_P(F, li, h_src, W, SC)
        F.begin()
        S.cc("AllGather", [SC["uT_loc"].opt()], [SC["uT_all"].opt()], PAIRS, writes=[PB["uT_all"]])
        for j in range(4):
            S.cc("AllGather", [SC["kt_loc"][j].opt()], [SC["kt_all"][j].opt()], PAIRS, writes=[PB["kt_all"]])
        for j in range(3):
            S.cc("AllGather", [SC["v_loc"][j].opt()], [SC["v_all"][j].opt()], PAIRS, writes=[PB["v_all"]])
        F.end()
        emit_S5(F, li, W, SC, PB)
        F.begin()
        for j in range(2):
            S.cc("AllGather", [SC["y_loc"][j].opt()], [SC["y_all"][j].opt()], PAIRS, reads=[PB["y_loc"]], writes=[PB["y_all"]])
        F.end()
        emit_AT(F, li, W, SC, PB)
        emit_MG(F, li, h_src, SC["h1"], W, SC, PB)
        emit_FF(F, li, SC["h1"], h_dst, W)
        h_src = h_dst
    F.begin()
    F.end(include_cc=True)
    return nc


def kernel(**inputs):
    x = np.asarray(inputs["x"], dtype=np.float32)
    P = {k: np.asarray(v, dtype=np.float32) for k, v in inputs.items() if k != "x"}
    NCORE = 8
    ident = np.eye(128, dtype=BF)
    iota = _c(np.tile(np.arange(SEG, dtype=np.float32)[None, :], (128, 1)))
    toks = [np.arange(hf * NT, (hf + 1) * NT) for hf in range(2)]
    rope = [rope_tables(t // 64, t % 64, t) for t in toks]
    tabs = [alibi_table_core(hf) for hf in range(2)]
    shared = {k: P[k] for k in ("norm_pre_mix", "norm_post_mix", "norm_pre_ffn", "norm_post_ffn", "w_in", "w_gate", "mla_w_uq", "mla_w_ukv",
                                "s5_w_glu", "w_out", "w_ffn_in", "w_ffn_out")}
    shared["subln"] = P["diff_subln"]
    shared["lamv"] = _c(np.concatenate([P["diff_lam_q1"], P["diff_lam_k1"], P["diff_lam_q2"], P["diff_lam_k2"]], axis=1))
    shared["gB"] = _c(np.concatenate([np.tile(P["gqa_q_norm"], (1, 8)), np.tile(P["gqa_k_norm"], (1, 2))], axis=1))
    shared["gC"] = _c(np.concatenate([P["mla_q_norm"], P["mla_kv_norm"]], axis=1))
    shared["w_br"] = _c(np.stack([P["w_br_a"], P["w_br_b"], P["w_br_c"], P["w_br_d"]], axis=1))
    shared["dcol"] = _c(P["s5_d"].reshape(2, 4, 128).transpose(0, 2, 1))
    shared["iota"] = iota
    shared["identd"] = ident
    s5keys = ("s5_a_re", "s5_a_im", "s5_log_dt", "s5_b_re", "s5_b_im", "s5_c_re", "s5_c_im")
    s5l = []
    for hf in range(2):
        per_layer = [s5_host_layout({k: P[k] for k in s5keys}, li, hf) for li in range(2)]
        s5l.append({k: _c(np.stack([per_layer[0][k], per_layer[1][k]])) for k in per_layer[0]})
    maps = []
    for c in range(NCORE):
        b, hf = c // 2, c % 2
        m = dict(shared)
        m["x_own"] = _c(x[b, toks[hf]])
        cb, sb_, cc, sc = rope[hf]
        m.update(cosb=cb, sinb=sb_, cosc=cc, sinc=sc, tab=tabs[hf])
        sel = np.zeros((128, 2), np.float32); sel[:, hf] = 1.0
        m["sel"] = sel
        m.update(s5l[hf])
        maps.append(m)
    res = _run(build_fused(), maps)
    out = np.zeros_like(x)
    for c in range(NCORE):
        out[c // 2, toks[c % 2]] = res[c]["out"]
    return out
```
